# Optimizing a Trainium2 kernel written in Bass

```python
import jax, jax.numpy as jnp
from jax import lax
import numpy as np

D_MODEL = 1024
BATCH = 8
SEQ = 2048
DEPTH = 2

HEAD_DIM = 64
ATTN_HEADS = D_MODEL // 128
ATTN_WIDTH = ATTN_HEADS * HEAD_DIM
MOBA_BLOCK = 256
MOBA_TOPK = 3
MOBA_Q_CHUNK = 32
POOL_WINDOWS = (2, 4, 8, 16)
POOL_GROUPS = len(POOL_WINDOWS)
POOL_WIDTH = D_MODEL // 2
POOL_GROUP_WIDTH = POOL_WIDTH // POOL_GROUPS
N_BRANCH = 2
IN_WIDTH = 3 * ATTN_WIDTH + POOL_WIDTH + N_BRANCH * D_MODEL
D_FF = ((8 * D_MODEL // 3 + 255) // 256) * 256
CONV_WIDTH = 3
RMS_EPS = 1e-6
NEG_INF = -1e30

kernel_name = "hybrid_moba_pool_convffn"


def rms_norm(x, g):
    xf = x.astype(jnp.float32)
    y = xf * lax.rsqrt(jnp.mean(xf * xf, axis=-1, keepdims=True) + RMS_EPS)
    return (y * g.astype(jnp.float32)).astype(x.dtype)


def moba_attention(q, k, v):
    B, S, H, Dh = q.shape
    nb = -(-S // MOBA_BLOCK)
    s_pad = nb * MOBA_BLOCK
    k_sel = min(MOBA_TOPK, nb)
    L = MOBA_BLOCK
    Q = MOBA_Q_CHUNK
    scale = Dh ** -0.5
    q = q.transpose(0, 2, 1, 3)
    pad = ((0, 0), (0, 0), (0, s_pad - S), (0, 0))
    k = jnp.pad(k.transpose(0, 2, 1, 3), pad)
    v = jnp.pad(v.transpose(0, 2, 1, 3), pad)
    k_blocks = k.reshape(B, H, nb, L, Dh)
    v_blocks = v.reshape(B, H, nb, L, Dh)
    counts = jnp.clip(S - jnp.arange(nb) * L, 1, L).astype(jnp.float32)
    k_mean = k_blocks.astype(jnp.float32).sum(axis=3) / counts[None, None, :, None]

    n_chunks = S // Q
    q_chunks = q.reshape(B, H, n_chunks, Q, Dh).transpose(2, 0, 1, 3, 4)
    b_idx = jnp.arange(B)[:, None, None, None]
    h_idx = jnp.arange(H)[None, :, None, None]

    def chunk_attn(args):
        c, qc = args
        q_pos = c * Q + jnp.arange(Q)
        own = (c * Q) // L
        gate = jnp.einsum('bhqd,bhnd->bhqn', qc.astype(jnp.float32), k_mean)
        gate = jnp.where(jnp.arange(nb) < own, gate, NEG_INF)
        _, sel = lax.top_k(gate, k_sel)
        sel_valid = jnp.arange(k_sel) < own
        k_g = k_blocks[b_idx, h_idx, sel]
        v_g = v_blocks[b_idx, h_idx, sel]
        k_own = lax.dynamic_slice_in_dim(k, own * L, L, axis=2)
        v_own = lax.dynamic_slice_in_dim(v, own * L, L, axis=2)
        s_sel = jnp.einsum('bhqd,bhqnld->bhqnl', qc, k_g).astype(jnp.float32) * scale
        s_sel = jnp.where(sel_valid[:, None], s_sel, NEG_INF)
        key_pos = own * L + jnp.arange(L)
        s_own = jnp.einsum('bhqd,bhld->bhql', qc, k_own).astype(jnp.float32) * scale
        s_own = jnp.where(key_pos[None, :] <= q_pos[:, None], s_own, NEG_INF)
        logits = jnp.concatenate([s_sel.reshape(B, H, Q, k_sel * L), s_own], axis=-1)
        p = jax.nn.softmax(logits, axis=-1).astype(v.dtype)
        p_sel = p[..., :k_sel * L].reshape(B, H, Q, k_sel, L)
        p_own = p[..., k_sel * L:]
        return (jnp.einsum('bhqnl,bhqnld->bhqd', p_sel, v_g)
                + jnp.einsum('bhql,bhld->bhqd', p_own, v_own))

    out = lax.map(chunk_attn, (jnp.arange(n_chunks), q_chunks))
    return out.transpose(1, 0, 3, 2, 4).reshape(B, S, H * Dh)


def multiscale_pool(u, w_pool, pool_scale):
    B, S, _ = u.shape
    uf = u.astype(jnp.float32).reshape(B, S, POOL_GROUPS, POOL_GROUP_WIDTH)
    cs = jnp.pad(jnp.cumsum(uf, axis=1), ((0, 0), (1, 0), (0, 0), (0, 0)))
    t = jnp.arange(S)
    win = jnp.array(POOL_WINDOWS, dtype=jnp.int32)
    lo = jnp.maximum(t[:, None] + 1 - win[None, :], 0)
    g_idx = jnp.arange(POOL_GROUPS)[None, :]
    window_sum = cs[:, 1:] - cs[:, lo, g_idx]
    count = (t[:, None] + 1 - lo).astype(jnp.float32)
    mixed = (window_sum / count[None, :, :, None] - uf).astype(u.dtype)
    y = jnp.einsum('bsgc,gcd->bsgd', mixed, w_pool).reshape(B, S, POOL_WIDTH)
    return y * pool_scale


def conv_gated_mlp(h, w_up, conv_w, conv_b, w_down):
    S = h.shape[1]
    a = h @ w_up
    ap = jnp.pad(a, ((0, 0), (CONV_WIDTH - 1, 0), (0, 0)))
    c = conv_b + sum(conv_w[j] * ap[:, j:j + S] for j in range(CONV_WIDTH))
    gate, val = jnp.split(c, 2, axis=-1)
    return (jax.nn.silu(gate) * val) @ w_down


def setup_inputs(seed: int = 0) -> dict:
    key = jax.random.key(seed)
    ks = jax.random.split(key, 16)
    f32 = jnp.float32
    n = lambda k, shape, fan_in: jax.random.normal(k, shape, f32) * (fan_in ** -0.5)
    return {
        "x": jax.random.normal(ks[0], (BATCH, SEQ, D_MODEL), f32),
        "norm_mix_g": 1.0 + 0.02 * jax.random.normal(ks[1], (DEPTH, D_MODEL), f32),
        "w_in": n(ks[2], (DEPTH, D_MODEL, IN_WIDTH), D_MODEL),
        "w_pool": n(ks[3], (DEPTH, POOL_GROUPS, POOL_GROUP_WIDTH, POOL_GROUP_WIDTH), POOL_GROUP_WIDTH),
        "pool_scale": 1.0 + 0.02 * jax.random.normal(ks[4], (DEPTH, POOL_WIDTH), f32),
        "w_branch_a": n(ks[5], (DEPTH, ATTN_WIDTH, D_MODEL), ATTN_WIDTH),
        "w_branch_b": n(ks[6], (DEPTH, POOL_WIDTH, D_MODEL), POOL_WIDTH),
        "w_out": n(ks[7], (DEPTH, D_MODEL, D_MODEL), D_MODEL),
        "norm_ffn_g": 1.0 + 0.02 * jax.random.normal(ks[8], (DEPTH, D_MODEL), f32),
        "w_up": n(ks[9], (DEPTH, D_MODEL, 2 * D_FF), D_MODEL),
        "conv_w": n(ks[10], (DEPTH, CONV_WIDTH, 2 * D_FF), CONV_WIDTH),
        "conv_b": 0.02 * jax.random.normal(ks[11], (DEPTH, 2 * D_FF), f32),
        "w_down": n(ks[12], (DEPTH, D_FF, D_MODEL), D_FF),
        "norm_final_g": 1.0 + 0.02 * jax.random.normal(ks[13], (D_MODEL,), f32),
    }


def reference(x, norm_mix_g, w_in, w_pool, pool_scale, w_branch_a, w_branch_b, w_out,
              norm_ffn_g, w_up, conv_w, conv_b, w_down, norm_final_g):
    B, S, _ = x.shape
    splits = [ATTN_WIDTH, 2 * ATTN_WIDTH, 3 * ATTN_WIDTH, 3 * ATTN_WIDTH + POOL_WIDTH]
    for layer in range(DEPTH):
        h = rms_norm(x, norm_mix_g[layer])
        proj = h @ w_in[layer]
        q, k, v, u, gates = jnp.split(proj, splits, axis=-1)
        hs = (B, S, ATTN_HEADS, HEAD_DIM)
        y_a = moba_attention(q.reshape(hs), k.reshape(hs), v.reshape(hs)) @ w_branch_a[layer]
        y_b = multiscale_pool(u, w_pool[layer], pool_scale[layer]) @ w_branch_b[layer]
        g = jax.nn.sigmoid(gates.astype(jnp.float32)).astype(x.dtype).reshape(B, S, N_BRANCH, D_MODEL)
        merged = g[:, :, 0] * y_a + g[:, :, 1] * y_b
        x = x + merged @ w_out[layer]
        h = rms_norm(x, norm_ffn_g[layer])
        x = x + conv_gated_mlp(h, w_up[layer], conv_w[layer], conv_b[layer], w_down[layer])
    return rms_norm(x, norm_final_g)
```

```python
from contextlib import ExitStack
import numpy as np
import concourse.bass as bass
import concourse.mybir as mybir
from concourse.bass_utils import run_bass_kernel_spmd

F32 = mybir.dt.float32
BF16 = mybir.dt.bfloat16
ALU = mybir.AluOpType
AF = mybir.ActivationFunctionType
AX = mybir.AxisListType

D = 1024
S_LEN = 2048
L_DEPTH = 2
NCH = 4
DFF = 2816
NFT = DFF // 128
WINS = (2, 4, 8, 16)
NEG = -30000.0
EPS = 1e-6

PV_GMIX = 0
PV_GFFN = 8
PV_PSC = 16
PV_CW0 = 20
PV_CW1 = PV_CW0 + 44
PV_CW2 = PV_CW1 + 44
PV_CB = PV_CW2 + 44
PV_L = PV_CB + 44
PV_GFIN = 2 * PV_L
PV_INVC = PV_GFIN + 8
PV_EPS = PV_INVC + 64
NPV = PV_EPS + 1
CB_ID = 0
CB_ONES = 128
CB_ESEL = 256
CB_TRI = CB_ESEL + 2048
NCB = CB_TRI + 128


class Res:
    __slots__ = ("name", "w", "rs", "excl")

    def __init__(self, name, excl=False):
        self.name = name
        self.w = None
        self.rs = {}
        self.excl = excl


class Ev:
    __slots__ = ("key", "val", "vc")

    def __init__(self, key, val, vc):
        self.key, self.val, self.vc = key, val, vc


class Slot:
    def __init__(self, key, sem):
        self.key, self.sem, self.count = key, sem, 0


class Sched:
    EPOCH = 8192

    def __init__(self, nc, stack):
        self.nc, self.stack = nc, stack
        self.E = {"pe": nc.tensor, "act": nc.scalar, "dve": nc.vector,
                  "pool": nc.gpsimd, "sp": nc.sync}
        self.pos = {e: 0 for e in self.E}
        self.known = {e: {} for e in self.E}
        self.sems = {}
        self.last = {}
        self.nslot = 0
        self.scoped = set()
        self.dead = False

    def _sem(self, key, epoch):
        k = (key, epoch)
        if k not in self.sems:
            self.sems[k] = self.stack.enter_context(self.nc.semaphore("s_%s_%d" % (key, epoch)))
        return self.sems[k]

    def slot(self, name, scoped=False):
        self.nslot += 1
        key = "d_%s" % name
        if scoped:
            self.scoped.add(key)
        return Slot(key, self._sem(key, 0))

    def _wait(self, eng, ev):
        kn = self.known[eng]
        if kn.get(ev.key, 0) >= ev.val:
            return
        if ev.key in self.E:
            ep, v = divmod(ev.val - 1, self.EPOCH)
            self.E[eng].wait_ge(self._sem(ev.key, ep), v + 1)
        else:
            self.E[eng].wait_ge(self._sem(ev.key, 0), ev.val)
        for k, v in ev.vc.items():
            if kn.get(k, 0) < v:
                kn[k] = v

    def _sync(self, eng, reads, writes, is_dma):
        for r in reads:
            if r.w is not None:
                ev = r.w
                if ev.key == eng and eng == "pe" and not is_dma:
                    continue
                self._wait(eng, ev)
            if r.excl:
                for ev in list(r.rs.values()):
                    if ev.key != eng:
                        self._wait(eng, ev)
        for r in writes:
            if r.w is not None and (is_dma or r.w.key != eng or eng != "pe"):
                self._wait(eng, r.w)
            for ev in r.rs.values():
                if is_dma or ev.key != eng or eng != "pe":
                    self._wait(eng, ev)

    def _commit(self, ev, reads, writes):
        self.last[ev.key] = ev
        for r in reads:
            o = r.rs.get(ev.key)
            if o is None or o.val < ev.val:
                r.rs[ev.key] = ev
        for r in writes:
            r.w = ev
            r.rs = {}

    def op(self, eng, fn, reads=(), writes=()):
        if self.dead:
            return None
        self._sync(eng, reads, writes, False)
        ins = fn(self.E[eng])
        self.pos[eng] += 1
        p = self.pos[eng]
        ep, _ = divmod(p - 1, self.EPOCH)
        ins.then_inc(self._sem(eng, ep), 1)
        vc = dict(self.known[eng])
        vc[eng] = p
        ev = Ev(eng, p, vc)
        self._commit(ev, reads, writes)
        return ev

    def dma(self, q, slot, out, in_, reads=(), writes=()):
        if self.dead:
            return None
        self._sync(q, reads, writes, True)
        slot.count += 16
        self.E[q].dma_start(out=out, in_=in_).then_inc(slot.sem, 16)
        vc = dict(self.known[q])
        vc[slot.key] = slot.count
        ev = Ev(slot.key, slot.count, vc)
        self._commit(ev, reads, writes)
        return ev

    def snapshot(self):
        return [ev for k, ev in self.last.items() if k in self.E or k in self.scoped]

    def wait_events(self, eng, evs):
        if self.dead:
            return
        for ev in evs:
            if ev.key != eng:
                self._wait(eng, ev)

    def barrier(self, engines=("pe", "act", "dve", "sp")):
        if self.dead:
            return
        evs = [ev for k, ev in self.last.items() if k in self.E or k in self.scoped]
        for e in engines:
            for ev in evs:
                if ev.key == e:
                    continue
                self._wait(e, ev)


class _Stop(Exception):
    pass


def build(n_layers=L_DEPTH, final_norm=True, taps=(), ph=9):
    nc = bass.Bass("TRN2", target_bir_lowering=False)
    L = L_DEPTH

    def din(name, shape):
        return nc.dram_tensor(name, shape, F32, kind="ExternalInput").ap()

    xT_d = din("xT", [D, S_LEN])
    w_in_d = din("w_in", [L, D, 4096])
    w_pool_d = din("w_pool", [L, 4, 128, 128])
    w_a_d = din("w_branch_a", [L, 512, D])
    w_b_d = din("w_branch_b", [L, 512, D])
    w_out_d = din("w_out", [L, D, D])
    w_up_d = din("w_up", [L, D, 2 * DFF])
    w_down_d = din("w_down", [L, DFF, D])
    pvec_d = din("pvec", [128, NPV])
    cb_d = din("cb", [128, NCB])
    yT_d = nc.dram_tensor("yT", [D, S_LEN], F32, kind="ExternalOutput").ap()
    tap_d = {}
    tap_ev = []

    with ExitStack() as es:
        S = Sched(nc, es)

        uid = [0]

        def sb(stack, name, shape, dt):
            uid[0] += 1
            return stack.enter_context(nc.sbuf_tensor("%s_u%d" % (name, uid[0]), shape, dt))

        ps = es.enter_context(nc.psum_tensor("ps", [128, 4096], F32))
        PB = [Res("pb%d" % i, True) for i in range(8)]

        def bank(b, lo=0, hi=512):
            return ps[:, b * 512 + lo:b * 512 + hi]

        xT = sb(es, "xTs", [128, 8, S_LEN], F32)
        hT = sb(es, "hTs", [128, 8, S_LEN], BF16)
        pv = sb(es, "pv", [128, NPV], F32)
        cb = sb(es, "cbs", [128, NCB], BF16)
        R_x = [Res("x%d" % c) for c in range(NCH)]
        R_h = [Res("h%d" % c) for c in range(NCH)]
        R_pv = Res("pv")
        R_cb = Res("cb")
        ident = cb[:, CB_ID:CB_ID + 128]
        ones = cb[:, CB_ONES:CB_ONES + 128]
        tri = cb[:, CB_TRI:CB_TRI + 128]

        def esel(hp, n):
            o = CB_ESEL + (hp * 8 + n) * 128
            return cb[:, o:o + 128]

        w_qkv = [sb(es, "wqkv%d" % i, [128, 3, 8, 128], BF16) for i in range(2)]
        R_wqkv = [Res("wqkv%d" % i) for i in range(2)]
        sl_wqkv = [S.slot("wqkv%d" % i) for i in range(2)]
        w_u = [sb(es, "wu%d" % i, [128, 8, 128], BF16) for i in range(2)]
        w_pl = [sb(es, "wpl%d" % i, [128, 128], BF16) for i in range(2)]
        R_wu = [Res("wu%d" % i) for i in range(2)]
        sl_wu = [S.slot("wu%d" % i) for i in range(2)]
        w_m = [sb(es, "wm%d" % i, [128, 24, 128], BF16) for i in range(2)]
        R_wm = [Res("wm%d" % i) for i in range(2)]
        sl_wm = [S.slot("wm%d" % i) for i in range(2)]
        R_wo = [Res("wo%d" % i) for i in range(2)]
        sl_wo = [S.slot("wo%d" % i, True) for i in range(2)]
        NUP = 2
        R_wup = [Res("wup%d" % i) for i in range(NUP)]
        sl_wup = [S.slot("wup%d" % i, True) for i in range(NUP)]
        FPASS = ((0, 8), (8, 15), (15, 22))
        R_wdn = [Res("wdn%d" % i) for i in range(2)]
        sl_wdn = [S.slot("wdn%d" % i, True) for i in range(2)]

        sl_in = S.slot("in")
        sl_cb = S.slot("cb")
        sl_x = [S.slot("x%d" % c, True) for c in range(NCH)]
        sl_out = [S.slot("out%d" % c, True) for c in range(2)]
        sl_tap = S.slot("tap", True)

        def tap(name, ap, reads):
            if name not in taps or S.dead:
                return
            t = nc.dram_tensor("tap_" + name, list(ap.shape), ap.dtype, kind="ExternalOutput").ap()
            tap_d[name] = t
            tap_ev.append(S.dma("sp", S.slot("tap_" + name, True), t, ap, reads=reads))

        S.dma("sp", sl_in, pv[:, :], pvec_d, writes=[R_pv])
        S.dma("pool", sl_cb, cb[:, :], cb_d, writes=[R_cb])
        xv = xT_d.rearrange("(c p) t -> p c t", p=128)
        for c in range(NCH):
            S.dma("sp", sl_x[c], xT[:, :, c * 512:(c + 1) * 512], xv[:, :, c * 512:(c + 1) * 512],
                  writes=[R_x[c]])

        ctr = {"wu": 0, "wqkv": 0, "wm": 0, "wo": 0, "wup": 0, "wdn": 0, "pb": 0}

        def wview(d, l):
            return d[l].rearrange("(kc p) n -> p kc n", p=128)

        def rmsnorm(gcol, out_fn, tag):
            with ExitStack() as ns:
                sq = [sb(ns, "sq%s%d" % (tag, i), [128, 8, 512], BF16) for i in range(2)]
                R_sq = [Res("sq%d" % i) for i in range(2)]
                rs = [sb(ns, "rs%s%d" % (tag, i), [128, 512], F32) for i in range(2)]
                R_rs = [Res("rs%d" % i) for i in range(2)]
                for c in range(NCH):
                    i = c % 2
                    cs = slice(c * 512, (c + 1) * 512)
                    S.op("act", lambda e: e.activation(sq[i][:, :, :], xT[:, :, cs], AF.Square),
                         reads=[R_x[c]], writes=[R_sq[i]])
                    b = i
                    S.op("pe", lambda e: [e.matmul(bank(b), ones, sq[i][:, kc, :], start=(kc == 0),
                                                   stop=(kc == 7)) for kc in range(8)][-1],
                         reads=[R_sq[i], R_cb], writes=[PB[b]])
                    S.op("act", lambda e: e.activation(rs[i][:, :], bank(b), AF.Sqrt,
                                                       bias=pv[:, PV_EPS:PV_EPS + 1], scale=1.0 / D),
                         reads=[PB[b], R_pv], writes=[R_rs[i]])
                    S.op("dve", lambda e: e.reciprocal(rs[i][:, :], rs[i][:, :]),
                         reads=[R_rs[i]], writes=[R_rs[i]])
                    dst, dres = out_fn(c)
                    for kc in range(8):
                        S.op("dve", lambda e: e.scalar_tensor_tensor(
                            dst[:, kc, :], xT[:, kc, cs], pv[:, gcol + kc:gcol + kc + 1], rs[i][:, :],
                            ALU.mult, ALU.mult),
                            reads=[R_x[c], R_rs[i], R_pv], writes=dres)
                S.barrier()

        def h_out(c):
            return hT[:, :, c * 512:(c + 1) * 512], [R_h[c]]

        try:
            for l in range(n_layers):
                pb = l * PV_L
                rmsnorm(pb + PV_GMIX, h_out, "a%d" % l)
                if l == 0:
                    tap("h0", hT[:, :, :], R_h)
                if ph == 1 and l == n_layers - 1: S.dead = True
                win = wview(w_in_d, l)

                with ExitStack() as ms:
                    pooledT = sb(ms, "pooledT", [128, 4, S_LEN], BF16)
                    attnT = sb(ms, "attnT", [128, 4, S_LEN], BF16)
                    R_pool = [Res("pool%d" % c) for c in range(NCH)]
                    R_attn = [Res("attn%d" % c) for c in range(NCH)]

                    with ExitStack() as pscope:
                        uT = sb(pscope, "uT", [128, S_LEN], F32)
                        uT2 = sb(pscope, "uT2", [128, S_LEN], F32)
                        tA = sb(pscope, "tA", [128, S_LEN], F32)
                        tB = sb(pscope, "tB", [128, S_LEN], F32)
                        mx = sb(pscope, "mx", [128, S_LEN], BF16)
                        t16 = sb(pscope, "t16", [128, 16], F32)
                        R_u, R_A, R_B, R_mx, R_t16 = Res("u"), Res("tA"), Res("tB"), Res("mx"), Res("t16")

                        def load_u(g):
                            i = ctr["wu"] % 2
                            ctr["wu"] += 1
                            S.dma("pool", sl_wu[i], w_u[i][:, :, :],
                                  win[:, :, 1536 + g * 128:1536 + (g + 1) * 128], writes=[R_wu[i]])
                            S.dma("pool", sl_wu[i], w_pl[i][:, :], w_pool_d[l, g], writes=[R_wu[i]])
                            return i

                        def u_mm(g, wi):
                            hb = 4 * (g % 2)
                            for c in range(NCH):
                                cs = slice(c * 512, (c + 1) * 512)
                                S.op("pe", lambda e: [e.matmul(bank(hb + c), w_u[wi][:, kc, :], hT[:, kc, cs],
                                                               start=(kc == 0), stop=(kc == 7))
                                                      for kc in range(8)][-1],
                                     reads=[R_wu[wi], R_h[c]], writes=[PB[hb + c]])

                        wis = {0: load_u(0)}
                        u_mm(0, wis[0])
                        uTs = [uT, uT2]
                        R_us = [R_u, Res("u2")]
                        for g in range(4):
                            wi = wis[g]
                            hb = 4 * (g % 2)
                            uTg, R_ug = uTs[g % 2], R_us[g % 2]
                            S.op("act", lambda e: e.activation(uTg[:, :], ps[:, hb * 512:(hb + 4) * 512], AF.Copy),
                                 reads=PB[hb:hb + 4], writes=[R_ug])
                            if g < 3:
                                wis[g + 1] = load_u(g + 1)
                                u_mm(g + 1, wis[g + 1])
                            src, rsrc = uTg, R_ug
                            tmp = [(tA, R_A), (tB, R_B)]
                            d = 1
                            k = 0
                            while d < WINS[g]:
                                dst, rdst = tmp[k % 2]
                                k += 1
                                S.op("dve", lambda e: e.tensor_tensor(dst[:, d:], src[:, d:], src[:, :S_LEN - d],
                                                                      ALU.add),
                                     reads=[rsrc], writes=[rdst])
                                S.op("dve", lambda e: e.tensor_copy(dst[:, :d], src[:, :d]),
                                     reads=[rsrc], writes=[rdst])
                                src, rsrc = dst, rdst
                                d *= 2
                            w = WINS[g]
                            S.op("dve", lambda e: e.scalar_tensor_tensor(mx[:, :], src[:, :], 1.0 / w, uTg[:, :],
                                                                          ALU.mult, ALU.subtract),
                                 reads=[rsrc, R_ug], writes=[R_mx])
                            S.op("dve", lambda e: e.tensor_tensor(t16[:, :], src[:, :16],
                                                                  pv[:, PV_INVC + g * 16:PV_INVC + (g + 1) * 16],
                                                                  ALU.mult),
                                 reads=[rsrc, R_pv], writes=[R_t16])
                            S.op("dve", lambda e: e.tensor_tensor(mx[:, :16], t16[:, :], uTg[:, :16], ALU.subtract),
                                 reads=[R_t16, R_ug], writes=[R_mx])
                            for c in range(NCH):
                                cs = slice(c * 512, (c + 1) * 512)
                                S.op("pe", lambda e: e.matmul(bank(hb + c), w_pl[wi][:, :], mx[:, cs],
                                                              start=True, stop=True),
                                     reads=[R_wu[wi], R_mx], writes=[PB[hb + c]])
                                S.op("act", lambda e: e.activation(
                                    pooledT[:, g, cs], bank(hb + c), AF.Copy,
                                    scale=pv[:, pb + PV_PSC + g:pb + PV_PSC + g + 1]),
                                    reads=[PB[hb + c], R_pv], writes=[R_pool[c]])
                        S.barrier()
                    if l == 0:
                        tap("pooled0", pooledT[:, :, :], R_pool)

                    if ph == 2 and l == n_layers - 1: S.dead = True
                    with ExitStack() as ascope:
                        NSET = 2
                        qz = [[sb(ascope, "qz%d_%d" % (h_, i), [128, S_LEN], BF16) for i in range(NSET)]
                              for h_ in range(2)]
                        kT = [sb(ascope, "kT%d" % i, [128, S_LEN], BF16) for i in range(NSET)]
                        Vt = [sb(ascope, "V%d" % i, [128, 16, 2, 65], BF16) for i in range(NSET)]
                        nmT = [sb(ascope, "nm%d" % i, [128, 1024], BF16) for i in range(NSET)]
                        ksum = [sb(ascope, "ks%d" % i, [128, 8], F32) for i in range(NSET)]
                        ksb = [sb(ascope, "ksb%d" % i, [128, 16], BF16) for i in range(NSET)]
                        gpad = sb(ascope, "gpad", [128, 2, 8], F32)
                        top8 = sb(ascope, "top8", [128, 2, 8], F32)
                        gw1 = sb(ascope, "gw1", [128, 2, 8], F32)
                        gw2 = sb(ascope, "gw2", [128, 2, 8], F32)
                        R_gw1, R_gw2 = Res("gw1"), Res("gw2")
                        btok = sb(ascope, "btok", [128, 8, 64], BF16)
                        Pt = [sb(ascope, "P%d" % i, [128, 512], BF16) for i in range(3)]
                        rec = sb(ascope, "rec", [128, 4], F32)
                        atok = [sb(ascope, "atok%d" % i, [128, 4, 128], BF16) for i in range(2)]
                        R_q = [[Res("q") for _ in range(NCH)] for _ in range(NSET)]
                        R_k = [[Res("k") for _ in range(NCH)] for _ in range(NSET)]
                        R_v = [[Res("v") for _ in range(NCH)] for _ in range(NSET)]
                        R_nm = [[Res("nm") for _ in range(2)] for _ in range(NSET)]
                        R_ks = [Res("ks") for _ in range(NSET)]
                        R_ksb = [Res("ksb") for _ in range(NSET)]
                        R_gp, R_t8, R_bt = Res("gp"), Res("t8"), Res("bt")
                        R_P = [Res("P") for _ in range(3)]
                        R_rec = Res("rec")
                        R_at = [Res("at") for _ in range(2)]
                        pctr = {"p": 0, "s": 0, "o": 0, "proj": 0}

                        for i in range(NSET):
                            S.op("dve", lambda e: e.memset(Vt[i][:, :, :, 64:65], 1.0), writes=R_v[i])
                        S.op("dve", lambda e: e.memset(btok[:, :, :], 0.0), writes=[R_bt])
                        for i in range(NSET):
                            S.op("dve", lambda e: e.memset(ksb[i][:, :], 0.0), writes=[R_ksb[i]])
                            S.op("dve", lambda e: e.memset(qz[0][i][64:128, :], 0.0), writes=R_q[i])
                            S.op("dve", lambda e: e.memset(qz[1][i][0:64, :], 0.0), writes=R_q[i])
                            S.op("dve", lambda e: e.memset(nmT[i][64:128, :], 0.0), writes=R_nm[i])

                        if ph == 211:
                            S.dead = True

                        def load_qkv(m):
                            i = ctr["wqkv"] % 2
                            ctr["wqkv"] += 1
                            for j in range(3):
                                S.dma("pool", sl_wqkv[i], w_qkv[i][:, j, :, :],
                                      win[:, :, j * 512 + m * 128:j * 512 + (m + 1) * 128], writes=[R_wqkv[i]])
                            return i

                        def project(m, wi, st):
                            for c in range(NCH):
                                cs = slice(c * 512, (c + 1) * 512)
                                for j, (dstT, rr) in enumerate(((None, R_q[st]), (kT[st], R_k[st]))):
                                    b = pctr["proj"] % 2
                                    pctr["proj"] += 1
                                    S.op("pe", lambda e: [e.matmul(bank(b), w_qkv[wi][:, j, kc, :], hT[:, kc, cs],
                                                                   start=(kc == 0), stop=(kc == 7))
                                                          for kc in range(8)][-1],
                                         reads=[R_wqkv[wi], R_h[c]], writes=[PB[b]])
                                    if j == 0:
                                        for h_ in range(2):
                                            S.op("act", lambda e: e.activation(
                                                qz[h_][st][h_ * 64:(h_ + 1) * 64, cs],
                                                ps[h_ * 64:(h_ + 1) * 64, b * 512:(b + 1) * 512], AF.Copy),
                                                reads=[PB[b]], writes=[rr[c]])
                                    else:
                                        S.op("act", lambda e: e.activation(dstT[:, cs], bank(b), AF.Copy),
                                             reads=[PB[b]], writes=[rr[c]])
                                    if j == 1 and ph != 212:
                                        S.op("dve", lambda e: e.tensor_reduce(
                                            ksum[st][:, 2 * c:2 * c + 2],
                                            bank(b).rearrange("p (n k) -> p n k", k=256), AX.X, ALU.add),
                                            reads=[PB[b]], writes=[R_ks[st]])
                            if ph == 212:
                                S.dead = True
                            for hp in range(2):
                                S.op("dve", lambda e: e.tensor_copy(ksb[st][hp * 64:(hp + 1) * 64, hp * 8:hp * 8 + 8],
                                                                    ksum[st][hp * 64:(hp + 1) * 64, :]),
                                     reads=[R_ks[st]], writes=[R_ksb[st]])
                            if ph == 213:
                                S.dead = True
                            for tg in range(4):
                                b = pctr["proj"] % 2
                                pctr["proj"] += 1

                                def vmm(e):
                                    ins = None
                                    for t4 in range(4):
                                        tt = tg * 4 + t4
                                        for kc in range(8):
                                            ins = e.matmul(bank(b, t4 * 128, (t4 + 1) * 128),
                                                           hT[:, kc, tt * 128:(tt + 1) * 128],
                                                           w_qkv[wi][:, 2, kc, :], start=(kc == 0), stop=(kc == 7))
                                    return ins
                                S.op("pe", vmm, reads=[R_wqkv[wi], R_h[tg]], writes=[PB[b]])
                                S.op("dve", lambda e: e.tensor_copy(
                                    Vt[st][:, tg * 4:(tg + 1) * 4, :, 0:64],
                                    bank(b).rearrange("p (t h d) -> p t h d", t=4, h=2)),
                                    reads=[PB[b]], writes=[R_v[st][tg]])

                        def masks_a(m, st):
                            GOFF = 1.0e4
                            S.op("dve", lambda e: e.memset(gpad[:, :, :], 0.0), writes=[R_gp])

                            def gmm(e):
                                ins = None
                                for j in range(8):
                                    qs = slice((8 + j) * 128, (9 + j) * 128)
                                    e.matmul(bank(6, j * 16, j * 16 + 16), qz[0][st][:, qs], ksb[st][:, :],
                                             start=True, stop=False, skip_group_check=True)
                                    ins = e.matmul(bank(6, j * 16, j * 16 + 16), qz[1][st][:, qs], ksb[st][:, :],
                                                   start=False, stop=True, skip_group_check=True)
                                return ins
                            S.op("pe", gmm, reads=[R_q[st][2], R_q[st][3], R_ksb[st]], writes=[PB[6]])
                            chains = []
                            for j in range(8):
                                chains.append(lambda j=j: mask_chain(st, j, GOFF))
                            return chains

                        def mask_chain(st, j, GOFF):
                            if True:
                                qt = 8 + j
                                own = qt // 2
                                S.op("dve", lambda e: e.tensor_scalar(
                                    gpad[:, :, 0:own],
                                    bank(6, j * 16, j * 16 + 16).rearrange("p (h n) -> p h n", h=2)[:, :, 0:own],
                                    GOFF, None, ALU.add),
                                    reads=[PB[6]], writes=[R_gp])
                                src, rsrc = gpad, R_gp
                                for it_ in range(2):
                                    S.op("dve", lambda e: e.tensor_reduce(top8[:, :, it_], src[:, :, :], AX.X, ALU.max),
                                         reads=[rsrc], writes=[R_t8])
                                    dst, rdst = (gw1, R_gw1) if it_ == 0 else (gw2, R_gw2)
                                    for hp in range(2):
                                        S.op("dve", lambda e: e.scalar_tensor_tensor(
                                            dst[:, hp, :], src[:, hp, :], top8[:, hp, it_:it_ + 1], src[:, hp, :],
                                            ALU.is_lt, ALU.mult),
                                            reads=[rsrc, R_t8], writes=[rdst])
                                    src, rsrc = dst, rdst
                                S.op("dve", lambda e: e.tensor_reduce(top8[:, :, 2], src[:, :, :], AX.X, ALU.max),
                                     reads=[rsrc], writes=[R_t8])
                                for hp in range(2):
                                    S.op("dve", lambda e: e.tensor_scalar(
                                        btok[:, j, hp * 32:hp * 32 + 8], gpad[:, hp, :], top8[:, hp, 2:3], NEG,
                                        ALU.is_lt, ALU.mult),
                                        reads=[R_gp, R_t8], writes=[R_bt])
                                S.op("dve", lambda e: e.memset(
                                    btok[:, j, :].rearrange("p (h c) -> p h c", h=2)[:, :, own:own + 1], 0.0),
                                    writes=[R_bt])

                        def masks_b(m, st):
                            for c2 in range(2):
                                trb = ps[:, 7 * 512:8 * 512].bitcast(BF16)

                                def tmm(e):
                                    ins = None
                                    for j in range(4):
                                        ins = e.transpose(trb[0:64, j * 128:(j + 1) * 128], btok[:, c2 * 4 + j, :], ident)
                                    return ins
                                S.op("pe", tmm, reads=[R_bt, R_cb], writes=[PB[7]])
                                S.op("dve", lambda e: e.tensor_copy(nmT[st][0:64, c2 * 512:(c2 + 1) * 512],
                                                                    trb[0:64, 0:512]),
                                     reads=[PB[7]], writes=[R_nm[st][c2]])

                        def attention(m, st, inter=()):
                            inter = list(inter)
                            tiles = [(c, hp, kt) for c in range(NCH) for hp in range(2) for kt in range(4 * c + 4)]
                            info = {}

                            def smm_op(i):
                                c, hp, kt = tiles[i]
                                hs = slice(hp * 64, (hp + 1) * 64)
                                r = kt - 4 * c
                                col0 = 128 * max(r, 0)
                                sbk = (2, 3, 0)[i % 3]
                                need_bias = (c >= 2) and (kt < 4 * c + 2)

                                def smm(e):
                                    last = not need_bias and r < 0
                                    ins = e.matmul(bank(sbk, col0, 512), kT[st][:, kt * 128:(kt + 1) * 128],
                                                   qz[hp][st][:, c * 512 + col0:(c + 1) * 512],
                                                   start=True, stop=last, skip_group_check=True)
                                    if need_bias:
                                        ins = e.matmul(bank(sbk, col0, 512), esel(hp, kt // 2),
                                                       nmT[st][:, (c - 2) * 512 + col0:(c - 1) * 512],
                                                       start=False, stop=(r < 0), skip_group_check=True)
                                    if r >= 0:
                                        ins = e.matmul(bank(sbk, col0, col0 + 128), ident, tri,
                                                       start=False, stop=True, skip_group_check=True)
                                    return ins
                                rd = [R_k[st][kt // 4], R_q[st][c], R_cb]
                                if need_bias:
                                    rd.append(R_nm[st][c - 2])
                                S.op("pe", smm, reads=rd, writes=[PB[sbk]])
                                info[i] = (sbk, col0, r)

                            pend = []

                            def flush_pending(force=False):
                                while pend and (force or pend[0][0] <= 0):
                                    pend.pop(0)[1]()
                                for p_ in pend:
                                    p_[0] -= 1

                            smm_op(0)
                            smm_op(1)
                            ob = None
                            for i, (c, hp, kt) in enumerate(tiles):
                                hs = slice(hp * 64, (hp + 1) * 64)
                                ab = c % 2
                                if kt == 0:
                                    ob = 4 + pctr["o"] % 2
                                    pctr["o"] += 1
                                if i + 2 < len(tiles):
                                    smm_op(i + 2)
                                sbk, col0, r = info[i]
                                pi = pctr["p"] % 3
                                pctr["p"] += 1
                                S.op("act", lambda e: e.activation(Pt[pi][:, col0:512], bank(sbk, col0, 512),
                                                                   AF.Exp, scale=0.125),
                                     reads=[PB[sbk]], writes=[R_P[pi]])

                                def pvmm(e):
                                    ins = None
                                    for j in range(max(r, 0), 4):
                                        ins = e.matmul(bank(ob, j * 128, j * 128 + 65),
                                                       Pt[pi][:, j * 128:(j + 1) * 128], Vt[st][:, kt, hp, :],
                                                       start=(kt == 0 and j == 0), stop=(kt == 4 * c + j),
                                                       skip_group_check=True)
                                    return ins
                                S.op("pe", pvmm, reads=[R_P[pi], R_v[st][kt // 4]], writes=[PB[ob]])
                                flush_pending()
                                if kt == 4 * c + 3:
                                    o3 = bank(ob).rearrange("p (j d) -> p j d", j=4)
                                    S.op("dve", lambda e: e.reciprocal(rec[:, :], o3[:, :, 64]),
                                         reads=[PB[ob]], writes=[R_rec])
                                    S.op("dve", lambda e: e.tensor_tensor(
                                        atok[ab][:, :, hs], o3[:, :, 0:64],
                                        rec[:, :].unsqueeze(2).to_broadcast([128, 4, 64]), ALU.mult),
                                        reads=[PB[ob], R_rec], writes=[R_at[ab]])
                                    if inter:
                                        inter.pop(0)()
                                    if hp == 1:
                                        def do_tr(c=c, ab=ab):
                                            trb = ps[:, 7 * 512:8 * 512].bitcast(BF16)

                                            def tmm2(e):
                                                ins = None
                                                for j in range(4):
                                                    ins = e.transpose(trb[:, j * 128:(j + 1) * 128],
                                                                      atok[ab][:, j, :], ident)
                                                return ins
                                            S.op("pe", tmm2, reads=[R_at[ab], R_cb], writes=[PB[7]])
                                            S.op("dve", lambda e: e.tensor_copy(
                                                attnT[:, m, c * 512:(c + 1) * 512], trb[:, 0:512]),
                                                reads=[PB[7]], writes=[R_attn[c]])
                                        pend.append([2, do_tr])
                            flush_pending(force=True)
                            while inter:
                                inter.pop(0)()

                        wnext = load_qkv(0)
                        project(0, wnext, 0)
                        for ch in masks_a(0, 0):
                            ch()
                        for m in range(4):
                            st = m % NSET
                            if m < 3:
                                wnext = load_qkv(m + 1)
                                project(m + 1, wnext, (m + 1) % NSET)
                            masks_b(m, st)
                            if l == 0 and m == 0:
                                tap("q0", qz[0][0][:, :], R_q[0])
                                tap("k0", kT[0][:, :], R_k[0])
                                tap("v0", Vt[0][:, :, :, :], R_v[0])
                                tap("nm0", nmT[0][0:64, :], R_nm[0])
                            inter = masks_a(m + 1, (m + 1) % NSET) if m < 3 else ()
                            attention(m, st, inter)
                        S.barrier()
                    if l == 0:
                        tap("attn0", attnT[:, :, :], R_attn)

                    if ph == 3 and l == n_layers - 1: S.dead = True
                    with ExitStack() as gscope:
                        snap_g = S.snapshot()
                        mergedT = sb(gscope, "mergedT", [128, 8, S_LEN], BF16)
                        w_o = [sb(gscope, "wo%d" % i, [128, 8, 128], BF16) for i in range(2)]
                        R_mg = [Res("mg%d" % c) for c in range(NCH)]
                        s0 = [sb(gscope, "s0_%d" % i, [128, 512], F32) for i in range(2)]
                        s1 = [sb(gscope, "s1_%d" % i, [128, 512], F32) for i in range(2)]
                        R_s0 = [Res("s0") for _ in range(2)]
                        R_s1 = [Res("s1") for _ in range(2)]
                        wa_v = wview(w_a_d, l)
                        wb_v = wview(w_b_d, l)

                        def load_m(m):
                            i = ctr["wm"] % 2
                            ctr["wm"] += 1
                            ms_ = slice(m * 128, (m + 1) * 128)
                            S.dma("pool", sl_wm[i], w_m[i][:, 0:4, :], wa_v[:, :, ms_], writes=[R_wm[i]])
                            S.dma("pool", sl_wm[i], w_m[i][:, 4:8, :], wb_v[:, :, ms_], writes=[R_wm[i]])
                            S.dma("pool", sl_wm[i], w_m[i][:, 8:16, :], win[:, :, 2048 + m * 128:2048 + (m + 1) * 128],
                                  writes=[R_wm[i]])
                            S.dma("pool", sl_wm[i], w_m[i][:, 16:24, :], win[:, :, 3072 + m * 128:3072 + (m + 1) * 128],
                                  writes=[R_wm[i]])
                            return i

                        nxt = load_m(0)
                        it = 0
                        for m in range(8):
                            wi = nxt
                            if m < 7:
                                nxt = load_m(m + 1)
                            for c in range(NCH):
                                cs = slice(c * 512, (c + 1) * 512)
                                b0 = 4 * (it % 2)
                                ti = it % 2
                                it += 1
                                S.op("pe", lambda e: [e.matmul(bank(b0), w_m[wi][:, kc, :], attnT[:, kc, cs],
                                                               start=(kc == 0), stop=(kc == 3)) for kc in range(4)][-1],
                                     reads=[R_wm[wi], R_attn[c]], writes=[PB[b0]])
                                S.op("pe", lambda e: [e.matmul(bank(b0 + 1), w_m[wi][:, 4 + kc, :], pooledT[:, kc, cs],
                                                               start=(kc == 0), stop=(kc == 3)) for kc in range(4)][-1],
                                     reads=[R_wm[wi], R_pool[c]], writes=[PB[b0 + 1]])
                                S.op("pe", lambda e: [e.matmul(bank(b0 + 2), w_m[wi][:, 8 + kc, :], hT[:, kc, cs],
                                                               start=(kc == 0), stop=(kc == 7)) for kc in range(8)][-1],
                                     reads=[R_wm[wi], R_h[c]], writes=[PB[b0 + 2]])
                                S.op("pe", lambda e: [e.matmul(bank(b0 + 3), w_m[wi][:, 16 + kc, :], hT[:, kc, cs],
                                                               start=(kc == 0), stop=(kc == 7)) for kc in range(8)][-1],
                                     reads=[R_wm[wi], R_h[c]], writes=[PB[b0 + 3]])
                                S.op("act", lambda e: e.activation(s0[ti][:, :], bank(b0 + 2), AF.Sigmoid),
                                     reads=[PB[b0 + 2]], writes=[R_s0[ti]])
                                S.op("act", lambda e: e.activation(s1[ti][:, :], bank(b0 + 3), AF.Sigmoid),
                                     reads=[PB[b0 + 3]], writes=[R_s1[ti]])
                                S.op("dve", lambda e: e.tensor_tensor(s0[ti][:, :], s0[ti][:, :], bank(b0), ALU.mult),
                                     reads=[R_s0[ti], PB[b0]], writes=[R_s0[ti]])
                                S.op("dve", lambda e: e.tensor_tensor(s1[ti][:, :], s1[ti][:, :], bank(b0 + 1), ALU.mult),
                                     reads=[R_s1[ti], PB[b0 + 1]], writes=[R_s1[ti]])
                                S.op("dve", lambda e: e.tensor_tensor(mergedT[:, m, cs], s0[ti][:, :], s1[ti][:, :],
                                                                      ALU.add),
                                     reads=[R_s0[ti], R_s1[ti]], writes=[R_mg[c]])
                        if l == 0:
                            tap("merged0", mergedT[:, :, :], R_mg)
                        wo_v = wview(w_out_d, l)

                        def load_o(m):
                            i = ctr["wo"] % 2
                            ctr["wo"] += 1
                            S.dma("pool", sl_wo[i], w_o[i][:, :, :], wo_v[:, :, m * 128:(m + 1) * 128], writes=[R_wo[i]])
                            return i

                        S.wait_events("pool", snap_g)
                        nxt = load_o(0)
                        for m in range(8):
                            wi = nxt
                            if m < 7:
                                nxt = load_o(m + 1)
                            for c in range(NCH):
                                cs = slice(c * 512, (c + 1) * 512)
                                b = ctr["pb"] % 8
                                ctr["pb"] += 1
                                S.op("pe", lambda e: [e.matmul(bank(b), w_o[wi][:, kc, :], mergedT[:, kc, cs],
                                                               start=(kc == 0), stop=(kc == 7)) for kc in range(8)][-1],
                                     reads=[R_wo[wi], R_mg[c]], writes=[PB[b]])
                                S.op("dve", lambda e: e.tensor_tensor(xT[:, m, cs], xT[:, m, cs], bank(b), ALU.add),
                                     reads=[R_x[c], PB[b]], writes=[R_x[c]])
                        S.barrier()
                if l == 0:
                    tap("x0m", xT[:, :, :], R_x)

                if ph == 4 and l == n_layers - 1: S.dead = True
                snap_f = S.snapshot()
                wup_v = wview(w_up_d, l)
                wdn_v = wview(w_down_d, l)
                with ExitStack() as fs:
                    hid = sb(fs, "hid", [128, 8, S_LEN], BF16)
                    w_up = [sb(fs, "wup%d" % i, [128, 2, 8, 128], BF16) for i in range(NUP)]
                    w_dn = [sb(fs, "wdn%d" % i, [128, 8, 128], BF16) for i in range(2)]
                    R_hid = [Res("hid%d" % i) for i in range(8)]
                    S.wait_events("pool", snap_f)
                    pre_up = [None]
                    deferred_norm = [True]
                    tg2 = None
                    R_tg2 = [Res("tg%d" % i) for i in range(2)]
                    R_tv2 = [Res("tv%d" % i) for i in range(2)]

                    def load_up(i_):
                        i = ctr["wup"] % NUP
                        ctr["wup"] += 1
                        S.dma("pool", sl_wup[i], w_up[i][:, 0, :, :], wup_v[:, :, i_ * 128:(i_ + 1) * 128],
                              writes=[R_wup[i]])
                        S.dma("pool", sl_wup[i], w_up[i][:, 1, :, :],
                              wup_v[:, :, DFF + i_ * 128:DFF + (i_ + 1) * 128], writes=[R_wup[i]])
                        return i

                    def load_dn(m, lo, hi):
                        i = ctr["wdn"] % 2
                        ctr["wdn"] += 1
                        S.dma("pool", sl_wdn[i], w_dn[i][:, 0:hi - lo, :], wdn_v[:, lo:hi, m * 128:(m + 1) * 128],
                              writes=[R_wdn[i]])
                        return i

                    pre_up[0] = load_up(0)
                    rmsnorm(pb + PV_GFFN, h_out, "f%d" % l)
                    tg2 = [sb(fs, "tg%d" % i, [128, S_LEN], F32) for i in range(2)]
                    tv2 = [sb(fs, "tv%d" % i, [128, S_LEN], F32) for i in range(2)]
                    for (lo, hi) in FPASS:
                        nxt = pre_up[0] if lo == 0 else load_up(lo)
                        for i_ in range(lo, hi):
                            wi = nxt
                            if i_ + 1 < hi:
                                nxt = load_up(i_ + 1)
                            il = i_ - lo
                            tg, tv, R_tg, R_tv = tg2[i_ % 2], tv2[i_ % 2], R_tg2[i_ % 2], R_tv2[i_ % 2]
                            for half, (tt, rt) in enumerate(((tg, R_tg), (tv, R_tv))):
                                hb = 4 * half
                                col = i_ + 22 * half
                                for c in range(NCH):
                                    cs = slice(c * 512, (c + 1) * 512)
                                    S.op("pe", lambda e: [e.matmul(bank(hb + c), w_up[wi][:, half, kc, :], hT[:, kc, cs],
                                                                   start=(kc == 0), stop=(kc == 7))
                                                          for kc in range(8)][-1],
                                         reads=[R_wup[wi], R_h[c]], writes=[PB[hb + c]])
                                pfull = ps[:, hb * 512:(hb + 4) * 512]
                                c0 = pv[:, pb + PV_CW0 + col:pb + PV_CW0 + col + 1]
                                c1 = pv[:, pb + PV_CW1 + col:pb + PV_CW1 + col + 1]
                                c2 = pv[:, pb + PV_CW2 + col:pb + PV_CW2 + col + 1]
                                cbias = pv[:, pb + PV_CB + col:pb + PV_CB + col + 1]
                                S.op("act", lambda e: e.activation(tt[:, :], pfull, AF.Identity, bias=cbias, scale=c2),
                                     reads=PB[hb:hb + 4] + [R_pv], writes=[rt])
                                S.op("dve", lambda e: e.scalar_tensor_tensor(
                                    tt[:, 1:], pfull[:, 0:S_LEN - 1], c1, tt[:, 1:], ALU.mult, ALU.add),
                                    reads=PB[hb:hb + 4] + [rt, R_pv], writes=[rt])
                                S.op("dve", lambda e: e.scalar_tensor_tensor(
                                    tt[:, 2:], pfull[:, 0:S_LEN - 2], c0, tt[:, 2:], ALU.mult, ALU.add),
                                    reads=PB[hb:hb + 4] + [rt, R_pv], writes=[rt])
                                if half == 0:
                                    S.op("act", lambda e: e.activation(tt[:, :], tt[:, :], AF.Silu),
                                         reads=[rt], writes=[rt])
                            S.op("pool", lambda e: e.tensor_tensor(hid[:, il, :], tg[:, :], tv[:, :], ALU.mult),
                                 reads=[R_tg, R_tv], writes=[R_hid[il]])
                        if l == 0 and lo == 0:
                            tap("hid0", hid[:, :, :], R_hid)
                        nk = hi - lo
                        nxt = load_dn(0, lo, hi)
                        for m in range(8):
                            wi = nxt
                            if m < 7:
                                nxt = load_dn(m + 1, lo, hi)
                            for c in range(NCH):
                                cs = slice(c * 512, (c + 1) * 512)
                                b = ctr["pb"] % 8
                                ctr["pb"] += 1
                                S.op("pe", lambda e: [e.matmul(bank(b), w_dn[wi][:, kc, :], hid[:, kc, cs],
                                                               start=(kc == 0), stop=(kc == nk - 1))
                                                      for kc in range(nk)][-1],
                                     reads=[R_wdn[wi]] + R_hid[0:nk], writes=[PB[b]])
                                S.op("dve", lambda e: e.tensor_tensor(xT[:, m, cs], xT[:, m, cs], bank(b), ALU.add),
                                     reads=[R_x[c], PB[b]], writes=[R_x[c]])
                    S.barrier()
                if l == 0:
                    tap("x0f", xT[:, :, :], R_x)

        except _Stop:
            pass
        S.dead = False
        S.barrier()

        with ExitStack() as os_:
            stg = [sb(os_, "stg%d" % i, [128, 8, 512], F32) for i in range(2)]
            R_stg = [Res("stg%d" % i) for i in range(2)]
            yv = yT_d.rearrange("(c p) t -> p c t", p=128)
            outs = []

            def y_out(c):
                return stg[c % 2][:, :, :], [R_stg[c % 2]]

            if final_norm:
                with ExitStack() as ns:
                    sq = [sb(ns, "sqz%d" % i, [128, 8, 512], BF16) for i in range(2)]
                    R_sq = [Res("sq%d" % i) for i in range(2)]
                    rs = [sb(ns, "rsz%d" % i, [128, 512], F32) for i in range(2)]
                    R_rs = [Res("rs%d" % i) for i in range(2)]
                    for c in range(NCH):
                        i = c % 2
                        cs = slice(c * 512, (c + 1) * 512)
                        S.op("act", lambda e: e.activation(sq[i][:, :, :], xT[:, :, cs], AF.Square),
                             reads=[R_x[c]], writes=[R_sq[i]])
                        b = i
                        S.op("pe", lambda e: [e.matmul(bank(b), ones, sq[i][:, kc, :], start=(kc == 0),
                                                       stop=(kc == 7)) for kc in range(8)][-1],
                             reads=[R_sq[i], R_cb], writes=[PB[b]])
                        S.op("act", lambda e: e.activation(rs[i][:, :], bank(b), AF.Sqrt,
                                                           bias=pv[:, PV_EPS:PV_EPS + 1], scale=1.0 / D),
                             reads=[PB[b], R_pv], writes=[R_rs[i]])
                        S.op("dve", lambda e: e.reciprocal(rs[i][:, :], rs[i][:, :]),
                             reads=[R_rs[i]], writes=[R_rs[i]])
                        for kc in range(8):
                            S.op("dve", lambda e: e.scalar_tensor_tensor(
                                stg[i][:, kc, :], xT[:, kc, cs], pv[:, PV_GFIN + kc:PV_GFIN + kc + 1], rs[i][:, :],
                                ALU.mult, ALU.mult),
                                reads=[R_x[c], R_rs[i], R_pv], writes=[R_stg[i]])
                        outs.append(S.dma("sp", sl_out[i], yv[:, :, cs], stg[i][:, :, :], reads=[R_stg[i]]))
            else:
                for c in range(NCH):
                    cs = slice(c * 512, (c + 1) * 512)
                    outs.append(S.dma("sp", sl_out[c % 2], yv[:, :, cs], xT[:, :, cs], reads=[R_x[c]]))
            lastout = {}
            for ev in outs:
                lastout[ev.key] = ev
            for ev in lastout.values():
                S._wait("sp", ev)
            for ev in tap_ev:
                S._wait("sp", ev)
            S.barrier()
    return nc, tap_d


def host_consts():
    cbm = np.zeros((128, NCB), np.float32)
    cbm[:, CB_ID:CB_ID + 128] = np.eye(128, dtype=np.float32)
    cbm[:, CB_ONES:CB_ONES + 128] = 1.0
    for hp in range(2):
        for n in range(8):
            o = CB_ESEL + (hp * 8 + n) * 128
            cbm[32 * hp + n, o:o + 128] = 1.0
    key = np.arange(128)[:, None]
    q = np.arange(128)[None, :]
    cbm[:, CB_TRI:CB_TRI + 128] = np.where(key <= q, 0.0, NEG).astype(np.float32)
    return cbm


def host_pvec(norm_mix_g, norm_ffn_g, pool_scale, conv_w, conv_b, norm_final_g):
    pvm = np.zeros((128, NPV), np.float32)

    def cols(v):
        v = np.asarray(v, np.float32)
        return np.ascontiguousarray(v.reshape(-1, 128).T)

    for l in range(L_DEPTH):
        b = l * PV_L
        pvm[:, b + PV_GMIX:b + PV_GMIX + 8] = cols(norm_mix_g[l])
        pvm[:, b + PV_GFFN:b + PV_GFFN + 8] = cols(norm_ffn_g[l])
        pvm[:, b + PV_PSC:b + PV_PSC + 4] = cols(pool_scale[l])
        pvm[:, b + PV_CW0:b + PV_CW0 + 44] = cols(conv_w[l][0])
        pvm[:, b + PV_CW1:b + PV_CW1 + 44] = cols(conv_w[l][1])
        pvm[:, b + PV_CW2:b + PV_CW2 + 44] = cols(conv_w[l][2])
        pvm[:, b + PV_CB:b + PV_CB + 44] = cols(conv_b[l])
    pvm[:, PV_GFIN:PV_GFIN + 8] = cols(norm_final_g)
    pvm[:, PV_EPS] = EPS
    for g, w in enumerate(WINS):
        t = np.arange(16)
        pvm[:, PV_INVC + g * 16:PV_INVC + (g + 1) * 16] = (1.0 / np.minimum(w, t + 1)).astype(np.float32)[None, :]
    return pvm


_NC_CACHE = {}


def make_in_maps(x, norm_mix_g, w_in, w_pool, pool_scale, w_branch_a, w_branch_b, w_out,
                 norm_ffn_g, w_up, conv_w, conv_b, w_down, norm_final_g):
    f = lambda a: np.ascontiguousarray(np.asarray(a, dtype=np.float32))
    x = f(x)
    shared = {
        "w_in": f(w_in), "w_pool": f(w_pool), "w_branch_a": f(w_branch_a), "w_branch_b": f(w_branch_b),
        "w_out": f(w_out), "w_up": f(w_up), "w_down": f(w_down),
        "pvec": host_pvec(f(norm_mix_g), f(norm_ffn_g), f(pool_scale), f(conv_w), f(conv_b), f(norm_final_g)),
        "cb": host_consts(),
    }
    in_maps = []
    for b in range(8):
        d = dict(shared)
        d["xT"] = np.ascontiguousarray(x[b].T)
        in_maps.append(d)
    return in_maps


def kernel(x, norm_mix_g, w_in, w_pool, pool_scale, w_branch_a, w_branch_b, w_out,
           norm_ffn_g, w_up, conv_w, conv_b, w_down, norm_final_g):
    in_maps = make_in_maps(x, norm_mix_g, w_in, w_pool, pool_scale, w_branch_a, w_branch_b, w_out,
                           norm_ffn_g, w_up, conv_w, conv_b, w_down, norm_final_g)
    if "nc" not in _NC_CACHE:
        _NC_CACHE["nc"] = build()[0]
    nc = _NC_CACHE["nc"]
    res = run_bass_kernel_spmd(nc, in_maps, core_ids=list(range(8)))
    out = np.stack([np.ascontiguousarray(np.asarray(r["yT"], dtype=np.float32).T) for r in res.results], axis=0)
    return out.astype(np.float32)
```

```python
from contextlib import ExitStack
import numpy as np
import concourse.bass as bass
import concourse.mybir as mybir
from concourse.bass_utils import run_bass_kernel_spmd

F32 = mybir.dt.float32
BF16 = mybir.dt.bfloat16
ALU = mybir.AluOpType
AF = mybir.ActivationFunctionType
AX = mybir.AxisListType

D = 1024
S_LEN = 2048
L_DEPTH = 2
NCH = 4
DFF = 2816
NFT = DFF // 128
WINS = (2, 4, 8, 16)
NEG = -30000.0
EPS = 1e-6

PV_GMIX = 0
PV_GFFN = 8
PV_PSC = 16
PV_CW0 = 20
PV_CW1 = PV_CW0 + 44
PV_CW2 = PV_CW1 + 44
PV_CB = PV_CW2 + 44
PV_L = PV_CB + 44
PV_GFIN = 2 * PV_L
PV_INVC = PV_GFIN + 8
PV_EPS = PV_INVC + 64
NPV = PV_EPS + 1
CB_ID = 0
CB_ONES = 128
CB_ESEL = 256
CB_TRI = CB_ESEL + 2048
NCB = CB_TRI + 128


class Res:
    __slots__ = ("name", "w", "rs", "excl")

    def __init__(self, name, excl=False):
        self.name = name
        self.w = None
        self.rs = {}
        self.excl = excl


class Ev:
    __slots__ = ("key", "val", "vc")

    def __init__(self, key, val, vc):
        self.key, self.val, self.vc = key, val, vc


class Slot:
    def __init__(self, key, sem):
        self.key, self.sem, self.count = key, sem, 0


class Sched:
    EPOCH = 8192

    def __init__(self, nc, stack):
        self.nc, self.stack = nc, stack
        self.E = {"pe": nc.tensor, "act": nc.scalar, "dve": nc.vector,
                  "pool": nc.gpsimd, "sp": nc.sync}
        self.pos = {e: 0 for e in self.E}
        self.known = {e: {} for e in self.E}
        self.sems = {}
        self.last = {}
        self.nslot = 0
        self.scoped = set()
        self.dead = False

    def _sem(self, key, epoch):
        k = (key, epoch)
        if k not in self.sems:
            self.sems[k] = self.stack.enter_context(self.nc.semaphore("s_%s_%d" % (key, epoch)))
        return self.sems[k]

    def slot(self, name, scoped=False):
        self.nslot += 1
        key = "d_%s" % name
        if scoped:
            self.scoped.add(key)
        return Slot(key, self._sem(key, 0))

    def _wait(self, eng, ev):
        kn = self.known[eng]
        if kn.get(ev.key, 0) >= ev.val:
            return
        if ev.key in self.E:
            ep, v = divmod(ev.val - 1, self.EPOCH)
            self.E[eng].wait_ge(self._sem(ev.key, ep), v + 1)
        else:
            self.E[eng].wait_ge(self._sem(ev.key, 0), ev.val)
        for k, v in ev.vc.items():
            if kn.get(k, 0) < v:
                kn[k] = v

    def _sync(self, eng, reads, writes, is_dma):
        for r in reads:
            if r.w is not None:
                ev = r.w
                if ev.key == eng and eng == "pe" and not is_dma:
                    continue
                self._wait(eng, ev)
            if r.excl:
                for ev in list(r.rs.values()):
                    if ev.key != eng:
                        self._wait(eng, ev)
        for r in writes:
            if r.w is not None and (is_dma or r.w.key != eng or eng != "pe"):
                self._wait(eng, r.w)
            for ev in r.rs.values():
                if is_dma or ev.key != eng or eng != "pe":
                    self._wait(eng, ev)

    def _commit(self, ev, reads, writes):
        self.last[ev.key] = ev
        for r in reads:
            o = r.rs.get(ev.key)
            if o is None or o.val < ev.val:
                r.rs[ev.key] = ev
        for r in writes:
            r.w = ev
            r.rs = {}

    def op(self, eng, fn, reads=(), writes=()):
        if self.dead:
            return None
        self._sync(eng, reads, writes, False)
        ins = fn(self.E[eng])
        self.pos[eng] += 1
        p = self.pos[eng]
        ep, _ = divmod(p - 1, self.EPOCH)
        ins.then_inc(self._sem(eng, ep), 1)
        vc = dict(self.known[eng])
        vc[eng] = p
        ev = Ev(eng, p, vc)
        self._commit(ev, reads, writes)
        return ev

    def dma(self, q, slot, out, in_, reads=(), writes=()):
        if self.dead:
            return None
        self._sync(q, reads, writes, True)
        slot.count += 16
        self.E[q].dma_start(out=out, in_=in_).then_inc(slot.sem, 16)
        vc = dict(self.known[q])
        vc[slot.key] = slot.count
        ev = Ev(slot.key, slot.count, vc)
        self._commit(ev, reads, writes)
        return ev

    def snapshot(self):
        return [ev for k, ev in self.last.items() if k in self.E or k in self.scoped]

    def wait_events(self, eng, evs):
        if self.dead:
            return
        for ev in evs:
            if ev.key != eng:
                self._wait(eng, ev)

    def barrier(self, engines=("pe", "act", "dve", "sp")):
        if self.dead:
            return
        evs = [ev for k, ev in self.last.items() if k in self.E or k in self.scoped]
        for e in engines:
            for ev in evs:
                if ev.key == e:
                    continue
                self._wait(e, ev)


class _Stop(Exception):
    pass


def build(n_layers=L_DEPTH, final_norm=True, taps=(), ph=9):
    nc = bass.Bass("TRN2", target_bir_lowering=False)
    L = L_DEPTH

    def din(name, shape):
        return nc.dram_tensor(name, shape, F32, kind="ExternalInput").ap()

    xT_d = din("xT", [D, S_LEN])
    w_in_d = din("w_in", [L, D, 4096])
    w_pool_d = din("w_pool", [L, 4, 128, 128])
    w_a_d = din("w_branch_a", [L, 512, D])
    w_b_d = din("w_branch_b", [L, 512, D])
    w_out_d = din("w_out", [L, D, D])
    w_up_d = din("w_up", [L, D, 2 * DFF])
    w_down_d = din("w_down", [L, DFF, D])
    pvec_d = din("pvec", [128, NPV])
    cb_d = din("cb", [128, NCB])
    yT_d = nc.dram_tensor("yT", [D, S_LEN], F32, kind="ExternalOutput").ap()
    tap_d = {}
    tap_ev = []

    with ExitStack() as es:
        S = Sched(nc, es)

        uid = [0]

        def sb(stack, name, shape, dt):
            uid[0] += 1
            return stack.enter_context(nc.sbuf_tensor("%s_u%d" % (name, uid[0]), shape, dt))

        ps = es.enter_context(nc.psum_tensor("ps", [128, 4096], F32))
        PB = [Res("pb%d" % i, True) for i in range(8)]

        def bank(b, lo=0, hi=512):
            return ps[:, b * 512 + lo:b * 512 + hi]

        xT = sb(es, "xTs", [128, 8, S_LEN], F32)
        hT = sb(es, "hTs", [128, 8, S_LEN], BF16)
        pv = sb(es, "pv", [128, NPV], F32)
        cb = sb(es, "cbs", [128, NCB], BF16)
        R_x = [Res("x%d" % c) for c in range(NCH)]
        R_h = [Res("h%d" % c) for c in range(NCH)]
        R_pv = Res("pv")
        R_cb = Res("cb")
        ident = cb[:, CB_ID:CB_ID + 128]
        ones = cb[:, CB_ONES:CB_ONES + 128]
        tri = cb[:, CB_TRI:CB_TRI + 128]

        def esel(hp, n):
            o = CB_ESEL + (hp * 8 + n) * 128
            return cb[:, o:o + 128]

        w_qkv = [sb(es, "wqkv%d" % i, [128, 3, 8, 128], BF16) for i in range(2)]
        R_wqkv = [Res("wqkv%d" % i) for i in range(2)]
        sl_wqkv = [S.slot("wqkv%d" % i) for i in range(2)]
        w_u = [sb(es, "wu%d" % i, [128, 8, 128], BF16) for i in range(2)]
        w_pl = [sb(es, "wpl%d" % i, [128, 128], BF16) for i in range(2)]
        R_wu = [Res("wu%d" % i) for i in range(2)]
        sl_wu = [S.slot("wu%d" % i) for i in range(2)]
        w_m = [sb(es, "wm%d" % i, [128, 24, 128], BF16) for i in range(2)]
        R_wm = [Res("wm%d" % i) for i in range(2)]
        sl_wm = [S.slot("wm%d" % i) for i in range(2)]
        R_wo = [Res("wo%d" % i) for i in range(2)]
        sl_wo = [S.slot("wo%d" % i, True) for i in range(2)]
        NUP = 2
        R_wup = [Res("wup%d" % i) for i in range(NUP)]
        sl_wup = [S.slot("wup%d" % i, True) for i in range(NUP)]
        FPASS = ((0, 8), (8, 15), (15, 22))
        R_wdn = [Res("wdn%d" % i) for i in range(2)]
        sl_wdn = [S.slot("wdn%d" % i, True) for i in range(2)]

        sl_in = S.slot("in")
        sl_cb = S.slot("cb")
        sl_x = [S.slot("x%d" % c, True) for c in range(NCH)]
        sl_out = [S.slot("out%d" % c, True) for c in range(2)]
        sl_tap = S.slot("tap", True)

        def tap(name, ap, reads):
            if name not in taps or S.dead:
                return
            t = nc.dram_tensor("tap_" + name, list(ap.shape), ap.dtype, kind="ExternalOutput").ap()
            tap_d[name] = t
            tap_ev.append(S.dma("sp", S.slot("tap_" + name, True), t, ap, reads=reads))

        S.dma("sp", sl_in, pv[:, :], pvec_d, writes=[R_pv])
        S.dma("pool", sl_cb, cb[:, :], cb_d, writes=[R_cb])
        xv = xT_d.rearrange("(c p) t -> p c t", p=128)
        for c in range(NCH):
            S.dma("sp", sl_x[c], xT[:, :, c * 512:(c + 1) * 512], xv[:, :, c * 512:(c + 1) * 512],
                  writes=[R_x[c]])

        ctr = {"wu": 0, "wqkv": 0, "wm": 0, "wo": 0, "wup": 0, "wdn": 0, "pb": 0}

        def wview(d, l):
            return d[l].rearrange("(kc p) n -> p kc n", p=128)

        def rmsnorm(gcol, out_fn, tag):
            with ExitStack() as ns:
                sq = [sb(ns, "sq%s%d" % (tag, i), [128, 8, 512], BF16) for i in range(2)]
                R_sq = [Res("sq%d" % i) for i in range(2)]
                rs = [sb(ns, "rs%s%d" % (tag, i), [128, 512], F32) for i in range(2)]
                R_rs = [Res("rs%d" % i) for i in range(2)]
                for c in range(NCH):
                    i = c % 2
                    cs = slice(c * 512, (c + 1) * 512)
                    S.op("act", lambda e: e.activation(sq[i][:, :, :], xT[:, :, cs], AF.Square),
                         reads=[R_x[c]], writes=[R_sq[i]])
                    b = i
                    S.op("pe", lambda e: [e.matmul(bank(b), ones, sq[i][:, kc, :], start=(kc == 0),
                                                   stop=(kc == 7)) for kc in range(8)][-1],
                         reads=[R_sq[i], R_cb], writes=[PB[b]])
                    S.op("act", lambda e: e.activation(rs[i][:, :], bank(b), AF.Sqrt,
                                                       bias=pv[:, PV_EPS:PV_EPS + 1], scale=1.0 / D),
                         reads=[PB[b], R_pv], writes=[R_rs[i]])
                    S.op("dve", lambda e: e.reciprocal(rs[i][:, :], rs[i][:, :]),
                         reads=[R_rs[i]], writes=[R_rs[i]])
                    dst, dres = out_fn(c)
                    for kc in range(8):
                        S.op("dve", lambda e: e.scalar_tensor_tensor(
                            dst[:, kc, :], xT[:, kc, cs], pv[:, gcol + kc:gcol + kc + 1], rs[i][:, :],
                            ALU.mult, ALU.mult),
                            reads=[R_x[c], R_rs[i], R_pv], writes=dres)
                S.barrier()

        def h_out(c):
            return hT[:, :, c * 512:(c + 1) * 512], [R_h[c]]

        try:
            for l in range(n_layers):
                pb = l * PV_L
                rmsnorm(pb + PV_GMIX, h_out, "a%d" % l)
                if l == 0:
                    tap("h0", hT[:, :, :], R_h)
                if ph == 1 and l == n_layers - 1: S.dead = True
                win = wview(w_in_d, l)

                with ExitStack() as ms:
                    pooledT = sb(ms, "pooledT", [128, 4, S_LEN], BF16)
                    attnT = sb(ms, "attnT", [128, 4, S_LEN], BF16)
                    R_pool = [Res("pool%d" % c) for c in range(NCH)]
                    R_attn = [Res("attn%d" % c) for c in range(NCH)]

                    with ExitStack() as pscope:
                        uT = sb(pscope, "uT", [128, S_LEN], F32)
                        uT2 = sb(pscope, "uT2", [128, S_LEN], F32)
                        tA = sb(pscope, "tA", [128, S_LEN], F32)
                        tB = sb(pscope, "tB", [128, S_LEN], F32)
                        mx = sb(pscope, "mx", [128, S_LEN], BF16)
                        t16 = sb(pscope, "t16", [128, 16], F32)
                        R_u, R_A, R_B, R_mx, R_t16 = Res("u"), Res("tA"), Res("tB"), Res("mx"), Res("t16")

                        def load_u(g):
                            i = ctr["wu"] % 2
                            ctr["wu"] += 1
                            S.dma("pool", sl_wu[i], w_u[i][:, :, :],
                                  win[:, :, 1536 + g * 128:1536 + (g + 1) * 128], writes=[R_wu[i]])
                            S.dma("pool", sl_wu[i], w_pl[i][:, :], w_pool_d[l, g], writes=[R_wu[i]])
                            return i

                        def u_mm(g, wi):
                            hb = 4 * (g % 2)
                            for c in range(NCH):
                                cs = slice(c * 512, (c + 1) * 512)
                                S.op("pe", lambda e: [e.matmul(bank(hb + c), w_u[wi][:, kc, :], hT[:, kc, cs],
                                                               start=(kc == 0), stop=(kc == 7))
                                                      for kc in range(8)][-1],
                                     reads=[R_wu[wi], R_h[c]], writes=[PB[hb + c]])

                        wis = {0: load_u(0)}
                        u_mm(0, wis[0])
                        uTs = [uT, uT2]
                        R_us = [R_u, Res("u2")]
                        for g in range(4):
                            wi = wis[g]
                            hb = 4 * (g % 2)
                            uTg, R_ug = uTs[g % 2], R_us[g % 2]
                            S.op("act", lambda e: e.activation(uTg[:, :], ps[:, hb * 512:(hb + 4) * 512], AF.Copy),
                                 reads=PB[hb:hb + 4], writes=[R_ug])
                            if g < 3:
                                wis[g + 1] = load_u(g + 1)
                                u_mm(g + 1, wis[g + 1])
                            src, rsrc = uTg, R_ug
                            tmp = [(tA, R_A), (tB, R_B)]
                            d = 1
                            k = 0
                            while d < WINS[g]:
                                dst, rdst = tmp[k % 2]
                                k += 1
                                S.op("dve", lambda e: e.tensor_tensor(dst[:, d:], src[:, d:], src[:, :S_LEN - d],
                                                                      ALU.add),
                                     reads=[rsrc], writes=[rdst])
                                S.op("dve", lambda e: e.tensor_copy(dst[:, :d], src[:, :d]),
                                     reads=[rsrc], writes=[rdst])
                                src, rsrc = dst, rdst
                                d *= 2
                            w = WINS[g]
                            S.op("dve", lambda e: e.scalar_tensor_tensor(mx[:, :], src[:, :], 1.0 / w, uTg[:, :],
                                                                          ALU.mult, ALU.subtract),
                                 reads=[rsrc, R_ug], writes=[R_mx])
                            S.op("dve", lambda e: e.tensor_tensor(t16[:, :], src[:, :16],
                                                                  pv[:, PV_INVC + g * 16:PV_INVC + (g + 1) * 16],
                                                                  ALU.mult),
                                 reads=[rsrc, R_pv], writes=[R_t16])
                            S.op("dve", lambda e: e.tensor_tensor(mx[:, :16], t16[:, :], uTg[:, :16], ALU.subtract),
                                 reads=[R_t16, R_ug], writes=[R_mx])
                            for c in range(NCH):
                                cs = slice(c * 512, (c + 1) * 512)
                                S.op("pe", lambda e: e.matmul(bank(hb + c), w_pl[wi][:, :], mx[:, cs],
                                                              start=True, stop=True),
                                     reads=[R_wu[wi], R_mx], writes=[PB[hb + c]])
                                S.op("act", lambda e: e.activation(
                                    pooledT[:, g, cs], bank(hb + c), AF.Copy,
                                    scale=pv[:, pb + PV_PSC + g:pb + PV_PSC + g + 1]),
                                    reads=[PB[hb + c], R_pv], writes=[R_pool[c]])
                        S.barrier()
                    if l == 0:
                        tap("pooled0", pooledT[:, :, :], R_pool)

                    if ph == 2 and l == n_layers - 1: S.dead = True
                    with ExitStack() as ascope:
                        NSET = 2
                        qz = [[sb(ascope, "qz%d_%d" % (h_, i), [128, S_LEN], BF16) for i in range(NSET)]
                              for h_ in range(2)]
                        kT = [sb(ascope, "kT%d" % i, [128, S_LEN], BF16) for i in range(NSET)]
                        Vt = [sb(ascope, "V%d" % i, [128, 16, 2, 65], BF16) for i in range(NSET)]
                        nmT = [sb(ascope, "nm%d" % i, [128, 1024], BF16) for i in range(NSET)]
                        ksum = [sb(ascope, "ks%d" % i, [128, 8], F32) for i in range(NSET)]
                        ksb = [sb(ascope, "ksb%d" % i, [128, 16], BF16) for i in range(NSET)]
                        gpad = sb(ascope, "gpad", [128, 2, 8], F32)
                        top8 = sb(ascope, "top8", [128, 2, 8], F32)
                        gw1 = sb(ascope, "gw1", [128, 2, 8], F32)
                        gw2 = sb(ascope, "gw2", [128, 2, 8], F32)
                        R_gw1, R_gw2 = Res("gw1"), Res("gw2")
                        btok = sb(ascope, "btok", [128, 8, 64], BF16)
                        Pt = [sb(ascope, "P%d" % i, [128, 512], BF16) for i in range(3)]
                        rec = sb(ascope, "rec", [128, 4], F32)
                        atok = [sb(ascope, "atok%d" % i, [128, 4, 128], BF16) for i in range(2)]
                        R_q = [[Res("q") for _ in range(NCH)] for _ in range(NSET)]
                        R_k = [[Res("k") for _ in range(NCH)] for _ in range(NSET)]
                        R_v = [[Res("v") for _ in range(NCH)] for _ in range(NSET)]
                        R_nm = [[Res("nm") for _ in range(2)] for _ in range(NSET)]
                        R_ks = [Res("ks") for _ in range(NSET)]
                        R_ksb = [Res("ksb") for _ in range(NSET)]
                        R_gp, R_t8, R_bt = Res("gp"), Res("t8"), Res("bt")
                        R_P = [Res("P") for _ in range(3)]
                        R_rec = Res("rec")
                        R_at = [Res("at") for _ in range(2)]
                        pctr = {"p": 0, "s": 0, "o": 0, "proj": 0}

                        for i in range(NSET):
                            S.op("dve", lambda e: e.memset(Vt[i][:, :, :, 64:65], 1.0), writes=R_v[i])
                        S.op("dve", lambda e: e.memset(btok[:, :, :], 0.0), writes=[R_bt])
                        for i in range(NSET):
                            S.op("dve", lambda e: e.memset(ksb[i][:, :], 0.0), writes=[R_ksb[i]])
                            S.op("dve", lambda e: e.memset(qz[0][i][64:128, :], 0.0), writes=R_q[i])
                            S.op("dve", lambda e: e.memset(qz[1][i][0:64, :], 0.0), writes=R_q[i])
                            S.op("dve", lambda e: e.memset(nmT[i][64:128, :], 0.0), writes=R_nm[i])

                        if ph == 211:
                            S.dead = True

                        def load_qkv(m):
                            i = ctr["wqkv"] % 2
                            ctr["wqkv"] += 1
                            for j in range(3):
                                S.dma("pool", sl_wqkv[i], w_qkv[i][:, j, :, :],
                                      win[:, :, j * 512 + m * 128:j * 512 + (m + 1) * 128], writes=[R_wqkv[i]])
                            return i

                        def project(m, wi, st):
                            for c in range(NCH):
                                cs = slice(c * 512, (c + 1) * 512)
                                for j, (dstT, rr) in enumerate(((None, R_q[st]), (kT[st], R_k[st]))):
                                    b = pctr["proj"] % 2
                                    pctr["proj"] += 1
                                    S.op("pe", lambda e: [e.matmul(bank(b), w_qkv[wi][:, j, kc, :], hT[:, kc, cs],
                                                                   start=(kc == 0), stop=(kc == 7))
                                                          for kc in range(8)][-1],
                                         reads=[R_wqkv[wi], R_h[c]], writes=[PB[b]])
                                    if j == 0:
                                        for h_ in range(2):
                                            S.op("act", lambda e: e.activation(
                                                qz[h_][st][h_ * 64:(h_ + 1) * 64, cs],
                                                ps[h_ * 64:(h_ + 1) * 64, b * 512:(b + 1) * 512], AF.Copy),
                                                reads=[PB[b]], writes=[rr[c]])
                                    else:
                                        S.op("act", lambda e: e.activation(dstT[:, cs], bank(b), AF.Copy),
                                             reads=[PB[b]], writes=[rr[c]])
                                    if j == 1 and ph != 212:
                                        S.op("dve", lambda e: e.tensor_reduce(
                                            ksum[st][:, 2 * c:2 * c + 2],
                                            bank(b).rearrange("p (n k) -> p n k", k=256), AX.X, ALU.add),
                                            reads=[PB[b]], writes=[R_ks[st]])
                            if ph == 212:
                                S.dead = True
                            for hp in range(2):
                                S.op("dve", lambda e: e.tensor_copy(ksb[st][hp * 64:(hp + 1) * 64, hp * 8:hp * 8 + 8],
                                                                    ksum[st][hp * 64:(hp + 1) * 64, :]),
                                     reads=[R_ks[st]], writes=[R_ksb[st]])
                            if ph == 213:
                                S.dead = True
                            for tg in range(4):
                                b = pctr["proj"] % 2
                                pctr["proj"] += 1

                                def vmm(e):
                                    ins = None
                                    for t4 in range(4):
                                        tt = tg * 4 + t4
                                        for kc in range(8):
                                            ins = e.matmul(bank(b, t4 * 128, (t4 + 1) * 128),
                                                           hT[:, kc, tt * 128:(tt + 1) * 128],
                                                           w_qkv[wi][:, 2, kc, :], start=(kc == 0), stop=(kc == 7))
                                    return ins
                                S.op("pe", vmm, reads=[R_wqkv[wi], R_h[tg]], writes=[PB[b]])
                                S.op("dve", lambda e: e.tensor_copy(
                                    Vt[st][:, tg * 4:(tg + 1) * 4, :, 0:64],
                                    bank(b).rearrange("p (t h d) -> p t h d", t=4, h=2)),
                                    reads=[PB[b]], writes=[R_v[st][tg]])

                        def masks_a(m, st):
                            GOFF = 1.0e4
                            S.op("dve", lambda e: e.memset(gpad[:, :, :], 0.0), writes=[R_gp])

                            def gmm(e):
                                ins = None
                                for j in range(8):
                                    qs = slice((8 + j) * 128, (9 + j) * 128)
                                    e.matmul(bank(6, j * 16, j * 16 + 16), qz[0][st][:, qs], ksb[st][:, :],
                                             start=True, stop=False, skip_group_check=True)
                                    ins = e.matmul(bank(6, j * 16, j * 16 + 16), qz[1][st][:, qs], ksb[st][:, :],
                                                   start=False, stop=True, skip_group_check=True)
                                return ins
                            S.op("pe", gmm, reads=[R_q[st][2], R_q[st][3], R_ksb[st]], writes=[PB[6]])
                            chains = []
                            for j in range(8):
                                chains.append(lambda j=j: mask_chain(st, j, GOFF))
                            return chains

                        def mask_chain(st, j, GOFF):
                            if True:
                                qt = 8 + j
                                own = qt // 2
                                S.op("dve", lambda e: e.tensor_scalar(
                                    gpad[:, :, 0:own],
                                    bank(6, j * 16, j * 16 + 16).rearrange("p (h n) -> p h n", h=2)[:, :, 0:own],
                                    GOFF, None, ALU.add),
                                    reads=[PB[6]], writes=[R_gp])
                                src, rsrc = gpad, R_gp
                                for it_ in range(2):
                                    S.op("dve", lambda e: e.tensor_reduce(top8[:, :, it_], src[:, :, :], AX.X, ALU.max),
                                         reads=[rsrc], writes=[R_t8])
                                    dst, rdst = (gw1, R_gw1) if it_ == 0 else (gw2, R_gw2)
                                    for hp in range(2):
                                        S.op("dve", lambda e: e.scalar_tensor_tensor(
                                            dst[:, hp, :], src[:, hp, :], top8[:, hp, it_:it_ + 1], src[:, hp, :],
                                            ALU.is_lt, ALU.mult),
                                            reads=[rsrc, R_t8], writes=[rdst])
                                    src, rsrc = dst, rdst
                                S.op("dve", lambda e: e.tensor_reduce(top8[:, :, 2], src[:, :, :], AX.X, ALU.max),
                                     reads=[rsrc], writes=[R_t8])
                                for hp in range(2):
                                    S.op("dve", lambda e: e.tensor_scalar(
                                        btok[:, j, hp * 32:hp * 32 + 8], gpad[:, hp, :], top8[:, hp, 2:3], NEG,
                                        ALU.is_lt, ALU.mult),
                                        reads=[R_gp, R_t8], writes=[R_bt])
                                S.op("dve", lambda e: e.memset(
                                    btok[:, j, :].rearrange("p (h c) -> p h c", h=2)[:, :, own:own + 1], 0.0),
                                    writes=[R_bt])

                        def masks_b(m, st):
                            for c2 in range(2):
                                trb = ps[:, 7 * 512:8 * 512].bitcast(BF16)

                                def tmm(e):
                                    ins = None
                                    for j in range(4):
                                        ins = e.transpose(trb[0:64, j * 128:(j + 1) * 128], btok[:, c2 * 4 + j, :], ident)
                                    return ins
                                S.op("pe", tmm, reads=[R_bt, R_cb], writes=[PB[7]])
                                S.op("dve", lambda e: e.tensor_copy(nmT[st][0:64, c2 * 512:(c2 + 1) * 512],
                                                                    trb[0:64, 0:512]),
                                     reads=[PB[7]], writes=[R_nm[st][c2]])

                        def attention(m, st, inter=()):
                            inter = list(inter)
                            tiles = [(c, hp, kt) for c in range(NCH) for hp in range(2) for kt in range(4 * c + 4)]
                            info = {}

                            def smm_op(i):
                                c, hp, kt = tiles[i]
                                hs = slice(hp * 64, (hp + 1) * 64)
                                r = kt - 4 * c
                                col0 = 128 * max(r, 0)
                                sbk = (2, 3, 0)[i % 3]
                                need_bias = (c >= 2) and (kt < 4 * c + 2)

                                def smm(e):
                                    last = not need_bias and r < 0
                                    ins = e.matmul(bank(sbk, col0, 512), kT[st][:, kt * 128:(kt + 1) * 128],
                                                   qz[hp][st][:, c * 512 + col0:(c + 1) * 512],
                                                   start=True, stop=last, skip_group_check=True)
                                    if need_bias:
                                        ins = e.matmul(bank(sbk, col0, 512), esel(hp, kt // 2),
                                                       nmT[st][:, (c - 2) * 512 + col0:(c - 1) * 512],
                                                       start=False, stop=(r < 0), skip_group_check=True)
                                    if r >= 0:
                                        ins = e.matmul(bank(sbk, col0, col0 + 128), ident, tri,
                                                       start=False, stop=True, skip_group_check=True)
                                    return ins
                                rd = [R_k[st][kt // 4], R_q[st][c], R_cb]
                                if need_bias:
                                    rd.append(R_nm[st][c - 2])
                                S.op("pe", smm, reads=rd, writes=[PB[sbk]])
                                info[i] = (sbk, col0, r)

                            pend = []

                            def flush_pending(force=False):
                                while pend and (force or pend[0][0] <= 0):
                                    pend.pop(0)[1]()
                                for p_ in pend:
                                    p_[0] -= 1

                            smm_op(0)
                            smm_op(1)
                            ob = None
                            for i, (c, hp, kt) in enumerate(tiles):
                                hs = slice(hp * 64, (hp + 1) * 64)
                                ab = c % 2
                                if kt == 0:
                                    ob = 4 + pctr["o"] % 2
                                    pctr["o"] += 1
                                if i + 2 < len(tiles):
                                    smm_op(i + 2)
                                sbk, col0, r = info[i]
                                pi = pctr["p"] % 3
                                pctr["p"] += 1
                                S.op("act", lambda e: e.activation(Pt[pi][:, col0:512], bank(sbk, col0, 512),
                                                                   AF.Exp, scale=0.125),
                                     reads=[PB[sbk]], writes=[R_P[pi]])

                                def pvmm(e):
                                    ins = None
                                    for j in range(max(r, 0), 4):
                                        ins = e.matmul(bank(ob, j * 128, j * 128 + 65),
                                                       Pt[pi][:, j * 128:(j + 1) * 128], Vt[st][:, kt, hp, :],
                                                       start=(kt == 0 and j == 0), stop=(kt == 4 * c + j),
                                                       skip_group_check=True)
                                    return ins
                                S.op("pe", pvmm, reads=[R_P[pi], R_v[st][kt // 4]], writes=[PB[ob]])
                                flush_pending()
                                if kt == 4 * c + 3:
                                    o3 = bank(ob).rearrange("p (j d) -> p j d", j=4)
                                    S.op("dve", lambda e: e.reciprocal(rec[:, :], o3[:, :, 64]),
                                         reads=[PB[ob]], writes=[R_rec])
                                    S.op("dve", lambda e: e.tensor_tensor(
                                        atok[ab][:, :, hs], o3[:, :, 0:64],
                                        rec[:, :].unsqueeze(2).to_broadcast([128, 4, 64]), ALU.mult),
                                        reads=[PB[ob], R_rec], writes=[R_at[ab]])
                                    if inter:
                                        inter.pop(0)()
                                    if hp == 1:
                                        def do_tr(c=c, ab=ab):
                                            trb = ps[:, 7 * 512:8 * 512].bitcast(BF16)

                                            def tmm2(e):
                                                ins = None
                                                for j in range(4):
                                                    ins = e.transpose(trb[:, j * 128:(j + 1) * 128],
                                                                      atok[ab][:, j, :], ident)
                                                return ins
                                            S.op("pe", tmm2, reads=[R_at[ab], R_cb], writes=[PB[7]])
                                            S.op("dve", lambda e: e.tensor_copy(
                                                attnT[:, m, c * 512:(c + 1) * 512], trb[:, 0:512]),
                                                reads=[PB[7]], writes=[R_attn[c]])
                                        pend.append([2, do_tr])
                            flush_pending(force=True)
                            while inter:
                                inter.pop(0)()

                        wnext = load_qkv(0)
                        project(0, wnext, 0)
                        for ch in masks_a(0, 0):
                            ch()
                        for m in range(4):
                            st = m % NSET
                            if m < 3:
                                wnext = load_qkv(m + 1)
                                project(m + 1, wnext, (m + 1) % NSET)
                            masks_b(m, st)
                            if l == 0 and m == 0:
                                tap("q0", qz[0][0][:, :], R_q[0])
                                tap("k0", kT[0][:, :], R_k[0])
                                tap("v0", Vt[0][:, :, :, :], R_v[0])
                                tap("nm0", nmT[0][0:64, :], R_nm[0])
                            inter = masks_a(m + 1, (m + 1) % NSET) if m < 3 else ()
                            attention(m, st, inter)
                        S.barrier()
                    if l == 0:
                        tap("attn0", attnT[:, :, :], R_attn)

                    if ph == 3 and l == n_layers - 1: S.dead = True
                    with ExitStack() as gscope:
                        snap_g = S.snapshot()
                        mergedT = sb(gscope, "mergedT", [128, 8, S_LEN], BF16)
                        w_o = [sb(gscope, "wo%d" % i, [128, 8, 128], BF16) for i in range(2)]
                        R_mg = [Res("mg%d" % c) for c in range(NCH)]
                        s0 = [sb(gscope, "s0_%d" % i, [128, 512], F32) for i in range(2)]
                        s1 = [sb(gscope, "s1_%d" % i, [128, 512], F32) for i in range(2)]
                        R_s0 = [Res("s0") for _ in range(2)]
                        R_s1 = [Res("s1") for _ in range(2)]
                        wa_v = wview(w_a_d, l)
                        wb_v = wview(w_b_d, l)

                        def load_m(m):
                            i = ctr["wm"] % 2
                            ctr["wm"] += 1
                            ms_ = slice(m * 128, (m + 1) * 128)
                            S.dma("pool", sl_wm[i], w_m[i][:, 0:4, :], wa_v[:, :, ms_], writes=[R_wm[i]])
                            S.dma("pool", sl_wm[i], w_m[i][:, 4:8, :], wb_v[:, :, ms_], writes=[R_wm[i]])
                            S.dma("pool", sl_wm[i], w_m[i][:, 8:16, :], win[:, :, 2048 + m * 128:2048 + (m + 1) * 128],
                                  writes=[R_wm[i]])
                            S.dma("pool", sl_wm[i], w_m[i][:, 16:24, :], win[:, :, 3072 + m * 128:3072 + (m + 1) * 128],
                                  writes=[R_wm[i]])
                            return i

                        nxt = load_m(0)
                        it = 0
                        for m in range(8):
                            wi = nxt
                            if m < 7:
                                nxt = load_m(m + 1)
                            for c in range(NCH):
                                cs = slice(c * 512, (c + 1) * 512)
                                b0 = 4 * (it % 2)
                                ti = it % 2
                                it += 1
                                S.op("pe", lambda e: [e.matmul(bank(b0), w_m[wi][:, kc, :], attnT[:, kc, cs],
                                                               start=(kc == 0), stop=(kc == 3)) for kc in range(4)][-1],
                                     reads=[R_wm[wi], R_attn[c]], writes=[PB[b0]])
                                S.op("pe", lambda e: [e.matmul(bank(b0 + 1), w_m[wi][:, 4 + kc, :], pooledT[:, kc, cs],
                                                               start=(kc == 0), stop=(kc == 3)) for kc in range(4)][-1],
                                     reads=[R_wm[wi], R_pool[c]], writes=[PB[b0 + 1]])
                                S.op("pe", lambda e: [e.matmul(bank(b0 + 2), w_m[wi][:, 8 + kc, :], hT[:, kc, cs],
                                                               start=(kc == 0), stop=(kc == 7)) for kc in range(8)][-1],
                                     reads=[R_wm[wi], R_h[c]], writes=[PB[b0 + 2]])
                                S.op("pe", lambda e: [e.matmul(bank(b0 + 3), w_m[wi][:, 16 + kc, :], hT[:, kc, cs],
                                                               start=(kc == 0), stop=(kc == 7)) for kc in range(8)][-1],
                                     reads=[R_wm[wi], R_h[c]], writes=[PB[b0 + 3]])
                                S.op("act", lambda e: e.activation(s0[ti][:, :], bank(b0 + 2), AF.Sigmoid),
                                     reads=[PB[b0 + 2]], writes=[R_s0[ti]])
                                S.op("act", lambda e: e.activation(s1[ti][:, :], bank(b0 + 3), AF.Sigmoid),
                                     reads=[PB[b0 + 3]], writes=[R_s1[ti]])
                                S.op("dve", lambda e: e.tensor_tensor(s0[ti][:, :], s0[ti][:, :], bank(b0), ALU.mult),
                                     reads=[R_s0[ti], PB[b0]], writes=[R_s0[ti]])
                                S.op("dve", lambda e: e.tensor_tensor(s1[ti][:, :], s1[ti][:, :], bank(b0 + 1), ALU.mult),
                                     reads=[R_s1[ti], PB[b0 + 1]], writes=[R_s1[ti]])
                                S.op("dve", lambda e: e.tensor_tensor(mergedT[:, m, cs], s0[ti][:, :], s1[ti][:, :],
                                                                      ALU.add),
                                     reads=[R_s0[ti], R_s1[ti]], writes=[R_mg[c]])
                        if l == 0:
                            tap("merged0", mergedT[:, :, :], R_mg)
                        wo_v = wview(w_out_d, l)

                        def load_o(m):
                            i = ctr["wo"] % 2
                            ctr["wo"] += 1
                            S.dma("pool", sl_wo[i], w_o[i][:, :, :], wo_v[:, :, m * 128:(m + 1) * 128], writes=[R_wo[i]])
                            return i

                        S.wait_events("pool", snap_g)
                        nxt = load_o(0)
                        for m in range(8):
                            wi = nxt
                            if m < 7:
                                nxt = load_o(m + 1)
                            for c in range(NCH):
                                cs = slice(c * 512, (c + 1) * 512)
                                b = ctr["pb"] % 8
                                ctr["pb"] += 1
                                S.op("pe", lambda e: [e.matmul(bank(b), w_o[wi][:, kc, :], mergedT[:, kc, cs],
                                                               start=(kc == 0), stop=(kc == 7)) for kc in range(8)][-1],
                                     reads=[R_wo[wi], R_mg[c]], writes=[PB[b]])
                                S.op("dve", lambda e: e.tensor_tensor(xT[:, m, cs], xT[:, m, cs], bank(b), ALU.add),
                                     reads=[R_x[c], PB[b]], writes=[R_x[c]])
                        S.barrier()
                if l == 0:
                    tap("x0m", xT[:, :, :], R_x)

                if ph == 4 and l == n_layers - 1: S.dead = True
                snap_f = S.snapshot()
                wup_v = wview(w_up_d, l)
                wdn_v = wview(w_down_d, l)
                with ExitStack() as fs:
                    hid = sb(fs, "hid", [128, 8, S_LEN], BF16)
                    w_up = [sb(fs, "wup%d" % i, [128, 2, 8, 128], BF16) for i in range(NUP)]
                    w_dn = [sb(fs, "wdn%d" % i, [128, 8, 128], BF16) for i in range(2)]
                    R_hid = [Res("hid%d" % i) for i in range(8)]
                    S.wait_events("pool", snap_f)
                    pre_up = [None]
                    deferred_norm = [True]
                    tg2 = None
                    R_tg2 = [Res("tg%d" % i) for i in range(2)]
                    R_tv2 = [Res("tv%d" % i) for i in range(2)]

                    def load_up(i_):
                        i = ctr["wup"] % NUP
                        ctr["wup"] += 1
                        S.dma("pool", sl_wup[i], w_up[i][:, 0, :, :], wup_v[:, :, i_ * 128:(i_ + 1) * 128],
                              writes=[R_wup[i]])
                        S.dma("pool", sl_wup[i], w_up[i][:, 1, :, :],
                              wup_v[:, :, DFF + i_ * 128:DFF + (i_ + 1) * 128], writes=[R_wup[i]])
                        return i

                    def load_dn(m, lo, hi):
                        i = ctr["wdn"] % 2
                        ctr["wdn"] += 1
                        S.dma("pool", sl_wdn[i], w_dn[i][:, 0:hi - lo, :], wdn_v[:, lo:hi, m * 128:(m + 1) * 128],
                              writes=[R_wdn[i]])
                        return i

                    pre_up[0] = load_up(0)
                    rmsnorm(pb + PV_GFFN, h_out, "f%d" % l)
                    tg2 = [sb(fs, "tg%d" % i, [128, S_LEN], F32) for i in range(2)]
                    tv2 = [sb(fs, "tv%d" % i, [128, S_LEN], F32) for i in range(2)]
                    for (lo, hi) in FPASS:
                        nxt = pre_up[0] if lo == 0 else load_up(lo)
                        for i_ in range(lo, hi):
                            wi = nxt
                            if i_ + 1 < hi:
                                nxt = load_up(i_ + 1)
                            il = i_ - lo
                            tg, tv, R_tg, R_tv = tg2[i_ % 2], tv2[i_ % 2], R_tg2[i_ % 2], R_tv2[i_ % 2]
                            for half, (tt, rt) in enumerate(((tg, R_tg), (tv, R_tv))):
                                hb = 4 * half
                                col = i_ + 22 * half
                                for c in range(NCH):
                                    cs = slice(c * 512, (c + 1) * 512)
                                    S.op("pe", lambda e: [e.matmul(bank(hb + c), w_up[wi][:, half, kc, :], hT[:, kc, cs],
                                                                   start=(kc == 0), stop=(kc == 7))
                                                          for kc in range(8)][-1],
                                         reads=[R_wup[wi], R_h[c]], writes=[PB[hb + c]])
                                pfull = ps[:, hb * 512:(hb + 4) * 512]
                                c0 = pv[:, pb + PV_CW0 + col:pb + PV_CW0 + col + 1]
                                c1 = pv[:, pb + PV_CW1 + col:pb + PV_CW1 + col + 1]
                                c2 = pv[:, pb + PV_CW2 + col:pb + PV_CW2 + col + 1]
                                cbias = pv[:, pb + PV_CB + col:pb + PV_CB + col + 1]
                                S.op("act", lambda e: e.activation(tt[:, :], pfull, AF.Identity, bias=cbias, scale=c2),
                                     reads=PB[hb:hb + 4] + [R_pv], writes=[rt])
                                S.op("dve", lambda e: e.scalar_tensor_tensor(
                                    tt[:, 1:], pfull[:, 0:S_LEN - 1], c1, tt[:, 1:], ALU.mult, ALU.add),
                                    reads=PB[hb:hb + 4] + [rt, R_pv], writes=[rt])
                                S.op("dve", lambda e: e.scalar_tensor_tensor(
                                    tt[:, 2:], pfull[:, 0:S_LEN - 2], c0, tt[:, 2:], ALU.mult, ALU.add),
                                    reads=PB[hb:hb + 4] + [rt, R_pv], writes=[rt])
                                if half == 0:
                                    S.op("act", lambda e: e.activation(tt[:, :], tt[:, :], AF.Silu),
                                         reads=[rt], writes=[rt])
                            S.op("dve", lambda e: e.tensor_tensor(hid[:, il, :], tg[:, :], tv[:, :], ALU.mult),
                                 reads=[R_tg, R_tv], writes=[R_hid[il]])
                        if l == 0 and lo == 0:
                            tap("hid0", hid[:, :, :], R_hid)
                        nk = hi - lo
                        nxt = load_dn(0, lo, hi)
                        for m in range(8):
                            wi = nxt
                            if m < 7:
                                nxt = load_dn(m + 1, lo, hi)
                            for c in range(NCH):
                                cs = slice(c * 512, (c + 1) * 512)
                                b = ctr["pb"] % 8
                                ctr["pb"] += 1
                                S.op("pe", lambda e: [e.matmul(bank(b), w_dn[wi][:, kc, :], hid[:, kc, cs],
                                                               start=(kc == 0), stop=(kc == nk - 1))
                                                      for kc in range(nk)][-1],
                                     reads=[R_wdn[wi]] + R_hid[0:nk], writes=[PB[b]])
                                S.op("dve", lambda e: e.tensor_tensor(xT[:, m, cs], xT[:, m, cs], bank(b), ALU.add),
                                     reads=[R_x[c], PB[b]], writes=[R_x[c]])
                    S.barrier()
                if l == 0:
                    tap("x0f", xT[:, :, :], R_x)

        except _Stop:
            pass
        S.dead = False
        S.barrier()

        with ExitStack() as os_:
            stg = [sb(os_, "stg%d" % i, [128, 8, 512], F32) for i in range(2)]
            R_stg = [Res("stg%d" % i) for i in range(2)]
            yv = yT_d.rearrange("(c p) t -> p c t", p=128)
            outs = []

            def y_out(c):
                return stg[c % 2][:, :, :], [R_stg[c % 2]]

            if final_norm:
                with ExitStack() as ns:
                    sq = [sb(ns, "sqz%d" % i, [128, 8, 512], BF16) for i in range(2)]
                    R_sq = [Res("sq%d" % i) for i in range(2)]
                    rs = [sb(ns, "rsz%d" % i, [128, 512], F32) for i in range(2)]
                    R_rs = [Res("rs%d" % i) for i in range(2)]
                    for c in range(NCH):
                        i = c % 2
                        cs = slice(c * 512, (c + 1) * 512)
                        S.op("act", lambda e: e.activation(sq[i][:, :, :], xT[:, :, cs], AF.Square),
                             reads=[R_x[c]], writes=[R_sq[i]])
                        b = i
                        S.op("pe", lambda e: [e.matmul(bank(b), ones, sq[i][:, kc, :], start=(kc == 0),
                                                       stop=(kc == 7)) for kc in range(8)][-1],
                             reads=[R_sq[i], R_cb], writes=[PB[b]])
                        S.op("act", lambda e: e.activation(rs[i][:, :], bank(b), AF.Sqrt,
                                                           bias=pv[:, PV_EPS:PV_EPS + 1], scale=1.0 / D),
                             reads=[PB[b], R_pv], writes=[R_rs[i]])
                        S.op("dve", lambda e: e.reciprocal(rs[i][:, :], rs[i][:, :]),
                             reads=[R_rs[i]], writes=[R_rs[i]])
                        for kc in range(8):
                            S.op("dve", lambda e: e.scalar_tensor_tensor(
                                stg[i][:, kc, :], xT[:, kc, cs], pv[:, PV_GFIN + kc:PV_GFIN + kc + 1], rs[i][:, :],
                                ALU.mult, ALU.mult),
                                reads=[R_x[c], R_rs[i], R_pv], writes=[R_stg[i]])
                        outs.append(S.dma("sp", sl_out[i], yv[:, :, cs], stg[i][:, :, :], reads=[R_stg[i]]))
            else:
                for c in range(NCH):
                    cs = slice(c * 512, (c + 1) * 512)
                    outs.append(S.dma("sp", sl_out[c % 2], yv[:, :, cs], xT[:, :, cs], reads=[R_x[c]]))
            lastout = {}
            for ev in outs:
                lastout[ev.key] = ev
            for ev in lastout.values():
                S._wait("sp", ev)
            for ev in tap_ev:
                S._wait("sp", ev)
            S.barrier()
    return nc, tap_d


def host_consts():
    cbm = np.zeros((128, NCB), np.float32)
    cbm[:, CB_ID:CB_ID + 128] = np.eye(128, dtype=np.float32)
    cbm[:, CB_ONES:CB_ONES + 128] = 1.0
    for hp in range(2):
        for n in range(8):
            o = CB_ESEL + (hp * 8 + n) * 128
            cbm[32 * hp + n, o:o + 128] = 1.0
    key = np.arange(128)[:, None]
    q = np.arange(128)[None, :]
    cbm[:, CB_TRI:CB_TRI + 128] = np.where(key <= q, 0.0, NEG).astype(np.float32)
    return cbm


def host_pvec(norm_mix_g, norm_ffn_g, pool_scale, conv_w, conv_b, norm_final_g):
    pvm = np.zeros((128, NPV), np.float32)

    def cols(v):
        v = np.asarray(v, np.float32)
        return np.ascontiguousarray(v.reshape(-1, 128).T)

    for l in range(L_DEPTH):
        b = l * PV_L
        pvm[:, b + PV_GMIX:b + PV_GMIX + 8] = cols(norm_mix_g[l])
        pvm[:, b + PV_GFFN:b + PV_GFFN + 8] = cols(norm_ffn_g[l])
        pvm[:, b + PV_PSC:b + PV_PSC + 4] = cols(pool_scale[l])
        pvm[:, b + PV_CW0:b + PV_CW0 + 44] = cols(conv_w[l][0])
        pvm[:, b + PV_CW1:b + PV_CW1 + 44] = cols(conv_w[l][1])
        pvm[:, b + PV_CW2:b + PV_CW2 + 44] = cols(conv_w[l][2])
        pvm[:, b + PV_CB:b + PV_CB + 44] = cols(conv_b[l])
    pvm[:, PV_GFIN:PV_GFIN + 8] = cols(norm_final_g)
    pvm[:, PV_EPS] = EPS
    for g, w in enumerate(WINS):
        t = np.arange(16)
        pvm[:, PV_INVC + g * 16:PV_INVC + (g + 1) * 16] = (1.0 / np.minimum(w, t + 1)).astype(np.float32)[None, :]
    return pvm


_NC_CACHE = {}


def make_in_maps(x, norm_mix_g, w_in, w_pool, pool_scale, w_branch_a, w_branch_b, w_out,
                 norm_ffn_g, w_up, conv_w, conv_b, w_down, norm_final_g):
    f = lambda a: np.ascontiguousarray(np.asarray(a, dtype=np.float32))
    x = f(x)
    shared = {
        "w_in": f(w_in), "w_pool": f(w_pool), "w_branch_a": f(w_branch_a), "w_branch_b": f(w_branch_b),
        "w_out": f(w_out), "w_up": f(w_up), "w_down": f(w_down),
        "pvec": host_pvec(f(norm_mix_g), f(norm_ffn_g), f(pool_scale), f(conv_w), f(conv_b), f(norm_final_g)),
        "cb": host_consts(),
    }
    in_maps = []
    for b in range(8):
        d = dict(shared)
        d["xT"] = np.ascontiguousarray(x[b].T)
        in_maps.append(d)
    return in_maps


def kernel(x, norm_mix_g, w_in, w_pool, pool_scale, w_branch_a, w_branch_b, w_out,
           norm_ffn_g, w_up, conv_w, conv_b, w_down, norm_final_g):
    in_maps = make_in_maps(x, norm_mix_g, w_in, w_pool, pool_scale, w_branch_a, w_branch_b, w_out,
                           norm_ffn_g, w_up, conv_w, conv_b, w_down, norm_final_g)
    if "nc" not in _NC_CACHE:
        _NC_CACHE["nc"] = build()[0]
    nc = _NC_CACHE["nc"]
    res = run_bass_kernel_spmd(nc, in_maps, core_ids=list(range(8)))
    out = np.stack([np.ascontiguousarray(np.asarray(r["yT"], dtype=np.float32).T) for r in res.results], axis=0)
    return out.astype(np.float32)
```

```python
from contextlib import ExitStack
import numpy as np
import concourse.bass as bass
import concourse.mybir as mybir
from concourse.bass_utils import run_bass_kernel_spmd

F32 = mybir.dt.float32
BF16 = mybir.dt.bfloat16
ALU = mybir.AluOpType
AF = mybir.ActivationFunctionType
AX = mybir.AxisListType

D = 1024
S_LEN = 2048
L_DEPTH = 2
NCH = 4
DFF = 2816
NFT = DFF // 128
WINS = (2, 4, 8, 16)
NEG = -30000.0
EPS = 1e-6

PV_GMIX = 0
PV_GFFN = 8
PV_PSC = 16
PV_CW0 = 20
PV_CW1 = PV_CW0 + 44
PV_CW2 = PV_CW1 + 44
PV_CB = PV_CW2 + 44
PV_L = PV_CB + 44
PV_GFIN = 2 * PV_L
PV_INVC = PV_GFIN + 8
PV_EPS = PV_INVC + 64
NPV = PV_EPS + 1
CB_ID = 0
CB_ONES = 128
CB_ESEL = 256
CB_TRI = CB_ESEL + 2048
NCB = CB_TRI + 128


class Res:
    __slots__ = ("name", "w", "rs", "excl")

    def __init__(self, name, excl=False):
        self.name = name
        self.w = None
        self.rs = {}
        self.excl = excl


class Ev:
    __slots__ = ("key", "val", "vc")

    def __init__(self, key, val, vc):
        self.key, self.val, self.vc = key, val, vc


class Slot:
    def __init__(self, key, sem):
        self.key, self.sem, self.count = key, sem, 0


class Sched:
    EPOCH = 8192

    def __init__(self, nc, stack):
        self.nc, self.stack = nc, stack
        self.E = {"pe": nc.tensor, "act": nc.scalar, "dve": nc.vector,
                  "pool": nc.gpsimd, "sp": nc.sync}
        self.pos = {e: 0 for e in self.E}
        self.known = {e: {} for e in self.E}
        self.sems = {}
        self.last = {}
        self.nslot = 0
        self.scoped = set()
        self.dead = False

    def _sem(self, key, epoch):
        k = (key, epoch)
        if k not in self.sems:
            self.sems[k] = self.stack.enter_context(self.nc.semaphore("s_%s_%d" % (key, epoch)))
        return self.sems[k]

    def slot(self, name, scoped=False):
        self.nslot += 1
        key = "d_%s" % name
        if scoped:
            self.scoped.add(key)
        return Slot(key, self._sem(key, 0))

    def _wait(self, eng, ev):
        kn = self.known[eng]
        if kn.get(ev.key, 0) >= ev.val:
            return
        if ev.key in self.E:
            ep, v = divmod(ev.val - 1, self.EPOCH)
            self.E[eng].wait_ge(self._sem(ev.key, ep), v + 1)
        else:
            self.E[eng].wait_ge(self._sem(ev.key, 0), ev.val)
        for k, v in ev.vc.items():
            if kn.get(k, 0) < v:
                kn[k] = v

    def _sync(self, eng, reads, writes, is_dma):
        for r in reads:
            if r.w is not None:
                ev = r.w
                if ev.key == eng and eng == "pe" and not is_dma:
                    continue
                self._wait(eng, ev)
            if r.excl:
                for ev in list(r.rs.values()):
                    if ev.key != eng:
                        self._wait(eng, ev)
        for r in writes:
            if r.w is not None and (is_dma or r.w.key != eng or eng != "pe"):
                self._wait(eng, r.w)
            for ev in r.rs.values():
                if is_dma or ev.key != eng or eng != "pe":
                    self._wait(eng, ev)

    def _commit(self, ev, reads, writes):
        self.last[ev.key] = ev
        for r in reads:
            o = r.rs.get(ev.key)
            if o is None or o.val < ev.val:
                r.rs[ev.key] = ev
        for r in writes:
            r.w = ev
            r.rs = {}

    def op(self, eng, fn, reads=(), writes=()):
        if self.dead:
            return None
        self._sync(eng, reads, writes, False)
        ins = fn(self.E[eng])
        self.pos[eng] += 1
        p = self.pos[eng]
        ep, _ = divmod(p - 1, self.EPOCH)
        ins.then_inc(self._sem(eng, ep), 1)
        vc = dict(self.known[eng])
        vc[eng] = p
        ev = Ev(eng, p, vc)
        self._commit(ev, reads, writes)
        return ev

    def dma(self, q, slot, out, in_, reads=(), writes=()):
        if self.dead:
            return None
        self._sync(q, reads, writes, True)
        slot.count += 16
        self.E[q].dma_start(out=out, in_=in_).then_inc(slot.sem, 16)
        vc = dict(self.known[q])
        vc[slot.key] = slot.count
        ev = Ev(slot.key, slot.count, vc)
        self._commit(ev, reads, writes)
        return ev

    def snapshot(self):
        return [ev for k, ev in self.last.items() if k in self.E or k in self.scoped]

    def wait_events(self, eng, evs):
        if self.dead:
            return
        for ev in evs:
            if ev.key != eng:
                self._wait(eng, ev)

    def barrier(self, engines=("act", "dve", "sp")):
        if self.dead:
            return
        evs = [ev for k, ev in self.last.items() if k in self.E or k in self.scoped]
        for e in engines:
            for ev in evs:
                if ev.key == e:
                    continue
                self._wait(e, ev)


class _Stop(Exception):
    pass


def build(n_layers=L_DEPTH, final_norm=True, taps=(), ph=9):
    nc = bass.Bass("TRN2", target_bir_lowering=False)
    L = L_DEPTH

    def din(name, shape):
        return nc.dram_tensor(name, shape, F32, kind="ExternalInput").ap()

    xT_d = din("xT", [D, S_LEN])
    w_in_d = din("w_in", [L, D, 4096])
    w_pool_d = din("w_pool", [L, 4, 128, 128])
    w_a_d = din("w_branch_a", [L, 512, D])
    w_b_d = din("w_branch_b", [L, 512, D])
    w_out_d = din("w_out", [L, D, D])
    w_up_d = din("w_up", [L, D, 2 * DFF])
    w_down_d = din("w_down", [L, DFF, D])
    pvec_d = din("pvec", [128, NPV])
    cb_d = din("cb", [128, NCB])
    yT_d = nc.dram_tensor("yT", [D, S_LEN], F32, kind="ExternalOutput").ap()
    tap_d = {}
    tap_ev = []

    with ExitStack() as es:
        S = Sched(nc, es)

        uid = [0]

        def sb(stack, name, shape, dt):
            uid[0] += 1
            return stack.enter_context(nc.sbuf_tensor("%s_u%d" % (name, uid[0]), shape, dt))

        ps = es.enter_context(nc.psum_tensor("ps", [128, 4096], F32))
        PB = [Res("pb%d" % i, True) for i in range(8)]

        def bank(b, lo=0, hi=512):
            return ps[:, b * 512 + lo:b * 512 + hi]

        xT = sb(es, "xTs", [128, 8, S_LEN], F32)
        hT = sb(es, "hTs", [128, 8, S_LEN], BF16)
        pv = sb(es, "pv", [128, NPV], F32)
        cb = sb(es, "cbs", [128, NCB], BF16)
        R_x = [Res("x%d" % c) for c in range(NCH)]
        R_h = [Res("h%d" % c) for c in range(NCH)]
        R_pv = Res("pv")
        R_cb = Res("cb")
        ident = cb[:, CB_ID:CB_ID + 128]
        ones = cb[:, CB_ONES:CB_ONES + 128]
        tri = cb[:, CB_TRI:CB_TRI + 128]

        def esel(hp, n):
            o = CB_ESEL + (hp * 8 + n) * 128
            return cb[:, o:o + 128]

        w_qkv = [sb(es, "wqkv%d" % i, [128, 3, 8, 128], BF16) for i in range(2)]
        R_wqkv = [Res("wqkv%d" % i) for i in range(2)]
        sl_wqkv = [S.slot("wqkv%d" % i) for i in range(2)]
        w_u = [sb(es, "wu%d" % i, [128, 8, 128], BF16) for i in range(2)]
        w_pl = [sb(es, "wpl%d" % i, [128, 128], BF16) for i in range(2)]
        R_wu = [Res("wu%d" % i) for i in range(2)]
        sl_wu = [S.slot("wu%d" % i) for i in range(2)]
        w_m = [sb(es, "wm%d" % i, [128, 24, 128], BF16) for i in range(2)]
        R_wm = [Res("wm%d" % i) for i in range(2)]
        sl_wm = [S.slot("wm%d" % i) for i in range(2)]
        R_wo = [Res("wo%d" % i) for i in range(2)]
        sl_wo = [S.slot("wo%d" % i, True) for i in range(2)]
        NUP = 2
        R_wup = [Res("wup%d" % i) for i in range(NUP)]
        sl_wup = [S.slot("wup%d" % i, True) for i in range(NUP)]
        FPASS = ((0, 8), (8, 15), (15, 22))
        R_wdn = [Res("wdn%d" % i) for i in range(2)]
        sl_wdn = [S.slot("wdn%d" % i, True) for i in range(2)]

        sl_in = S.slot("in")
        sl_cb = S.slot("cb")
        sl_x = [S.slot("x%d" % c, True) for c in range(NCH)]
        sl_out = [S.slot("out%d" % c, True) for c in range(2)]
        sl_tap = S.slot("tap", True)

        def tap(name, ap, reads):
            if name not in taps or S.dead:
                return
            t = nc.dram_tensor("tap_" + name, list(ap.shape), ap.dtype, kind="ExternalOutput").ap()
            tap_d[name] = t
            tap_ev.append(S.dma("sp", S.slot("tap_" + name, True), t, ap, reads=reads))

        S.dma("sp", sl_in, pv[:, :], pvec_d, writes=[R_pv])
        S.dma("pool", sl_cb, cb[:, :], cb_d, writes=[R_cb])
        xv = xT_d.rearrange("(c p) t -> p c t", p=128)
        for c in range(NCH):
            S.dma("sp", sl_x[c], xT[:, :, c * 512:(c + 1) * 512], xv[:, :, c * 512:(c + 1) * 512],
                  writes=[R_x[c]])

        ctr = {"wu": 0, "wqkv": 0, "wm": 0, "wo": 0, "wup": 0, "wdn": 0, "pb": 0}

        def wview(d, l):
            return d[l].rearrange("(kc p) n -> p kc n", p=128)

        def rmsnorm(gcol, out_fn, tag):
            with ExitStack() as ns:
                sq = [sb(ns, "sq%s%d" % (tag, i), [128, 8, 512], BF16) for i in range(2)]
                R_sq = [Res("sq%d" % i) for i in range(2)]
                rs = [sb(ns, "rs%s%d" % (tag, i), [128, 512], F32) for i in range(2)]
                R_rs = [Res("rs%d" % i) for i in range(2)]
                for c in range(NCH):
                    i = c % 2
                    cs = slice(c * 512, (c + 1) * 512)
                    S.op("act", lambda e: e.activation(sq[i][:, :, :], xT[:, :, cs], AF.Square),
                         reads=[R_x[c]], writes=[R_sq[i]])
                    b = i
                    S.op("pe", lambda e: [e.matmul(bank(b), ones, sq[i][:, kc, :], start=(kc == 0),
                                                   stop=(kc == 7)) for kc in range(8)][-1],
                         reads=[R_sq[i], R_cb], writes=[PB[b]])
                    S.op("act", lambda e: e.activation(rs[i][:, :], bank(b), AF.Sqrt,
                                                       bias=pv[:, PV_EPS:PV_EPS + 1], scale=1.0 / D),
                         reads=[PB[b], R_pv], writes=[R_rs[i]])
                    S.op("dve", lambda e: e.reciprocal(rs[i][:, :], rs[i][:, :]),
                         reads=[R_rs[i]], writes=[R_rs[i]])
                    dst, dres = out_fn(c)
                    for kc in range(8):
                        S.op("dve", lambda e: e.scalar_tensor_tensor(
                            dst[:, kc, :], xT[:, kc, cs], pv[:, gcol + kc:gcol + kc + 1], rs[i][:, :],
                            ALU.mult, ALU.mult),
                            reads=[R_x[c], R_rs[i], R_pv], writes=dres)
                S.barrier()

        def h_out(c):
            return hT[:, :, c * 512:(c + 1) * 512], [R_h[c]]

        try:
            for l in range(n_layers):
                pb = l * PV_L
                rmsnorm(pb + PV_GMIX, h_out, "a%d" % l)
                if l == 0:
                    tap("h0", hT[:, :, :], R_h)
                if ph == 1 and l == n_layers - 1: S.dead = True
                win = wview(w_in_d, l)

                with ExitStack() as ms:
                    pooledT = sb(ms, "pooledT", [128, 4, S_LEN], BF16)
                    attnT = sb(ms, "attnT", [128, 4, S_LEN], BF16)
                    R_pool = [Res("pool%d" % c) for c in range(NCH)]
                    R_attn = [Res("attn%d" % c) for c in range(NCH)]

                    with ExitStack() as pscope:
                        uT = sb(pscope, "uT", [128, S_LEN], F32)
                        uT2 = sb(pscope, "uT2", [128, S_LEN], F32)
                        tA = sb(pscope, "tA", [128, S_LEN], F32)
                        tB = sb(pscope, "tB", [128, S_LEN], F32)
                        mx = sb(pscope, "mx", [128, S_LEN], BF16)
                        t16 = sb(pscope, "t16", [128, 16], F32)
                        R_u, R_A, R_B, R_mx, R_t16 = Res("u"), Res("tA"), Res("tB"), Res("mx"), Res("t16")

                        def load_u(g):
                            i = ctr["wu"] % 2
                            ctr["wu"] += 1
                            S.dma("pool", sl_wu[i], w_u[i][:, :, :],
                                  win[:, :, 1536 + g * 128:1536 + (g + 1) * 128], writes=[R_wu[i]])
                            S.dma("pool", sl_wu[i], w_pl[i][:, :], w_pool_d[l, g], writes=[R_wu[i]])
                            return i

                        def u_mm(g, wi):
                            hb = 4 * (g % 2)
                            for c in range(NCH):
                                cs = slice(c * 512, (c + 1) * 512)
                                S.op("pe", lambda e: [e.matmul(bank(hb + c), w_u[wi][:, kc, :], hT[:, kc, cs],
                                                               start=(kc == 0), stop=(kc == 7))
                                                      for kc in range(8)][-1],
                                     reads=[R_wu[wi], R_h[c]], writes=[PB[hb + c]])

                        wis = {0: load_u(0)}
                        u_mm(0, wis[0])
                        uTs = [uT, uT2]
                        R_us = [R_u, Res("u2")]
                        for g in range(4):
                            wi = wis[g]
                            hb = 4 * (g % 2)
                            uTg, R_ug = uTs[g % 2], R_us[g % 2]
                            S.op("act", lambda e: e.activation(uTg[:, :], ps[:, hb * 512:(hb + 4) * 512], AF.Copy),
                                 reads=PB[hb:hb + 4], writes=[R_ug])
                            if g < 3:
                                wis[g + 1] = load_u(g + 1)
                                u_mm(g + 1, wis[g + 1])
                            src, rsrc = uTg, R_ug
                            tmp = [(tA, R_A), (tB, R_B)]
                            d = 1
                            k = 0
                            while d < WINS[g]:
                                dst, rdst = tmp[k % 2]
                                k += 1
                                S.op("dve", lambda e: e.tensor_tensor(dst[:, d:], src[:, d:], src[:, :S_LEN - d],
                                                                      ALU.add),
                                     reads=[rsrc], writes=[rdst])
                                S.op("dve", lambda e: e.tensor_copy(dst[:, :d], src[:, :d]),
                                     reads=[rsrc], writes=[rdst])
                                src, rsrc = dst, rdst
                                d *= 2
                            w = WINS[g]
                            S.op("dve", lambda e: e.scalar_tensor_tensor(mx[:, :], src[:, :], 1.0 / w, uTg[:, :],
                                                                          ALU.mult, ALU.subtract),
                                 reads=[rsrc, R_ug], writes=[R_mx])
                            S.op("dve", lambda e: e.tensor_tensor(t16[:, :], src[:, :16],
                                                                  pv[:, PV_INVC + g * 16:PV_INVC + (g + 1) * 16],
                                                                  ALU.mult),
                                 reads=[rsrc, R_pv], writes=[R_t16])
                            S.op("dve", lambda e: e.tensor_tensor(mx[:, :16], t16[:, :], uTg[:, :16], ALU.subtract),
                                 reads=[R_t16, R_ug], writes=[R_mx])
                            for c in range(NCH):
                                cs = slice(c * 512, (c + 1) * 512)
                                S.op("pe", lambda e: e.matmul(bank(hb + c), w_pl[wi][:, :], mx[:, cs],
                                                              start=True, stop=True),
                                     reads=[R_wu[wi], R_mx], writes=[PB[hb + c]])
                                S.op("act", lambda e: e.activation(
                                    pooledT[:, g, cs], bank(hb + c), AF.Copy,
                                    scale=pv[:, pb + PV_PSC + g:pb + PV_PSC + g + 1]),
                                    reads=[PB[hb + c], R_pv], writes=[R_pool[c]])
                        S.barrier()
                    if l == 0:
                        tap("pooled0", pooledT[:, :, :], R_pool)

                    if ph == 2 and l == n_layers - 1: S.dead = True
                    with ExitStack() as ascope:
                        NSET = 2
                        qz = [[sb(ascope, "qz%d_%d" % (h_, i), [128, S_LEN], BF16) for i in range(NSET)]
                              for h_ in range(2)]
                        kT = [sb(ascope, "kT%d" % i, [128, S_LEN], BF16) for i in range(NSET)]
                        Vt = [sb(ascope, "V%d" % i, [128, 16, 2, 65], BF16) for i in range(NSET)]
                        nmT = [sb(ascope, "nm%d" % i, [128, 1024], BF16) for i in range(NSET)]
                        ksum = [sb(ascope, "ks%d" % i, [128, 8], F32) for i in range(NSET)]
                        ksb = [sb(ascope, "ksb%d" % i, [128, 16], BF16) for i in range(NSET)]
                        gpad = sb(ascope, "gpad", [128, 2, 8], F32)
                        top8 = sb(ascope, "top8", [128, 2, 8], F32)
                        gw1 = sb(ascope, "gw1", [128, 2, 8], F32)
                        gw2 = sb(ascope, "gw2", [128, 2, 8], F32)
                        R_gw1, R_gw2 = Res("gw1"), Res("gw2")
                        btok = sb(ascope, "btok", [128, 8, 64], BF16)
                        Pt = [sb(ascope, "P%d" % i, [128, 512], BF16) for i in range(3)]
                        rec = sb(ascope, "rec", [128, 4], F32)
                        atok = [sb(ascope, "atok%d" % i, [128, 4, 128], BF16) for i in range(2)]
                        R_q = [[Res("q") for _ in range(NCH)] for _ in range(NSET)]
                        R_k = [[Res("k") for _ in range(NCH)] for _ in range(NSET)]
                        R_v = [[Res("v") for _ in range(NCH)] for _ in range(NSET)]
                        R_nm = [[Res("nm") for _ in range(2)] for _ in range(NSET)]
                        R_ks = [Res("ks") for _ in range(NSET)]
                        R_ksb = [Res("ksb") for _ in range(NSET)]
                        R_gp, R_t8, R_bt = Res("gp"), Res("t8"), Res("bt")
                        R_P = [Res("P") for _ in range(3)]
                        R_rec = Res("rec")
                        R_at = [Res("at") for _ in range(2)]
                        pctr = {"p": 0, "s": 0, "o": 0, "proj": 0}

                        for i in range(NSET):
                            S.op("dve", lambda e: e.memset(Vt[i][:, :, :, 64:65], 1.0), writes=R_v[i])
                        S.op("dve", lambda e: e.memset(btok[:, :, :], 0.0), writes=[R_bt])
                        for i in range(NSET):
                            S.op("dve", lambda e: e.memset(ksb[i][:, :], 0.0), writes=[R_ksb[i]])
                            S.op("dve", lambda e: e.memset(qz[0][i][64:128, :], 0.0), writes=R_q[i])
                            S.op("dve", lambda e: e.memset(qz[1][i][0:64, :], 0.0), writes=R_q[i])
                            S.op("dve", lambda e: e.memset(nmT[i][64:128, :], 0.0), writes=R_nm[i])

                        if ph == 211:
                            S.dead = True

                        def load_qkv(m):
                            i = ctr["wqkv"] % 2
                            ctr["wqkv"] += 1
                            for j in range(3):
                                S.dma("pool", sl_wqkv[i], w_qkv[i][:, j, :, :],
                                      win[:, :, j * 512 + m * 128:j * 512 + (m + 1) * 128], writes=[R_wqkv[i]])
                            return i

                        def project(m, wi, st):
                            for c in range(NCH):
                                cs = slice(c * 512, (c + 1) * 512)
                                for j, (dstT, rr) in enumerate(((None, R_q[st]), (kT[st], R_k[st]))):
                                    b = pctr["proj"] % 2
                                    pctr["proj"] += 1
                                    S.op("pe", lambda e: [e.matmul(bank(b), w_qkv[wi][:, j, kc, :], hT[:, kc, cs],
                                                                   start=(kc == 0), stop=(kc == 7))
                                                          for kc in range(8)][-1],
                                         reads=[R_wqkv[wi], R_h[c]], writes=[PB[b]])
                                    if j == 0:
                                        for h_ in range(2):
                                            S.op("act", lambda e: e.activation(
                                                qz[h_][st][h_ * 64:(h_ + 1) * 64, cs],
                                                ps[h_ * 64:(h_ + 1) * 64, b * 512:(b + 1) * 512], AF.Copy),
                                                reads=[PB[b]], writes=[rr[c]])
                                    else:
                                        S.op("act", lambda e: e.activation(dstT[:, cs], bank(b), AF.Copy),
                                             reads=[PB[b]], writes=[rr[c]])
                                    if j == 1 and ph != 212:
                                        S.op("dve", lambda e: e.tensor_reduce(
                                            ksum[st][:, 2 * c:2 * c + 2],
                                            bank(b).rearrange("p (n k) -> p n k", k=256), AX.X, ALU.add),
                                            reads=[PB[b]], writes=[R_ks[st]])
                            if ph == 212:
                                S.dead = True
                            for hp in range(2):
                                S.op("dve", lambda e: e.tensor_copy(ksb[st][hp * 64:(hp + 1) * 64, hp * 8:hp * 8 + 8],
                                                                    ksum[st][hp * 64:(hp + 1) * 64, :]),
                                     reads=[R_ks[st]], writes=[R_ksb[st]])
                            if ph == 213:
                                S.dead = True
                            for tg in range(4):
                                b = pctr["proj"] % 2
                                pctr["proj"] += 1

                                def vmm(e):
                                    ins = None
                                    for t4 in range(4):
                                        tt = tg * 4 + t4
                                        for kc in range(8):
                                            ins = e.matmul(bank(b, t4 * 128, (t4 + 1) * 128),
                                                           hT[:, kc, tt * 128:(tt + 1) * 128],
                                                           w_qkv[wi][:, 2, kc, :], start=(kc == 0), stop=(kc == 7))
                                    return ins
                                S.op("pe", vmm, reads=[R_wqkv[wi], R_h[tg]], writes=[PB[b]])
                                S.op("dve", lambda e: e.tensor_copy(
                                    Vt[st][:, tg * 4:(tg + 1) * 4, :, 0:64],
                                    bank(b).rearrange("p (t h d) -> p t h d", t=4, h=2)),
                                    reads=[PB[b]], writes=[R_v[st][tg]])

                        def masks_a(m, st):
                            GOFF = 1.0e4
                            S.op("dve", lambda e: e.memset(gpad[:, :, :], 0.0), writes=[R_gp])

                            def gmm(e):
                                ins = None
                                for j in range(8):
                                    qs = slice((8 + j) * 128, (9 + j) * 128)
                                    e.matmul(bank(6, j * 16, j * 16 + 16), qz[0][st][:, qs], ksb[st][:, :],
                                             start=True, stop=False, skip_group_check=True)
                                    ins = e.matmul(bank(6, j * 16, j * 16 + 16), qz[1][st][:, qs], ksb[st][:, :],
                                                   start=False, stop=True, skip_group_check=True)
                                return ins
                            S.op("pe", gmm, reads=[R_q[st][2], R_q[st][3], R_ksb[st]], writes=[PB[6]])
                            chains = []
                            for j in range(8):
                                chains.append(lambda j=j: mask_chain(st, j, GOFF))
                            return chains

                        def mask_chain(st, j, GOFF):
                            if True:
                                qt = 8 + j
                                own = qt // 2
                                S.op("dve", lambda e: e.tensor_scalar(
                                    gpad[:, :, 0:own],
                                    bank(6, j * 16, j * 16 + 16).rearrange("p (h n) -> p h n", h=2)[:, :, 0:own],
                                    GOFF, None, ALU.add),
                                    reads=[PB[6]], writes=[R_gp])
                                src, rsrc = gpad, R_gp
                                for it_ in range(2):
                                    S.op("dve", lambda e: e.tensor_reduce(top8[:, :, it_], src[:, :, :], AX.X, ALU.max),
                                         reads=[rsrc], writes=[R_t8])
                                    dst, rdst = (gw1, R_gw1) if it_ == 0 else (gw2, R_gw2)
                                    for hp in range(2):
                                        S.op("dve", lambda e: e.scalar_tensor_tensor(
                                            dst[:, hp, :], src[:, hp, :], top8[:, hp, it_:it_ + 1], src[:, hp, :],
                                            ALU.is_lt, ALU.mult),
                                            reads=[rsrc, R_t8], writes=[rdst])
                                    src, rsrc = dst, rdst
                                S.op("dve", lambda e: e.tensor_reduce(top8[:, :, 2], src[:, :, :], AX.X, ALU.max),
                                     reads=[rsrc], writes=[R_t8])
                                for hp in range(2):
                                    S.op("dve", lambda e: e.tensor_scalar(
                                        btok[:, j, hp * 32:hp * 32 + 8], gpad[:, hp, :], top8[:, hp, 2:3], NEG,
                                        ALU.is_lt, ALU.mult),
                                        reads=[R_gp, R_t8], writes=[R_bt])
                                S.op("dve", lambda e: e.memset(
                                    btok[:, j, :].rearrange("p (h c) -> p h c", h=2)[:, :, own:own + 1], 0.0),
                                    writes=[R_bt])

                        def masks_b(m, st):
                            for c2 in range(2):
                                trb = ps[:, 7 * 512:8 * 512].bitcast(BF16)

                                def tmm(e):
                                    ins = None
                                    for j in range(4):
                                        ins = e.transpose(trb[0:64, j * 128:(j + 1) * 128], btok[:, c2 * 4 + j, :], ident)
                                    return ins
                                S.op("pe", tmm, reads=[R_bt, R_cb], writes=[PB[7]])
                                S.op("dve", lambda e: e.tensor_copy(nmT[st][0:64, c2 * 512:(c2 + 1) * 512],
                                                                    trb[0:64, 0:512]),
                                     reads=[PB[7]], writes=[R_nm[st][c2]])

                        def attention(m, st, inter=()):
                            inter = list(inter)
                            tiles = [(c, hp, kt) for c in range(NCH) for hp in range(2) for kt in range(4 * c + 4)]
                            info = {}

                            def smm_op(i):
                                c, hp, kt = tiles[i]
                                hs = slice(hp * 64, (hp + 1) * 64)
                                r = kt - 4 * c
                                col0 = 128 * max(r, 0)
                                sbk = (2, 3, 0)[i % 3]
                                need_bias = (c >= 2) and (kt < 4 * c + 2)

                                def smm(e):
                                    last = not need_bias and r < 0
                                    ins = e.matmul(bank(sbk, col0, 512), kT[st][:, kt * 128:(kt + 1) * 128],
                                                   qz[hp][st][:, c * 512 + col0:(c + 1) * 512],
                                                   start=True, stop=last, skip_group_check=True)
                                    if need_bias:
                                        ins = e.matmul(bank(sbk, col0, 512), esel(hp, kt // 2),
                                                       nmT[st][:, (c - 2) * 512 + col0:(c - 1) * 512],
                                                       start=False, stop=(r < 0), skip_group_check=True)
                                    if r >= 0:
                                        ins = e.matmul(bank(sbk, col0, col0 + 128), ident, tri,
                                                       start=False, stop=True, skip_group_check=True)
                                    return ins
                                rd = [R_k[st][kt // 4], R_q[st][c], R_cb]
                                if need_bias:
                                    rd.append(R_nm[st][c - 2])
                                S.op("pe", smm, reads=rd, writes=[PB[sbk]])
                                info[i] = (sbk, col0, r)

                            pend = []

                            def flush_pending(force=False):
                                while pend and (force or pend[0][0] <= 0):
                                    pend.pop(0)[1]()
                                for p_ in pend:
                                    p_[0] -= 1

                            smm_op(0)
                            smm_op(1)
                            ob = None
                            for i, (c, hp, kt) in enumerate(tiles):
                                hs = slice(hp * 64, (hp + 1) * 64)
                                ab = c % 2
                                if kt == 0:
                                    ob = 4 + pctr["o"] % 2
                                    pctr["o"] += 1
                                if i + 2 < len(tiles):
                                    smm_op(i + 2)
                                sbk, col0, r = info[i]
                                pi = pctr["p"] % 3
                                pctr["p"] += 1
                                S.op("act", lambda e: e.activation(Pt[pi][:, col0:512], bank(sbk, col0, 512),
                                                                   AF.Exp, scale=0.125),
                                     reads=[PB[sbk]], writes=[R_P[pi]])

                                def pvmm(e):
                                    ins = None
                                    for j in range(max(r, 0), 4):
                                        ins = e.matmul(bank(ob, j * 128, j * 128 + 65),
                                                       Pt[pi][:, j * 128:(j + 1) * 128], Vt[st][:, kt, hp, :],
                                                       start=(kt == 0 and j == 0), stop=(kt == 4 * c + j),
                                                       skip_group_check=True)
                                    return ins
                                S.op("pe", pvmm, reads=[R_P[pi], R_v[st][kt // 4]], writes=[PB[ob]])
                                flush_pending()
                                if kt == 4 * c + 3:
                                    o3 = bank(ob).rearrange("p (j d) -> p j d", j=4)
                                    S.op("dve", lambda e: e.reciprocal(rec[:, :], o3[:, :, 64]),
                                         reads=[PB[ob]], writes=[R_rec])
                                    S.op("dve", lambda e: e.tensor_tensor(
                                        atok[ab][:, :, hs], o3[:, :, 0:64],
                                        rec[:, :].unsqueeze(2).to_broadcast([128, 4, 64]), ALU.mult),
                                        reads=[PB[ob], R_rec], writes=[R_at[ab]])
                                    if inter:
                                        inter.pop(0)()
                                    if hp == 1:
                                        def do_tr(c=c, ab=ab):
                                            trb = ps[:, 7 * 512:8 * 512].bitcast(BF16)

                                            def tmm2(e):
                                                ins = None
                                                for j in range(4):
                                                    ins = e.transpose(trb[:, j * 128:(j + 1) * 128],
                                                                      atok[ab][:, j, :], ident)
                                                return ins
                                            S.op("pe", tmm2, reads=[R_at[ab], R_cb], writes=[PB[7]])
                                            S.op("dve", lambda e: e.tensor_copy(
                                                attnT[:, m, c * 512:(c + 1) * 512], trb[:, 0:512]),
                                                reads=[PB[7]], writes=[R_attn[c]])
                                        pend.append([2, do_tr])
                            flush_pending(force=True)
                            while inter:
                                inter.pop(0)()

                        wnext = load_qkv(0)
                        project(0, wnext, 0)
                        for ch in masks_a(0, 0):
                            ch()
                        for m in range(4):
                            st = m % NSET
                            if m < 3:
                                wnext = load_qkv(m + 1)
                                project(m + 1, wnext, (m + 1) % NSET)
                            masks_b(m, st)
                            if l == 0 and m == 0:
                                tap("q0", qz[0][0][:, :], R_q[0])
                                tap("k0", kT[0][:, :], R_k[0])
                                tap("v0", Vt[0][:, :, :, :], R_v[0])
                                tap("nm0", nmT[0][0:64, :], R_nm[0])
                            inter = masks_a(m + 1, (m + 1) % NSET) if m < 3 else ()
                            attention(m, st, inter)
                        S.barrier()
                    if l == 0:
                        tap("attn0", attnT[:, :, :], R_attn)

                    if ph == 3 and l == n_layers - 1: S.dead = True
                    with ExitStack() as gscope:
                        snap_g = S.snapshot()
                        mergedT = sb(gscope, "mergedT", [128, 8, S_LEN], BF16)
                        w_o = [sb(gscope, "wo%d" % i, [128, 8, 128], BF16) for i in range(2)]
                        R_mg = [Res("mg%d" % c) for c in range(NCH)]
                        s0 = [sb(gscope, "s0_%d" % i, [128, 512], F32) for i in range(2)]
                        s1 = [sb(gscope, "s1_%d" % i, [128, 512], F32) for i in range(2)]
                        R_s0 = [Res("s0") for _ in range(2)]
                        R_s1 = [Res("s1") for _ in range(2)]
                        wa_v = wview(w_a_d, l)
                        wb_v = wview(w_b_d, l)

                        def load_m(m):
                            i = ctr["wm"] % 2
                            ctr["wm"] += 1
                            ms_ = slice(m * 128, (m + 1) * 128)
                            S.dma("pool", sl_wm[i], w_m[i][:, 0:4, :], wa_v[:, :, ms_], writes=[R_wm[i]])
                            S.dma("pool", sl_wm[i], w_m[i][:, 4:8, :], wb_v[:, :, ms_], writes=[R_wm[i]])
                            S.dma("pool", sl_wm[i], w_m[i][:, 8:16, :], win[:, :, 2048 + m * 128:2048 + (m + 1) * 128],
                                  writes=[R_wm[i]])
                            S.dma("pool", sl_wm[i], w_m[i][:, 16:24, :], win[:, :, 3072 + m * 128:3072 + (m + 1) * 128],
                                  writes=[R_wm[i]])
                            return i

                        nxt = load_m(0)
                        it = 0
                        for m in range(8):
                            wi = nxt
                            if m < 7:
                                nxt = load_m(m + 1)
                            for c in range(NCH):
                                cs = slice(c * 512, (c + 1) * 512)
                                b0 = 4 * (it % 2)
                                ti = it % 2
                                it += 1
                                S.op("pe", lambda e: [e.matmul(bank(b0), w_m[wi][:, kc, :], attnT[:, kc, cs],
                                                               start=(kc == 0), stop=(kc == 3)) for kc in range(4)][-1],
                                     reads=[R_wm[wi], R_attn[c]], writes=[PB[b0]])
                                S.op("pe", lambda e: [e.matmul(bank(b0 + 1), w_m[wi][:, 4 + kc, :], pooledT[:, kc, cs],
                                                               start=(kc == 0), stop=(kc == 3)) for kc in range(4)][-1],
                                     reads=[R_wm[wi], R_pool[c]], writes=[PB[b0 + 1]])
                                S.op("pe", lambda e: [e.matmul(bank(b0 + 2), w_m[wi][:, 8 + kc, :], hT[:, kc, cs],
                                                               start=(kc == 0), stop=(kc == 7)) for kc in range(8)][-1],
                                     reads=[R_wm[wi], R_h[c]], writes=[PB[b0 + 2]])
                                S.op("pe", lambda e: [e.matmul(bank(b0 + 3), w_m[wi][:, 16 + kc, :], hT[:, kc, cs],
                                                               start=(kc == 0), stop=(kc == 7)) for kc in range(8)][-1],
                                     reads=[R_wm[wi], R_h[c]], writes=[PB[b0 + 3]])
                                S.op("act", lambda e: e.activation(s0[ti][:, :], bank(b0 + 2), AF.Sigmoid),
                                     reads=[PB[b0 + 2]], writes=[R_s0[ti]])
                                S.op("act", lambda e: e.activation(s1[ti][:, :], bank(b0 + 3), AF.Sigmoid),
                                     reads=[PB[b0 + 3]], writes=[R_s1[ti]])
                                S.op("dve", lambda e: e.tensor_tensor(s0[ti][:, :], s0[ti][:, :], bank(b0), ALU.mult),
                                     reads=[R_s0[ti], PB[b0]], writes=[R_s0[ti]])
                                S.op("dve", lambda e: e.tensor_tensor(s1[ti][:, :], s1[ti][:, :], bank(b0 + 1), ALU.mult),
                                     reads=[R_s1[ti], PB[b0 + 1]], writes=[R_s1[ti]])
                                S.op("dve", lambda e: e.tensor_tensor(mergedT[:, m, cs], s0[ti][:, :], s1[ti][:, :],
                                                                      ALU.add),
                                     reads=[R_s0[ti], R_s1[ti]], writes=[R_mg[c]])
                        if l == 0:
                            tap("merged0", mergedT[:, :, :], R_mg)
                        wo_v = wview(w_out_d, l)

                        def load_o(m):
                            i = ctr["wo"] % 2
                            ctr["wo"] += 1
                            S.dma("pool", sl_wo[i], w_o[i][:, :, :], wo_v[:, :, m * 128:(m + 1) * 128], writes=[R_wo[i]])
                            return i

                        S.wait_events("pool", snap_g)
                        nxt = load_o(0)
                        for m in range(8):
                            wi = nxt
                            if m < 7:
                                nxt = load_o(m + 1)
                            for c in range(NCH):
                                cs = slice(c * 512, (c + 1) * 512)
                                b = ctr["pb"] % 8
                                ctr["pb"] += 1
                                S.op("pe", lambda e: [e.matmul(bank(b), w_o[wi][:, kc, :], mergedT[:, kc, cs],
                                                               start=(kc == 0), stop=(kc == 7)) for kc in range(8)][-1],
                                     reads=[R_wo[wi], R_mg[c]], writes=[PB[b]])
                                S.op("dve", lambda e: e.tensor_tensor(xT[:, m, cs], xT[:, m, cs], bank(b), ALU.add),
                                     reads=[R_x[c], PB[b]], writes=[R_x[c]])
                        S.barrier()
                if l == 0:
                    tap("x0m", xT[:, :, :], R_x)

                if ph == 4 and l == n_layers - 1: S.dead = True
                snap_f = S.snapshot()
                wup_v = wview(w_up_d, l)
                wdn_v = wview(w_down_d, l)
                with ExitStack() as fs:
                    hid = sb(fs, "hid", [128, 8, S_LEN], BF16)
                    w_up = [sb(fs, "wup%d" % i, [128, 2, 8, 128], BF16) for i in range(NUP)]
                    w_dn = [sb(fs, "wdn%d" % i, [128, 8, 128], BF16) for i in range(2)]
                    R_hid = [Res("hid%d" % i) for i in range(8)]
                    S.wait_events("pool", snap_f)
                    pre_up = [None]
                    deferred_norm = [True]
                    tg2 = None
                    R_tg2 = [Res("tg%d" % i) for i in range(2)]
                    R_tv2 = [Res("tv%d" % i) for i in range(2)]

                    def load_up(i_):
                        i = ctr["wup"] % NUP
                        ctr["wup"] += 1
                        S.dma("pool", sl_wup[i], w_up[i][:, 0, :, :], wup_v[:, :, i_ * 128:(i_ + 1) * 128],
                              writes=[R_wup[i]])
                        S.dma("pool", sl_wup[i], w_up[i][:, 1, :, :],
                              wup_v[:, :, DFF + i_ * 128:DFF + (i_ + 1) * 128], writes=[R_wup[i]])
                        return i

                    def load_dn(m, lo, hi):
                        i = ctr["wdn"] % 2
                        ctr["wdn"] += 1
                        S.dma("pool", sl_wdn[i], w_dn[i][:, 0:hi - lo, :], wdn_v[:, lo:hi, m * 128:(m + 1) * 128],
                              writes=[R_wdn[i]])
                        return i

                    pre_up[0] = load_up(0)
                    rmsnorm(pb + PV_GFFN, h_out, "f%d" % l)
                    tg2 = [sb(fs, "tg%d" % i, [128, S_LEN], F32) for i in range(2)]
                    tv2 = [sb(fs, "tv%d" % i, [128, S_LEN], F32) for i in range(2)]
                    for (lo, hi) in FPASS:
                        nxt = pre_up[0] if lo == 0 else load_up(lo)
                        for i_ in range(lo, hi):
                            wi = nxt
                            if i_ + 1 < hi:
                                nxt = load_up(i_ + 1)
                            il = i_ - lo
                            tg, tv, R_tg, R_tv = tg2[i_ % 2], tv2[i_ % 2], R_tg2[i_ % 2], R_tv2[i_ % 2]
                            for half, (tt, rt) in enumerate(((tg, R_tg), (tv, R_tv))):
                                hb = 4 * half
                                col = i_ + 22 * half
                                for c in range(NCH):
                                    cs = slice(c * 512, (c + 1) * 512)
                                    S.op("pe", lambda e: [e.matmul(bank(hb + c), w_up[wi][:, half, kc, :], hT[:, kc, cs],
                                                                   start=(kc == 0), stop=(kc == 7))
                                                          for kc in range(8)][-1],
                                         reads=[R_wup[wi], R_h[c]], writes=[PB[hb + c]])
                                pfull = ps[:, hb * 512:(hb + 4) * 512]
                                c0 = pv[:, pb + PV_CW0 + col:pb + PV_CW0 + col + 1]
                                c1 = pv[:, pb + PV_CW1 + col:pb + PV_CW1 + col + 1]
                                c2 = pv[:, pb + PV_CW2 + col:pb + PV_CW2 + col + 1]
                                cbias = pv[:, pb + PV_CB + col:pb + PV_CB + col + 1]
                                S.op("act", lambda e: e.activation(tt[:, :], pfull, AF.Identity, bias=cbias, scale=c2),
                                     reads=PB[hb:hb + 4] + [R_pv], writes=[rt])
                                S.op("dve", lambda e: e.scalar_tensor_tensor(
                                    tt[:, 1:], pfull[:, 0:S_LEN - 1], c1, tt[:, 1:], ALU.mult, ALU.add),
                                    reads=PB[hb:hb + 4] + [rt, R_pv], writes=[rt])
                                S.op("dve", lambda e: e.scalar_tensor_tensor(
                                    tt[:, 2:], pfull[:, 0:S_LEN - 2], c0, tt[:, 2:], ALU.mult, ALU.add),
                                    reads=PB[hb:hb + 4] + [rt, R_pv], writes=[rt])
                                if half == 0:
                                    S.op("act", lambda e: e.activation(tt[:, :], tt[:, :], AF.Silu),
                                         reads=[rt], writes=[rt])
                            S.op("dve", lambda e: e.tensor_tensor(hid[:, il, :], tg[:, :], tv[:, :], ALU.mult),
                                 reads=[R_tg, R_tv], writes=[R_hid[il]])
                        if l == 0 and lo == 0:
                            tap("hid0", hid[:, :, :], R_hid)
                        nk = hi - lo
                        nxt = load_dn(0, lo, hi)
                        for m in range(8):
                            wi = nxt
                            if m < 7:
                                nxt = load_dn(m + 1, lo, hi)
                            for c in range(NCH):
                                cs = slice(c * 512, (c + 1) * 512)
                                b = ctr["pb"] % 8
                                ctr["pb"] += 1
                                S.op("pe", lambda e: [e.matmul(bank(b), w_dn[wi][:, kc, :], hid[:, kc, cs],
                                                               start=(kc == 0), stop=(kc == nk - 1))
                                                      for kc in range(nk)][-1],
                                     reads=[R_wdn[wi]] + R_hid[0:nk], writes=[PB[b]])
                                S.op("dve", lambda e: e.tensor_tensor(xT[:, m, cs], xT[:, m, cs], bank(b), ALU.add),
                                     reads=[R_x[c], PB[b]], writes=[R_x[c]])
                    S.barrier()
                if l == 0:
                    tap("x0f", xT[:, :, :], R_x)

        except _Stop:
            pass
        S.dead = False
        S.barrier()

        with ExitStack() as os_:
            stg = [sb(os_, "stg%d" % i, [128, 8, 512], F32) for i in range(2)]
            R_stg = [Res("stg%d" % i) for i in range(2)]
            yv = yT_d.rearrange("(c p) t -> p c t", p=128)
            outs = []

            def y_out(c):
                return stg[c % 2][:, :, :], [R_stg[c % 2]]

            if final_norm:
                with ExitStack() as ns:
                    sq = [sb(ns, "sqz%d" % i, [128, 8, 512], BF16) for i in range(2)]
                    R_sq = [Res("sq%d" % i) for i in range(2)]
                    rs = [sb(ns, "rsz%d" % i, [128, 512], F32) for i in range(2)]
                    R_rs = [Res("rs%d" % i) for i in range(2)]
                    for c in range(NCH):
                        i = c % 2
                        cs = slice(c * 512, (c + 1) * 512)
                        S.op("act", lambda e: e.activation(sq[i][:, :, :], xT[:, :, cs], AF.Square),
                             reads=[R_x[c]], writes=[R_sq[i]])
                        b = i
                        S.op("pe", lambda e: [e.matmul(bank(b), ones, sq[i][:, kc, :], start=(kc == 0),
                                                       stop=(kc == 7)) for kc in range(8)][-1],
                             reads=[R_sq[i], R_cb], writes=[PB[b]])
                        S.op("act", lambda e: e.activation(rs[i][:, :], bank(b), AF.Sqrt,
                                                           bias=pv[:, PV_EPS:PV_EPS + 1], scale=1.0 / D),
                             reads=[PB[b], R_pv], writes=[R_rs[i]])
                        S.op("dve", lambda e: e.reciprocal(rs[i][:, :], rs[i][:, :]),
                             reads=[R_rs[i]], writes=[R_rs[i]])
                        for kc in range(8):
                            S.op("dve", lambda e: e.scalar_tensor_tensor(
                                stg[i][:, kc, :], xT[:, kc, cs], pv[:, PV_GFIN + kc:PV_GFIN + kc + 1], rs[i][:, :],
                                ALU.mult, ALU.mult),
                                reads=[R_x[c], R_rs[i], R_pv], writes=[R_stg[i]])
                        outs.append(S.dma("sp", sl_out[i], yv[:, :, cs], stg[i][:, :, :], reads=[R_stg[i]]))
            else:
                for c in range(NCH):
                    cs = slice(c * 512, (c + 1) * 512)
                    outs.append(S.dma("sp", sl_out[c % 2], yv[:, :, cs], xT[:, :, cs], reads=[R_x[c]]))
            lastout = {}
            for ev in outs:
                lastout[ev.key] = ev
            for ev in lastout.values():
                S._wait("sp", ev)
            for ev in tap_ev:
                S._wait("sp", ev)
            S.barrier()
    return nc, tap_d


def host_consts():
    cbm = np.zeros((128, NCB), np.float32)
    cbm[:, CB_ID:CB_ID + 128] = np.eye(128, dtype=np.float32)
    cbm[:, CB_ONES:CB_ONES + 128] = 1.0
    for hp in range(2):
        for n in range(8):
            o = CB_ESEL + (hp * 8 + n) * 128
            cbm[32 * hp + n, o:o + 128] = 1.0
    key = np.arange(128)[:, None]
    q = np.arange(128)[None, :]
    cbm[:, CB_TRI:CB_TRI + 128] = np.where(key <= q, 0.0, NEG).astype(np.float32)
    return cbm


def host_pvec(norm_mix_g, norm_ffn_g, pool_scale, conv_w, conv_b, norm_final_g):
    pvm = np.zeros((128, NPV), np.float32)

    def cols(v):
        v = np.asarray(v, np.float32)
        return np.ascontiguousarray(v.reshape(-1, 128).T)

    for l in range(L_DEPTH):
        b = l * PV_L
        pvm[:, b + PV_GMIX:b + PV_GMIX + 8] = cols(norm_mix_g[l])
        pvm[:, b + PV_GFFN:b + PV_GFFN + 8] = cols(norm_ffn_g[l])
        pvm[:, b + PV_PSC:b + PV_PSC + 4] = cols(pool_scale[l])
        pvm[:, b + PV_CW0:b + PV_CW0 + 44] = cols(conv_w[l][0])
        pvm[:, b + PV_CW1:b + PV_CW1 + 44] = cols(conv_w[l][1])
        pvm[:, b + PV_CW2:b + PV_CW2 + 44] = cols(conv_w[l][2])
        pvm[:, b + PV_CB:b + PV_CB + 44] = cols(conv_b[l])
    pvm[:, PV_GFIN:PV_GFIN + 8] = cols(norm_final_g)
    pvm[:, PV_EPS] = EPS
    for g, w in enumerate(WINS):
        t = np.arange(16)
        pvm[:, PV_INVC + g * 16:PV_INVC + (g + 1) * 16] = (1.0 / np.minimum(w, t + 1)).astype(np.float32)[None, :]
    return pvm


_NC_CACHE = {}


def make_in_maps(x, norm_mix_g, w_in, w_pool, pool_scale, w_branch_a, w_branch_b, w_out,
                 norm_ffn_g, w_up, conv_w, conv_b, w_down, norm_final_g):
    f = lambda a: np.ascontiguousarray(np.asarray(a, dtype=np.float32))
    x = f(x)
    shared = {
        "w_in": f(w_in), "w_pool": f(w_pool), "w_branch_a": f(w_branch_a), "w_branch_b": f(w_branch_b),
        "w_out": f(w_out), "w_up": f(w_up), "w_down": f(w_down),
        "pvec": host_pvec(f(norm_mix_g), f(norm_ffn_g), f(pool_scale), f(conv_w), f(conv_b), f(norm_final_g)),
        "cb": host_consts(),
    }
    in_maps = []
    for b in range(8):
        d = dict(shared)
        d["xT"] = np.ascontiguousarray(x[b].T)
        in_maps.append(d)
    return in_maps


def kernel(x, norm_mix_g, w_in, w_pool, pool_scale, w_branch_a, w_branch_b, w_out,
           norm_ffn_g, w_up, conv_w, conv_b, w_down, norm_final_g):
    in_maps = make_in_maps(x, norm_mix_g, w_in, w_pool, pool_scale, w_branch_a, w_branch_b, w_out,
                           norm_ffn_g, w_up, conv_w, conv_b, w_down, norm_final_g)
    if "nc" not in _NC_CACHE:
        _NC_CACHE["nc"] = build()[0]
    nc = _NC_CACHE["nc"]
    res = run_bass_kernel_spmd(nc, in_maps, core_ids=list(range(8)))
    out = np.stack([np.ascontiguousarray(np.asarray(r["yT"], dtype=np.float32).T) for r in res.results], axis=0)
    return out.astype(np.float32)
```

```python
from contextlib import ExitStack
import numpy as np
import concourse.bass as bass
import concourse.mybir as mybir
from concourse.bass_utils import run_bass_kernel_spmd

F32 = mybir.dt.float32
BF16 = mybir.dt.bfloat16
ALU = mybir.AluOpType
AF = mybir.ActivationFunctionType
AX = mybir.AxisListType

D = 1024
S_LEN = 2048
L_DEPTH = 2
NCH = 4
DFF = 2816
NFT = DFF // 128
WINS = (2, 4, 8, 16)
NEG = -30000.0
EPS = 1e-6

PV_GMIX = 0
PV_GFFN = 8
PV_PSC = 16
PV_CW0 = 20
PV_CW1 = PV_CW0 + 44
PV_CW2 = PV_CW1 + 44
PV_CB = PV_CW2 + 44
PV_L = PV_CB + 44
PV_GFIN = 2 * PV_L
PV_INVC = PV_GFIN + 8
PV_EPS = PV_INVC + 64
NPV = PV_EPS + 1
CB_ID = 0
CB_ONES = 128
CB_ESEL = 256
CB_TRI = CB_ESEL + 2048
NCB = CB_TRI + 128


class Res:
    __slots__ = ("name", "w", "rs", "excl")

    def __init__(self, name, excl=False):
        self.name = name
        self.w = None
        self.rs = {}
        self.excl = excl


class Ev:
    __slots__ = ("key", "val", "vc")

    def __init__(self, key, val, vc):
        self.key, self.val, self.vc = key, val, vc


class Slot:
    def __init__(self, key, sem):
        self.key, self.sem, self.count = key, sem, 0


class Sched:
    EPOCH = 8192

    def __init__(self, nc, stack):
        self.nc, self.stack = nc, stack
        self.E = {"pe": nc.tensor, "act": nc.scalar, "dve": nc.vector,
                  "pool": nc.gpsimd, "sp": nc.sync}
        self.pos = {e: 0 for e in self.E}
        self.known = {e: {} for e in self.E}
        self.sems = {}
        self.last = {}
        self.nslot = 0
        self.scoped = set()
        self.dead = False

    def _sem(self, key, epoch):
        k = (key, epoch)
        if k not in self.sems:
            self.sems[k] = self.stack.enter_context(self.nc.semaphore("s_%s_%d" % (key, epoch)))
        return self.sems[k]

    def slot(self, name, scoped=False):
        self.nslot += 1
        key = "d_%s" % name
        if scoped:
            self.scoped.add(key)
        return Slot(key, self._sem(key, 0))

    def _wait(self, eng, ev):
        kn = self.known[eng]
        if kn.get(ev.key, 0) >= ev.val:
            return
        if ev.key in self.E:
            ep, v = divmod(ev.val - 1, self.EPOCH)
            self.E[eng].wait_ge(self._sem(ev.key, ep), v + 1)
        else:
            self.E[eng].wait_ge(self._sem(ev.key, 0), ev.val)
        for k, v in ev.vc.items():
            if kn.get(k, 0) < v:
                kn[k] = v

    def _sync(self, eng, reads, writes, is_dma):
        for r in reads:
            if r.w is not None:
                ev = r.w
                if ev.key == eng and eng == "pe" and not is_dma:
                    continue
                self._wait(eng, ev)
            if r.excl:
                for ev in list(r.rs.values()):
                    if ev.key != eng:
                        self._wait(eng, ev)
        for r in writes:
            if r.w is not None and (is_dma or r.w.key != eng or eng != "pe"):
                self._wait(eng, r.w)
            for ev in r.rs.values():
                if is_dma or ev.key != eng or eng != "pe":
                    self._wait(eng, ev)

    def _commit(self, ev, reads, writes):
        self.last[ev.key] = ev
        for r in reads:
            o = r.rs.get(ev.key)
            if o is None or o.val < ev.val:
                r.rs[ev.key] = ev
        for r in writes:
            r.w = ev
            r.rs = {}

    def op(self, eng, fn, reads=(), writes=()):
        if self.dead:
            return None
        self._sync(eng, reads, writes, False)
        ins = fn(self.E[eng])
        self.pos[eng] += 1
        p = self.pos[eng]
        ep, _ = divmod(p - 1, self.EPOCH)
        ins.then_inc(self._sem(eng, ep), 1)
        vc = dict(self.known[eng])
        vc[eng] = p
        ev = Ev(eng, p, vc)
        self._commit(ev, reads, writes)
        return ev

    def dma(self, q, slot, out, in_, reads=(), writes=()):
        if self.dead:
            return None
        self._sync(q, reads, writes, True)
        slot.count += 16
        self.E[q].dma_start(out=out, in_=in_).then_inc(slot.sem, 16)
        vc = dict(self.known[q])
        vc[slot.key] = slot.count
        ev = Ev(slot.key, slot.count, vc)
        self._commit(ev, reads, writes)
        return ev

    def snapshot(self):
        return [ev for k, ev in self.last.items() if k in self.E or k in self.scoped]

    def wait_events(self, eng, evs):
        if self.dead:
            return
        for ev in evs:
            if ev.key != eng:
                self._wait(eng, ev)

    def barrier(self, engines=("act", "dve", "sp")):
        if self.dead:
            return
        evs = [ev for k, ev in self.last.items() if k in self.E or k in self.scoped]
        for e in engines:
            for ev in evs:
                if ev.key == e:
                    continue
                self._wait(e, ev)


class _Stop(Exception):
    pass


def build(n_layers=L_DEPTH, final_norm=True, taps=(), ph=9):
    nc = bass.Bass("TRN2", target_bir_lowering=False)
    L = L_DEPTH

    def din(name, shape):
        return nc.dram_tensor(name, shape, F32, kind="ExternalInput").ap()

    xT_d = din("xT", [D, S_LEN])
    w_in_d = din("w_in", [L, D, 4096])
    w_pool_d = din("w_pool", [L, 4, 128, 128])
    w_a_d = din("w_branch_a", [L, 512, D])
    w_b_d = din("w_branch_b", [L, 512, D])
    w_out_d = din("w_out", [L, D, D])
    w_up_d = din("w_up", [L, D, 2 * DFF])
    w_down_d = din("w_down", [L, DFF, D])
    pvec_d = din("pvec", [128, NPV])
    cb_d = din("cb", [128, NCB])
    yT_d = nc.dram_tensor("yT", [D, S_LEN], F32, kind="ExternalOutput").ap()
    tap_d = {}
    tap_ev = []

    with ExitStack() as es:
        S = Sched(nc, es)

        uid = [0]

        def sb(stack, name, shape, dt):
            uid[0] += 1
            return stack.enter_context(nc.sbuf_tensor("%s_u%d" % (name, uid[0]), shape, dt))

        ps = es.enter_context(nc.psum_tensor("ps", [128, 4096], F32))
        PB = [Res("pb%d" % i, True) for i in range(8)]

        def bank(b, lo=0, hi=512):
            return ps[:, b * 512 + lo:b * 512 + hi]

        xT = sb(es, "xTs", [128, 8, S_LEN], F32)
        hT = sb(es, "hTs", [128, 8, S_LEN], BF16)
        pv = sb(es, "pv", [128, NPV], F32)
        cb = sb(es, "cbs", [128, NCB], BF16)
        R_x = [Res("x%d" % c) for c in range(NCH)]
        R_h = [Res("h%d" % c) for c in range(NCH)]
        R_pv = Res("pv")
        R_cb = Res("cb")
        ident = cb[:, CB_ID:CB_ID + 128]
        ones = cb[:, CB_ONES:CB_ONES + 128]
        tri = cb[:, CB_TRI:CB_TRI + 128]

        def esel(hp, n):
            o = CB_ESEL + (hp * 8 + n) * 128
            return cb[:, o:o + 128]

        w_qkv = [sb(es, "wqkv%d" % i, [128, 3, 8, 128], BF16) for i in range(2)]
        R_wqkv = [Res("wqkv%d" % i) for i in range(2)]
        sl_wqkv = [S.slot("wqkv%d" % i) for i in range(2)]
        w_u = [sb(es, "wu%d" % i, [128, 8, 128], BF16) for i in range(2)]
        w_pl = [sb(es, "wpl%d" % i, [128, 128], BF16) for i in range(4)]
        R_wpl = [Res("wpl%d" % i) for i in range(4)]
        sl_wpl = [S.slot("wpl%d" % i) for i in range(4)]
        R_wu = [Res("wu%d" % i) for i in range(2)]
        sl_wu = [S.slot("wu%d" % i) for i in range(2)]
        w_m = [sb(es, "wm%d" % i, [128, 24, 128], BF16) for i in range(2)]
        R_wm = [Res("wm%d" % i) for i in range(2)]
        sl_wm = [S.slot("wm%d" % i) for i in range(2)]
        R_wo = [Res("wo%d" % i) for i in range(2)]
        sl_wo = [S.slot("wo%d" % i, True) for i in range(2)]
        NUP = 2
        R_wup = [Res("wup%d" % i) for i in range(NUP)]
        sl_wup = [S.slot("wup%d" % i, True) for i in range(NUP)]
        FPASS = ((0, 8), (8, 15), (15, 22))
        R_wdn = [Res("wdn%d" % i) for i in range(2)]
        sl_wdn = [S.slot("wdn%d" % i, True) for i in range(2)]

        sl_in = S.slot("in")
        sl_cb = S.slot("cb")
        sl_x = [S.slot("x%d" % c, True) for c in range(NCH)]
        sl_out = [S.slot("out%d" % c, True) for c in range(2)]
        sl_tap = S.slot("tap", True)

        def tap(name, ap, reads):
            if name not in taps or S.dead:
                return
            t = nc.dram_tensor("tap_" + name, list(ap.shape), ap.dtype, kind="ExternalOutput").ap()
            tap_d[name] = t
            tap_ev.append(S.dma("sp", S.slot("tap_" + name, True), t, ap, reads=reads))

        S.dma("sp", sl_in, pv[:, :], pvec_d, writes=[R_pv])
        S.dma("pool", sl_cb, cb[:, :], cb_d, writes=[R_cb])
        xv = xT_d.rearrange("(c p) t -> p c t", p=128)
        for c in range(NCH):
            S.dma("sp", sl_x[c], xT[:, :, c * 512:(c + 1) * 512], xv[:, :, c * 512:(c + 1) * 512],
                  writes=[R_x[c]])

        ctr = {"wu": 0, "wqkv": 0, "wm": 0, "wo": 0, "wup": 0, "wdn": 0, "pb": 0}

        def wview(d, l):
            return d[l].rearrange("(kc p) n -> p kc n", p=128)

        def rmsnorm(gcol, out_fn, tag):
            with ExitStack() as ns:
                sq = [sb(ns, "sq%s%d" % (tag, i), [128, 8, 512], BF16) for i in range(2)]
                R_sq = [Res("sq%d" % i) for i in range(2)]
                rs = [sb(ns, "rs%s%d" % (tag, i), [128, 512], F32) for i in range(2)]
                R_rs = [Res("rs%d" % i) for i in range(2)]
                for c in range(NCH):
                    i = c % 2
                    cs = slice(c * 512, (c + 1) * 512)
                    S.op("act", lambda e: e.activation(sq[i][:, :, :], xT[:, :, cs], AF.Square),
                         reads=[R_x[c]], writes=[R_sq[i]])
                    b = i
                    S.op("pe", lambda e: [e.matmul(bank(b), ones, sq[i][:, kc, :], start=(kc == 0),
                                                   stop=(kc == 7)) for kc in range(8)][-1],
                         reads=[R_sq[i], R_cb], writes=[PB[b]])
                    S.op("act", lambda e: e.activation(rs[i][:, :], bank(b), AF.Sqrt,
                                                       bias=pv[:, PV_EPS:PV_EPS + 1], scale=1.0 / D),
                         reads=[PB[b], R_pv], writes=[R_rs[i]])
                    S.op("dve", lambda e: e.reciprocal(rs[i][:, :], rs[i][:, :]),
                         reads=[R_rs[i]], writes=[R_rs[i]])
                    dst, dres = out_fn(c)
                    for kc in range(8):
                        S.op("dve", lambda e: e.scalar_tensor_tensor(
                            dst[:, kc, :], xT[:, kc, cs], pv[:, gcol + kc:gcol + kc + 1], rs[i][:, :],
                            ALU.mult, ALU.mult),
                            reads=[R_x[c], R_rs[i], R_pv], writes=dres)
                S.barrier()

        def h_out(c):
            return hT[:, :, c * 512:(c + 1) * 512], [R_h[c]]

        try:
            for l in range(n_layers):
                pb = l * PV_L
                rmsnorm(pb + PV_GMIX, h_out, "a%d" % l)
                if l == 0:
                    tap("h0", hT[:, :, :], R_h)
                if ph == 1 and l == n_layers - 1: S.dead = True
                win = wview(w_in_d, l)

                with ExitStack() as ms:
                    pooledT = sb(ms, "pooledT", [128, 4, S_LEN], BF16)
                    attnT = sb(ms, "attnT", [128, 4, S_LEN], BF16)
                    R_pool = [Res("pool%d" % c) for c in range(NCH)]
                    R_attn = [Res("attn%d" % c) for c in range(NCH)]

                    with ExitStack() as pscope:
                        uT = sb(pscope, "uT", [128, S_LEN], F32)
                        uT2 = sb(pscope, "uT2", [128, S_LEN], F32)
                        uT3 = sb(pscope, "uT3", [128, S_LEN], F32)
                        tA = sb(pscope, "tA", [128, S_LEN], F32)
                        tB = sb(pscope, "tB", [128, S_LEN], F32)
                        mx = sb(pscope, "mx", [128, S_LEN], BF16)
                        t16 = sb(pscope, "t16", [128, 16], F32)
                        R_u, R_A, R_B, R_mx, R_t16 = Res("u"), Res("tA"), Res("tB"), Res("mx"), Res("t16")

                        def load_u(g):
                            i = ctr["wu"] % 2
                            ctr["wu"] += 1
                            S.dma("pool", sl_wu[i], w_u[i][:, :, :],
                                  win[:, :, 1536 + g * 128:1536 + (g + 1) * 128], writes=[R_wu[i]])
                            S.dma("pool", sl_wpl[g], w_pl[g][:, :], w_pool_d[l, g], writes=[R_wpl[g]])
                            return i

                        def u_mm(g, wi, hb):
                            for c in range(NCH):
                                cs = slice(c * 512, (c + 1) * 512)
                                S.op("pe", lambda e: [e.matmul(bank(hb + c), w_u[wi][:, kc, :], hT[:, kc, cs],
                                                               start=(kc == 0), stop=(kc == 7))
                                                      for kc in range(8)][-1],
                                     reads=[R_wu[wi], R_h[c]], writes=[PB[hb + c]])

                        def u_evac(pos):
                            hb = 4 * (pos % 2)
                            uTg, R_ug = uTs[pos % 3], R_us[pos % 3]
                            S.op("act", lambda e: e.activation(uTg[:, :], ps[:, hb * 512:(hb + 4) * 512], AF.Copy),
                                 reads=PB[hb:hb + 4], writes=[R_ug])

                        def chain_and_pool(pos, g, wi):
                            hb = 4 * (pos % 2)
                            uTg, R_ug = uTs[pos % 3], R_us[pos % 3]
                            src, rsrc = uTg, R_ug
                            tmp = [(tA, R_A), (tB, R_B)]
                            d = 1
                            k = 0
                            while d < WINS[g]:
                                dst, rdst = tmp[k % 2]
                                k += 1
                                S.op("dve", lambda e: e.tensor_tensor(dst[:, d:], src[:, d:], src[:, :S_LEN - d],
                                                                      ALU.add),
                                     reads=[rsrc], writes=[rdst])
                                S.op("dve", lambda e: e.tensor_copy(dst[:, :d], src[:, :d]),
                                     reads=[rsrc], writes=[rdst])
                                src, rsrc = dst, rdst
                                d *= 2
                            w = WINS[g]
                            S.op("dve", lambda e: e.scalar_tensor_tensor(mx[:, :], src[:, :], 1.0 / w, uTg[:, :],
                                                                          ALU.mult, ALU.subtract),
                                 reads=[rsrc, R_ug], writes=[R_mx])
                            S.op("dve", lambda e: e.tensor_tensor(t16[:, :], src[:, :16],
                                                                  pv[:, PV_INVC + g * 16:PV_INVC + (g + 1) * 16],
                                                                  ALU.mult),
                                 reads=[rsrc, R_pv], writes=[R_t16])
                            S.op("dve", lambda e: e.tensor_tensor(mx[:, :16], t16[:, :], uTg[:, :16], ALU.subtract),
                                 reads=[R_t16, R_ug], writes=[R_mx])
                            for c in range(NCH):
                                cs = slice(c * 512, (c + 1) * 512)
                                S.op("pe", lambda e: e.matmul(bank(hb + c), w_pl[g][:, :], mx[:, cs],
                                                              start=True, stop=True),
                                     reads=[R_wpl[g], R_mx], writes=[PB[hb + c]])
                                S.op("act", lambda e: e.activation(
                                    pooledT[:, g, cs], bank(hb + c), AF.Copy,
                                    scale=pv[:, pb + PV_PSC + g:pb + PV_PSC + g + 1]),
                                    reads=[PB[hb + c], R_pv], writes=[R_pool[c]])

                        uTs = [uT, uT2, uT3]
                        R_us = [R_u, Res("u2"), Res("u3")]
                        order = [3, 2, 1, 0]
                        wis = {}
                        wis[0] = load_u(order[0])
                        u_mm(order[0], wis[0], 0)
                        wis[1] = load_u(order[1])
                        u_mm(order[1], wis[1], 4)
                        u_evac(0)
                        u_evac(1)
                        wis[2] = load_u(order[2])
                        u_mm(order[2], wis[2], 0)
                        u_evac(2)
                        wis[3] = load_u(order[3])
                        u_mm(order[3], wis[3], 4)
                        chain_and_pool(0, order[0], wis[0])
                        u_evac(3)
                        chain_and_pool(1, order[1], wis[1])
                        chain_and_pool(2, order[2], wis[2])
                        chain_and_pool(3, order[3], wis[3])
                        S.barrier()
                    if l == 0:
                        tap("pooled0", pooledT[:, :, :], R_pool)

                    if ph == 2 and l == n_layers - 1: S.dead = True
                    with ExitStack() as ascope:
                        NSET = 2
                        qz = [[sb(ascope, "qz%d_%d" % (h_, i), [128, S_LEN], BF16) for i in range(NSET)]
                              for h_ in range(2)]
                        kT = [sb(ascope, "kT%d" % i, [128, S_LEN], BF16) for i in range(NSET)]
                        Vt = [sb(ascope, "V%d" % i, [128, 16, 2, 65], BF16) for i in range(NSET)]
                        nmT = [sb(ascope, "nm%d" % i, [128, 1024], BF16) for i in range(NSET)]
                        ksum = [sb(ascope, "ks%d" % i, [128, 8], F32) for i in range(NSET)]
                        ksb = [sb(ascope, "ksb%d" % i, [128, 16], BF16) for i in range(NSET)]
                        gpad = sb(ascope, "gpad", [128, 2, 8], F32)
                        top8 = sb(ascope, "top8", [128, 2, 8], F32)
                        gw1 = sb(ascope, "gw1", [128, 2, 8], F32)
                        gw2 = sb(ascope, "gw2", [128, 2, 8], F32)
                        R_gw1, R_gw2 = Res("gw1"), Res("gw2")
                        btok = sb(ascope, "btok", [128, 8, 64], BF16)
                        Pt = [sb(ascope, "P%d" % i, [128, 512], BF16) for i in range(3)]
                        rec = sb(ascope, "rec", [128, 4], F32)
                        atok = [sb(ascope, "atok%d" % i, [128, 4, 128], BF16) for i in range(2)]
                        R_q = [[Res("q") for _ in range(NCH)] for _ in range(NSET)]
                        R_k = [[Res("k") for _ in range(NCH)] for _ in range(NSET)]
                        R_v = [[Res("v") for _ in range(NCH)] for _ in range(NSET)]
                        R_nm = [[Res("nm") for _ in range(2)] for _ in range(NSET)]
                        R_ks = [Res("ks") for _ in range(NSET)]
                        R_ksb = [Res("ksb") for _ in range(NSET)]
                        R_gp, R_t8, R_bt = Res("gp"), Res("t8"), Res("bt")
                        R_P = [Res("P") for _ in range(3)]
                        R_rec = Res("rec")
                        R_at = [Res("at") for _ in range(2)]
                        pctr = {"p": 0, "s": 0, "o": 0, "proj": 0}

                        for i in range(NSET):
                            S.op("dve", lambda e: e.memset(Vt[i][:, :, :, 64:65], 1.0), writes=R_v[i])
                        S.op("dve", lambda e: e.memset(btok[:, :, :], 0.0), writes=[R_bt])
                        for i in range(NSET):
                            S.op("dve", lambda e: e.memset(ksb[i][:, :], 0.0), writes=[R_ksb[i]])
                            S.op("dve", lambda e: e.memset(qz[0][i][64:128, :], 0.0), writes=R_q[i])
                            S.op("dve", lambda e: e.memset(qz[1][i][0:64, :], 0.0), writes=R_q[i])
                            S.op("dve", lambda e: e.memset(nmT[i][64:128, :], 0.0), writes=R_nm[i])

                        if ph == 211:
                            S.dead = True

                        def load_qkv(m):
                            i = ctr["wqkv"] % 2
                            ctr["wqkv"] += 1
                            for j in range(3):
                                S.dma("pool", sl_wqkv[i], w_qkv[i][:, j, :, :],
                                      win[:, :, j * 512 + m * 128:j * 512 + (m + 1) * 128], writes=[R_wqkv[i]])
                            return i

                        def project(m, wi, st):
                            for c in range(NCH):
                                cs = slice(c * 512, (c + 1) * 512)
                                for j, (dstT, rr) in enumerate(((None, R_q[st]), (kT[st], R_k[st]))):
                                    b = pctr["proj"] % 2
                                    pctr["proj"] += 1
                                    S.op("pe", lambda e: [e.matmul(bank(b), w_qkv[wi][:, j, kc, :], hT[:, kc, cs],
                                                                   start=(kc == 0), stop=(kc == 7))
                                                          for kc in range(8)][-1],
                                         reads=[R_wqkv[wi], R_h[c]], writes=[PB[b]])
                                    if j == 0:
                                        for h_ in range(2):
                                            S.op("act", lambda e: e.activation(
                                                qz[h_][st][h_ * 64:(h_ + 1) * 64, cs],
                                                ps[h_ * 64:(h_ + 1) * 64, b * 512:(b + 1) * 512], AF.Copy),
                                                reads=[PB[b]], writes=[rr[c]])
                                    else:
                                        S.op("act", lambda e: e.activation(dstT[:, cs], bank(b), AF.Copy),
                                             reads=[PB[b]], writes=[rr[c]])
                                    if j == 1 and ph != 212:
                                        S.op("dve", lambda e: e.tensor_reduce(
                                            ksum[st][:, 2 * c:2 * c + 2],
                                            bank(b).rearrange("p (n k) -> p n k", k=256), AX.X, ALU.add),
                                            reads=[PB[b]], writes=[R_ks[st]])
                            if ph == 212:
                                S.dead = True
                            for hp in range(2):
                                S.op("dve", lambda e: e.tensor_copy(ksb[st][hp * 64:(hp + 1) * 64, hp * 8:hp * 8 + 8],
                                                                    ksum[st][hp * 64:(hp + 1) * 64, :]),
                                     reads=[R_ks[st]], writes=[R_ksb[st]])
                            if ph == 213:
                                S.dead = True
                            for tg in range(4):
                                b = pctr["proj"] % 2
                                pctr["proj"] += 1

                                def vmm(e):
                                    ins = None
                                    for t4 in range(4):
                                        tt = tg * 4 + t4
                                        for kc in range(8):
                                            ins = e.matmul(bank(b, t4 * 128, (t4 + 1) * 128),
                                                           hT[:, kc, tt * 128:(tt + 1) * 128],
                                                           w_qkv[wi][:, 2, kc, :], start=(kc == 0), stop=(kc == 7))
                                    return ins
                                S.op("pe", vmm, reads=[R_wqkv[wi], R_h[tg]], writes=[PB[b]])
                                S.op("dve", lambda e: e.tensor_copy(
                                    Vt[st][:, tg * 4:(tg + 1) * 4, :, 0:64],
                                    bank(b).rearrange("p (t h d) -> p t h d", t=4, h=2)),
                                    reads=[PB[b]], writes=[R_v[st][tg]])

                        def masks_a(m, st):
                            GOFF = 1.0e4
                            S.op("dve", lambda e: e.memset(gpad[:, :, :], 0.0), writes=[R_gp])

                            def gmm(e):
                                ins = None
                                for j in range(8):
                                    qs = slice((8 + j) * 128, (9 + j) * 128)
                                    e.matmul(bank(6, j * 16, j * 16 + 16), qz[0][st][:, qs], ksb[st][:, :],
                                             start=True, stop=False, skip_group_check=True)
                                    ins = e.matmul(bank(6, j * 16, j * 16 + 16), qz[1][st][:, qs], ksb[st][:, :],
                                                   start=False, stop=True, skip_group_check=True)
                                return ins
                            S.op("pe", gmm, reads=[R_q[st][2], R_q[st][3], R_ksb[st]], writes=[PB[6]])
                            chains = []
                            for j in range(8):
                                chains.append(lambda j=j: mask_chain(st, j, GOFF))
                            return chains

                        def mask_chain(st, j, GOFF):
                            if True:
                                qt = 8 + j
                                own = qt // 2
                                S.op("dve", lambda e: e.tensor_scalar(
                                    gpad[:, :, 0:own],
                                    bank(6, j * 16, j * 16 + 16).rearrange("p (h n) -> p h n", h=2)[:, :, 0:own],
                                    GOFF, None, ALU.add),
                                    reads=[PB[6]], writes=[R_gp])
                                src, rsrc = gpad, R_gp
                                for it_ in range(2):
                                    S.op("dve", lambda e: e.tensor_reduce(top8[:, :, it_], src[:, :, :], AX.X, ALU.max),
                                         reads=[rsrc], writes=[R_t8])
                                    dst, rdst = (gw1, R_gw1) if it_ == 0 else (gw2, R_gw2)
                                    for hp in range(2):
                                        S.op("dve", lambda e: e.scalar_tensor_tensor(
                                            dst[:, hp, :], src[:, hp, :], top8[:, hp, it_:it_ + 1], src[:, hp, :],
                                            ALU.is_lt, ALU.mult),
                                            reads=[rsrc, R_t8], writes=[rdst])
                                    src, rsrc = dst, rdst
                                S.op("dve", lambda e: e.tensor_reduce(top8[:, :, 2], src[:, :, :], AX.X, ALU.max),
                                     reads=[rsrc], writes=[R_t8])
                                for hp in range(2):
                                    S.op("dve", lambda e: e.tensor_scalar(
                                        btok[:, j, hp * 32:hp * 32 + 8], gpad[:, hp, :], top8[:, hp, 2:3], NEG,
                                        ALU.is_lt, ALU.mult),
                                        reads=[R_gp, R_t8], writes=[R_bt])
                                S.op("dve", lambda e: e.memset(
                                    btok[:, j, :].rearrange("p (h c) -> p h c", h=2)[:, :, own:own + 1], 0.0),
                                    writes=[R_bt])

                        def masks_b(m, st):
                            for c2 in range(2):
                                trb = ps[:, 7 * 512:8 * 512].bitcast(BF16)

                                def tmm(e):
                                    ins = None
                                    for j in range(4):
                                        ins = e.transpose(trb[0:64, j * 128:(j + 1) * 128], btok[:, c2 * 4 + j, :], ident)
                                    return ins
                                S.op("pe", tmm, reads=[R_bt, R_cb], writes=[PB[7]])
                                S.op("dve", lambda e: e.tensor_copy(nmT[st][0:64, c2 * 512:(c2 + 1) * 512],
                                                                    trb[0:64, 0:512]),
                                     reads=[PB[7]], writes=[R_nm[st][c2]])

                        def attention(m, st, inter=()):
                            inter = list(inter)
                            tiles = [(c, hp, kt) for c in range(NCH) for hp in range(2) for kt in range(4 * c + 4)]
                            info = {}

                            def smm_op(i):
                                c, hp, kt = tiles[i]
                                hs = slice(hp * 64, (hp + 1) * 64)
                                r = kt - 4 * c
                                col0 = 128 * max(r, 0)
                                sbk = (2, 3, 0)[i % 3]
                                need_bias = (c >= 2) and (kt < 4 * c + 2)

                                def smm(e):
                                    last = not need_bias and r < 0
                                    ins = e.matmul(bank(sbk, col0, 512), kT[st][:, kt * 128:(kt + 1) * 128],
                                                   qz[hp][st][:, c * 512 + col0:(c + 1) * 512],
                                                   start=True, stop=last, skip_group_check=True)
                                    if need_bias:
                                        ins = e.matmul(bank(sbk, col0, 512), esel(hp, kt // 2),
                                                       nmT[st][:, (c - 2) * 512 + col0:(c - 1) * 512],
                                                       start=False, stop=(r < 0), skip_group_check=True)
                                    if r >= 0:
                                        ins = e.matmul(bank(sbk, col0, col0 + 128), ident, tri,
                                                       start=False, stop=True, skip_group_check=True)
                                    return ins
                                rd = [R_k[st][kt // 4], R_q[st][c], R_cb]
                                if need_bias:
                                    rd.append(R_nm[st][c - 2])
                                S.op("pe", smm, reads=rd, writes=[PB[sbk]])
                                info[i] = (sbk, col0, r)

                            pend = []

                            def flush_pending(force=False):
                                while pend and (force or pend[0][0] <= 0):
                                    pend.pop(0)[1]()
                                for p_ in pend:
                                    p_[0] -= 1

                            smm_op(0)
                            smm_op(1)
                            ob = None
                            for i, (c, hp, kt) in enumerate(tiles):
                                hs = slice(hp * 64, (hp + 1) * 64)
                                ab = c % 2
                                if kt == 0:
                                    ob = 4 + pctr["o"] % 2
                                    pctr["o"] += 1
                                if i + 2 < len(tiles):
                                    smm_op(i + 2)
                                sbk, col0, r = info[i]
                                pi = pctr["p"] % 3
                                pctr["p"] += 1
                                S.op("act", lambda e: e.activation(Pt[pi][:, col0:512], bank(sbk, col0, 512),
                                                                   AF.Exp, scale=0.125),
                                     reads=[PB[sbk]], writes=[R_P[pi]])

                                def pvmm(e):
                                    ins = None
                                    for j in range(max(r, 0), 4):
                                        ins = e.matmul(bank(ob, j * 128, j * 128 + 65),
                                                       Pt[pi][:, j * 128:(j + 1) * 128], Vt[st][:, kt, hp, :],
                                                       start=(kt == 0 and j == 0), stop=(kt == 4 * c + j),
                                                       skip_group_check=True)
                                    return ins
                                S.op("pe", pvmm, reads=[R_P[pi], R_v[st][kt // 4]], writes=[PB[ob]])
                                flush_pending()
                                if kt == 4 * c + 3:
                                    o3 = bank(ob).rearrange("p (j d) -> p j d", j=4)
                                    S.op("dve", lambda e: e.reciprocal(rec[:, :], o3[:, :, 64]),
                                         reads=[PB[ob]], writes=[R_rec])
                                    S.op("dve", lambda e: e.tensor_tensor(
                                        atok[ab][:, :, hs], o3[:, :, 0:64],
                                        rec[:, :].unsqueeze(2).to_broadcast([128, 4, 64]), ALU.mult),
                                        reads=[PB[ob], R_rec], writes=[R_at[ab]])
                                    if inter:
                                        inter.pop(0)()
                                    if hp == 1:
                                        def do_tr(c=c, ab=ab):
                                            trb = ps[:, 7 * 512:8 * 512].bitcast(BF16)

                                            def tmm2(e):
                                                ins = None
                                                for j in range(4):
                                                    ins = e.transpose(trb[:, j * 128:(j + 1) * 128],
                                                                      atok[ab][:, j, :], ident)
                                                return ins
                                            S.op("pe", tmm2, reads=[R_at[ab], R_cb], writes=[PB[7]])
                                            S.op("dve", lambda e: e.tensor_copy(
                                                attnT[:, m, c * 512:(c + 1) * 512], trb[:, 0:512]),
                                                reads=[PB[7]], writes=[R_attn[c]])
                                        pend.append([2, do_tr])
                            flush_pending(force=True)
                            while inter:
                                inter.pop(0)()

                        wnext = load_qkv(0)
                        project(0, wnext, 0)
                        for ch in masks_a(0, 0):
                            ch()
                        for m in range(4):
                            st = m % NSET
                            if m < 3:
                                wnext = load_qkv(m + 1)
                                project(m + 1, wnext, (m + 1) % NSET)
                            masks_b(m, st)
                            if l == 0 and m == 0:
                                tap("q0", qz[0][0][:, :], R_q[0])
                                tap("k0", kT[0][:, :], R_k[0])
                                tap("v0", Vt[0][:, :, :, :], R_v[0])
                                tap("nm0", nmT[0][0:64, :], R_nm[0])
                            inter = masks_a(m + 1, (m + 1) % NSET) if m < 3 else ()
                            attention(m, st, inter)
                        S.barrier()
                    if l == 0:
                        tap("attn0", attnT[:, :, :], R_attn)

                    if ph == 3 and l == n_layers - 1: S.dead = True
                    with ExitStack() as gscope:
                        snap_g = S.snapshot()
                        mergedT = sb(gscope, "mergedT", [128, 8, S_LEN], BF16)
                        w_o = [sb(gscope, "wo%d" % i, [128, 8, 128], BF16) for i in range(2)]
                        R_mg = [Res("mg%d" % c) for c in range(NCH)]
                        s0 = [sb(gscope, "s0_%d" % i, [128, 512], F32) for i in range(2)]
                        s1 = [sb(gscope, "s1_%d" % i, [128, 512], F32) for i in range(2)]
                        R_s0 = [Res("s0") for _ in range(2)]
                        R_s1 = [Res("s1") for _ in range(2)]
                        wa_v = wview(w_a_d, l)
                        wb_v = wview(w_b_d, l)

                        def load_m(m):
                            i = ctr["wm"] % 2
                            ctr["wm"] += 1
                            ms_ = slice(m * 128, (m + 1) * 128)
                            S.dma("pool", sl_wm[i], w_m[i][:, 0:4, :], wa_v[:, :, ms_], writes=[R_wm[i]])
                            S.dma("pool", sl_wm[i], w_m[i][:, 4:8, :], wb_v[:, :, ms_], writes=[R_wm[i]])
                            S.dma("pool", sl_wm[i], w_m[i][:, 8:16, :], win[:, :, 2048 + m * 128:2048 + (m + 1) * 128],
                                  writes=[R_wm[i]])
                            S.dma("pool", sl_wm[i], w_m[i][:, 16:24, :], win[:, :, 3072 + m * 128:3072 + (m + 1) * 128],
                                  writes=[R_wm[i]])
                            return i

                        nxt = load_m(0)
                        it = 0
                        for m in range(8):
                            wi = nxt
                            if m < 7:
                                nxt = load_m(m + 1)
                            for c in range(NCH):
                                cs = slice(c * 512, (c + 1) * 512)
                                b0 = 4 * (it % 2)
                                ti = it % 2
                                it += 1
                                S.op("pe", lambda e: [e.matmul(bank(b0), w_m[wi][:, kc, :], attnT[:, kc, cs],
                                                               start=(kc == 0), stop=(kc == 3)) for kc in range(4)][-1],
                                     reads=[R_wm[wi], R_attn[c]], writes=[PB[b0]])
                                S.op("pe", lambda e: [e.matmul(bank(b0 + 1), w_m[wi][:, 4 + kc, :], pooledT[:, kc, cs],
                                                               start=(kc == 0), stop=(kc == 3)) for kc in range(4)][-1],
                                     reads=[R_wm[wi], R_pool[c]], writes=[PB[b0 + 1]])
                                S.op("pe", lambda e: [e.matmul(bank(b0 + 2), w_m[wi][:, 8 + kc, :], hT[:, kc, cs],
                                                               start=(kc == 0), stop=(kc == 7)) for kc in range(8)][-1],
                                     reads=[R_wm[wi], R_h[c]], writes=[PB[b0 + 2]])
                                S.op("pe", lambda e: [e.matmul(bank(b0 + 3), w_m[wi][:, 16 + kc, :], hT[:, kc, cs],
                                                               start=(kc == 0), stop=(kc == 7)) for kc in range(8)][-1],
                                     reads=[R_wm[wi], R_h[c]], writes=[PB[b0 + 3]])
                                S.op("act", lambda e: e.activation(s0[ti][:, :], bank(b0 + 2), AF.Sigmoid),
                                     reads=[PB[b0 + 2]], writes=[R_s0[ti]])
                                S.op("act", lambda e: e.activation(s1[ti][:, :], bank(b0 + 3), AF.Sigmoid),
                                     reads=[PB[b0 + 3]], writes=[R_s1[ti]])
                                S.op("dve", lambda e: e.tensor_tensor(s0[ti][:, :], s0[ti][:, :], bank(b0), ALU.mult),
                                     reads=[R_s0[ti], PB[b0]], writes=[R_s0[ti]])
                                S.op("dve", lambda e: e.tensor_tensor(s1[ti][:, :], s1[ti][:, :], bank(b0 + 1), ALU.mult),
                                     reads=[R_s1[ti], PB[b0 + 1]], writes=[R_s1[ti]])
                                S.op("dve", lambda e: e.tensor_tensor(mergedT[:, m, cs], s0[ti][:, :], s1[ti][:, :],
                                                                      ALU.add),
                                     reads=[R_s0[ti], R_s1[ti]], writes=[R_mg[c]])
                        if l == 0:
                            tap("merged0", mergedT[:, :, :], R_mg)
                        wo_v = wview(w_out_d, l)

                        def load_o(m):
                            i = ctr["wo"] % 2
                            ctr["wo"] += 1
                            S.dma("pool", sl_wo[i], w_o[i][:, :, :], wo_v[:, :, m * 128:(m + 1) * 128], writes=[R_wo[i]])
                            return i

                        S.wait_events("pool", snap_g)
                        nxt = load_o(0)
                        for m in range(8):
                            wi = nxt
                            if m < 7:
                                nxt = load_o(m + 1)
                            for c in range(NCH):
                                cs = slice(c * 512, (c + 1) * 512)
                                b = ctr["pb"] % 8
                                ctr["pb"] += 1
                                S.op("pe", lambda e: [e.matmul(bank(b), w_o[wi][:, kc, :], mergedT[:, kc, cs],
                                                               start=(kc == 0), stop=(kc == 7)) for kc in range(8)][-1],
                                     reads=[R_wo[wi], R_mg[c]], writes=[PB[b]])
                                S.op("dve", lambda e: e.tensor_tensor(xT[:, m, cs], xT[:, m, cs], bank(b), ALU.add),
                                     reads=[R_x[c], PB[b]], writes=[R_x[c]])
                        S.barrier()
                if l == 0:
                    tap("x0m", xT[:, :, :], R_x)

                if ph == 4 and l == n_layers - 1: S.dead = True
                snap_f = S.snapshot()
                wup_v = wview(w_up_d, l)
                wdn_v = wview(w_down_d, l)
                with ExitStack() as fs:
                    hid = sb(fs, "hid", [128, 8, S_LEN], BF16)
                    w_up = [sb(fs, "wup%d" % i, [128, 2, 8, 128], BF16) for i in range(NUP)]
                    w_dn = [sb(fs, "wdn%d" % i, [128, 8, 128], BF16) for i in range(2)]
                    R_hid = [Res("hid%d" % i) for i in range(8)]
                    S.wait_events("pool", snap_f)
                    pre_up = [None]
                    deferred_norm = [True]
                    tg2 = None
                    R_tg2 = [Res("tg%d" % i) for i in range(2)]
                    R_tv2 = [Res("tv%d" % i) for i in range(2)]

                    def load_up(i_):
                        i = ctr["wup"] % NUP
                        ctr["wup"] += 1
                        S.dma("pool", sl_wup[i], w_up[i][:, 0, :, :], wup_v[:, :, i_ * 128:(i_ + 1) * 128],
                              writes=[R_wup[i]])
                        S.dma("pool", sl_wup[i], w_up[i][:, 1, :, :],
                              wup_v[:, :, DFF + i_ * 128:DFF + (i_ + 1) * 128], writes=[R_wup[i]])
                        return i

                    def load_dn(m, lo, hi):
                        i = ctr["wdn"] % 2
                        ctr["wdn"] += 1
                        S.dma("pool", sl_wdn[i], w_dn[i][:, 0:hi - lo, :], wdn_v[:, lo:hi, m * 128:(m + 1) * 128],
                              writes=[R_wdn[i]])
                        return i

                    pre_up[0] = load_up(0)
                    rmsnorm(pb + PV_GFFN, h_out, "f%d" % l)
                    tg2 = [sb(fs, "tg%d" % i, [128, S_LEN], F32) for i in range(2)]
                    tv2 = [sb(fs, "tv%d" % i, [128, S_LEN], F32) for i in range(2)]
                    for (lo, hi) in FPASS:
                        nxt = pre_up[0] if lo == 0 else load_up(lo)
                        for i_ in range(lo, hi):
                            wi = nxt
                            if i_ + 1 < hi:
                                nxt = load_up(i_ + 1)
                            il = i_ - lo
                            tg, tv, R_tg, R_tv = tg2[i_ % 2], tv2[i_ % 2], R_tg2[i_ % 2], R_tv2[i_ % 2]
                            for half, (tt, rt) in enumerate(((tg, R_tg), (tv, R_tv))):
                                hb = 4 * half
                                col = i_ + 22 * half
                                for c in range(NCH):
                                    cs = slice(c * 512, (c + 1) * 512)
                                    S.op("pe", lambda e: [e.matmul(bank(hb + c), w_up[wi][:, half, kc, :], hT[:, kc, cs],
                                                                   start=(kc == 0), stop=(kc == 7))
                                                          for kc in range(8)][-1],
                                         reads=[R_wup[wi], R_h[c]], writes=[PB[hb + c]])
                                pfull = ps[:, hb * 512:(hb + 4) * 512]
                                c0 = pv[:, pb + PV_CW0 + col:pb + PV_CW0 + col + 1]
                                c1 = pv[:, pb + PV_CW1 + col:pb + PV_CW1 + col + 1]
                                c2 = pv[:, pb + PV_CW2 + col:pb + PV_CW2 + col + 1]
                                cbias = pv[:, pb + PV_CB + col:pb + PV_CB + col + 1]
                                S.op("act", lambda e: e.activation(tt[:, :], pfull, AF.Identity, bias=cbias, scale=c2),
                                     reads=PB[hb:hb + 4] + [R_pv], writes=[rt])
                                S.op("dve", lambda e: e.scalar_tensor_tensor(
                                    tt[:, 1:], pfull[:, 0:S_LEN - 1], c1, tt[:, 1:], ALU.mult, ALU.add),
                                    reads=PB[hb:hb + 4] + [rt, R_pv], writes=[rt])
                                S.op("dve", lambda e: e.scalar_tensor_tensor(
                                    tt[:, 2:], pfull[:, 0:S_LEN - 2], c0, tt[:, 2:], ALU.mult, ALU.add),
                                    reads=PB[hb:hb + 4] + [rt, R_pv], writes=[rt])
                                if half == 0:
                                    S.op("act", lambda e: e.activation(tt[:, :], tt[:, :], AF.Silu),
                                         reads=[rt], writes=[rt])
                            S.op("dve", lambda e: e.tensor_tensor(hid[:, il, :], tg[:, :], tv[:, :], ALU.mult),
                                 reads=[R_tg, R_tv], writes=[R_hid[il]])
                        if l == 0 and lo == 0:
                            tap("hid0", hid[:, :, :], R_hid)
                        nk = hi - lo
                        nxt = load_dn(0, lo, hi)
                        for m in range(8):
                            wi = nxt
                            if m < 7:
                                nxt = load_dn(m + 1, lo, hi)
                            for c in range(NCH):
                                cs = slice(c * 512, (c + 1) * 512)
                                b = ctr["pb"] % 8
                                ctr["pb"] += 1
                                S.op("pe", lambda e: [e.matmul(bank(b), w_dn[wi][:, kc, :], hid[:, kc, cs],
                                                               start=(kc == 0), stop=(kc == nk - 1))
                                                      for kc in range(nk)][-1],
                                     reads=[R_wdn[wi]] + R_hid[0:nk], writes=[PB[b]])
                                S.op("dve", lambda e: e.tensor_tensor(xT[:, m, cs], xT[:, m, cs], bank(b), ALU.add),
                                     reads=[R_x[c], PB[b]], writes=[R_x[c]])
                    S.barrier()
                if l == 0:
                    tap("x0f", xT[:, :, :], R_x)

        except _Stop:
            pass
        S.dead = False
        S.barrier()

        with ExitStack() as os_:
            stg = [sb(os_, "stg%d" % i, [128, 8, 512], F32) for i in range(2)]
            R_stg = [Res("stg%d" % i) for i in range(2)]
            yv = yT_d.rearrange("(c p) t -> p c t", p=128)
            outs = []

            def y_out(c):
                return stg[c % 2][:, :, :], [R_stg[c % 2]]

            if final_norm:
                with ExitStack() as ns:
                    sq = [sb(ns, "sqz%d" % i, [128, 8, 512], BF16) for i in range(2)]
                    R_sq = [Res("sq%d" % i) for i in range(2)]
                    rs = [sb(ns, "rsz%d" % i, [128, 512], F32) for i in range(2)]
                    R_rs = [Res("rs%d" % i) for i in range(2)]
                    for c in range(NCH):
                        i = c % 2
                        cs = slice(c * 512, (c + 1) * 512)
                        S.op("act", lambda e: e.activation(sq[i][:, :, :], xT[:, :, cs], AF.Square),
                             reads=[R_x[c]], writes=[R_sq[i]])
                        b = i
                        S.op("pe", lambda e: [e.matmul(bank(b), ones, sq[i][:, kc, :], start=(kc == 0),
                                                       stop=(kc == 7)) for kc in range(8)][-1],
                             reads=[R_sq[i], R_cb], writes=[PB[b]])
                        S.op("act", lambda e: e.activation(rs[i][:, :], bank(b), AF.Sqrt,
                                                           bias=pv[:, PV_EPS:PV_EPS + 1], scale=1.0 / D),
                             reads=[PB[b], R_pv], writes=[R_rs[i]])
                        S.op("dve", lambda e: e.reciprocal(rs[i][:, :], rs[i][:, :]),
                             reads=[R_rs[i]], writes=[R_rs[i]])
                        for kc in range(8):
                            S.op("dve", lambda e: e.scalar_tensor_tensor(
                                stg[i][:, kc, :], xT[:, kc, cs], pv[:, PV_GFIN + kc:PV_GFIN + kc + 1], rs[i][:, :],
                                ALU.mult, ALU.mult),
                                reads=[R_x[c], R_rs[i], R_pv], writes=[R_stg[i]])
                        outs.append(S.dma("sp", sl_out[i], yv[:, :, cs], stg[i][:, :, :], reads=[R_stg[i]]))
            else:
                for c in range(NCH):
                    cs = slice(c * 512, (c + 1) * 512)
                    outs.append(S.dma("sp", sl_out[c % 2], yv[:, :, cs], xT[:, :, cs], reads=[R_x[c]]))
            lastout = {}
            for ev in outs:
                lastout[ev.key] = ev
            for ev in lastout.values():
                S._wait("sp", ev)
            for ev in tap_ev:
                S._wait("sp", ev)
            S.barrier()
    return nc, tap_d


def host_consts():
    cbm = np.zeros((128, NCB), np.float32)
    cbm[:, CB_ID:CB_ID + 128] = np.eye(128, dtype=np.float32)
    cbm[:, CB_ONES:CB_ONES + 128] = 1.0
    for hp in range(2):
        for n in range(8):
            o = CB_ESEL + (hp * 8 + n) * 128
            cbm[32 * hp + n, o:o + 128] = 1.0
    key = np.arange(128)[:, None]
    q = np.arange(128)[None, :]
    cbm[:, CB_TRI:CB_TRI + 128] = np.where(key <= q, 0.0, NEG).astype(np.float32)
    return cbm


def host_pvec(norm_mix_g, norm_ffn_g, pool_scale, conv_w, conv_b, norm_final_g):
    pvm = np.zeros((128, NPV), np.float32)

    def cols(v):
        v = np.asarray(v, np.float32)
        return np.ascontiguousarray(v.reshape(-1, 128).T)

    for l in range(L_DEPTH):
        b = l * PV_L
        pvm[:, b + PV_GMIX:b + PV_GMIX + 8] = cols(norm_mix_g[l])
        pvm[:, b + PV_GFFN:b + PV_GFFN + 8] = cols(norm_ffn_g[l])
        pvm[:, b + PV_PSC:b + PV_PSC + 4] = cols(pool_scale[l])
        pvm[:, b + PV_CW0:b + PV_CW0 + 44] = cols(conv_w[l][0])
        pvm[:, b + PV_CW1:b + PV_CW1 + 44] = cols(conv_w[l][1])
        pvm[:, b + PV_CW2:b + PV_CW2 + 44] = cols(conv_w[l][2])
        pvm[:, b + PV_CB:b + PV_CB + 44] = cols(conv_b[l])
    pvm[:, PV_GFIN:PV_GFIN + 8] = cols(norm_final_g)
    pvm[:, PV_EPS] = EPS
    for g, w in enumerate(WINS):
        t = np.arange(16)
        pvm[:, PV_INVC + g * 16:PV_INVC + (g + 1) * 16] = (1.0 / np.minimum(w, t + 1)).astype(np.float32)[None, :]
    return pvm


_NC_CACHE = {}


def make_in_maps(x, norm_mix_g, w_in, w_pool, pool_scale, w_branch_a, w_branch_b, w_out,
                 norm_ffn_g, w_up, conv_w, conv_b, w_down, norm_final_g):
    f = lambda a: np.ascontiguousarray(np.asarray(a, dtype=np.float32))
    x = f(x)
    shared = {
        "w_in": f(w_in), "w_pool": f(w_pool), "w_branch_a": f(w_branch_a), "w_branch_b": f(w_branch_b),
        "w_out": f(w_out), "w_up": f(w_up), "w_down": f(w_down),
        "pvec": host_pvec(f(norm_mix_g), f(norm_ffn_g), f(pool_scale), f(conv_w), f(conv_b), f(norm_final_g)),
        "cb": host_consts(),
    }
    in_maps = []
    for b in range(8):
        d = dict(shared)
        d["xT"] = np.ascontiguousarray(x[b].T)
        in_maps.append(d)
    return in_maps


def kernel(x, norm_mix_g, w_in, w_pool, pool_scale, w_branch_a, w_branch_b, w_out,
           norm_ffn_g, w_up, conv_w, conv_b, w_down, norm_final_g):
    in_maps = make_in_maps(x, norm_mix_g, w_in, w_pool, pool_scale, w_branch_a, w_branch_b, w_out,
                           norm_ffn_g, w_up, conv_w, conv_b, w_down, norm_final_g)
    if "nc" not in _NC_CACHE:
        _NC_CACHE["nc"] = build()[0]
    nc = _NC_CACHE["nc"]
    res = run_bass_kernel_spmd(nc, in_maps, core_ids=list(range(8)))
    out = np.stack([np.ascontiguousarray(np.asarray(r["yT"], dtype=np.float32).T) for r in res.results], axis=0)
    return out.astype(np.float32)
```
